# Optimizing a Trainium2 kernel written in Bass

```python
import math
import jax, jax.numpy as jnp
from jax import lax
import numpy as np

D_MODEL = 1024
BATCH = 2
SEQ = 8192
DEPTH = 2

HEAD_DIM = 64
N_RET = 6
N_NSA = 6
N_KV = 2
HPG = N_NSA // N_KV
N_GM = 4
GM_DIM = 64
RET_W = N_RET * HEAD_DIM
NSA_W = N_NSA * HEAD_DIM
GM_W = N_GM * GM_DIM
MIX_W = RET_W + NSA_W + GM_W
RET_CHUNK = 128
CMP_LEN = 32
CMP_STRIDE = 16
SEL_BLOCK = 64
N_SEL = 16
WINDOW = 512
Q_BLOCK = 128
GM_CHUNK = 128
N_BUCKETS = 32
MAX_DISTANCE = 128
ROPE_THETA = 10000.0
D_FF = -(-8 * D_MODEL // (3 * 256)) * 256
SPLIT_SIZES = (RET_W, RET_W, RET_W, RET_W, NSA_W, 3 * 2 * N_KV * HEAD_DIM, 3 * N_NSA, 2 * GM_W)
IN_W = sum(SPLIT_SIZES)
EPS = 1e-6
BIG = 1e9
NEG = -1e30

kernel_name = "hybrid_retention_nsa_gmlp_block"


def rms_norm(x, gain=None):
    xf = x.astype(jnp.float32)
    y = xf * lax.rsqrt(jnp.mean(xf * xf, axis=-1, keepdims=True) + EPS)
    if gain is not None:
        y = y * gain.astype(jnp.float32)
    return y.astype(x.dtype)


def rope(x, pos):
    half = x.shape[-1] // 2
    inv = ROPE_THETA ** (-jnp.arange(half, dtype=jnp.float32) / half)
    ang = pos.astype(jnp.float32)[:, None] * inv[None, :]
    cos = jnp.cos(ang)[None, :, None, :]
    sin = jnp.sin(ang)[None, :, None, :]
    x1 = x[..., :half].astype(jnp.float32)
    x2 = x[..., half:].astype(jnp.float32)
    return jnp.concatenate([x1 * cos - x2 * sin, x2 * cos + x1 * sin], axis=-1).astype(x.dtype)


def t5_bucket(dist):
    n = jnp.maximum(dist, 0)
    max_exact = N_BUCKETS // 2
    nf = jnp.maximum(n, 1).astype(jnp.float32)
    large = max_exact + (jnp.log(nf / max_exact) / math.log(MAX_DISTANCE / max_exact)
                         * (N_BUCKETS - max_exact)).astype(jnp.int32)
    large = jnp.minimum(large, N_BUCKETS - 1)
    return jnp.where(n < max_exact, n, large)


def masked_softmax(logits, mask):
    s = jnp.where(mask, logits.astype(jnp.float32), NEG)
    return jax.nn.softmax(s, axis=-1) * mask


def retention(q, k, v, g):
    B, T, H, d = q.shape
    C = RET_CHUNK
    nc = T // C
    f32 = jnp.float32
    to_chunks = lambda a: a.astype(f32).reshape(B, nc, C, H, d).transpose(0, 3, 1, 2, 4)
    qc, kc, vc = to_chunks(q), to_chunks(k * d ** -0.5), to_chunks(v)
    log_gamma = jnp.log(1.0 - 2.0 ** (-5.0 - jnp.arange(H, dtype=f32)))
    idx = jnp.arange(C, dtype=f32)
    diff = idx[:, None] - idx[None, :]
    decay = jnp.where(diff >= 0, jnp.exp(jnp.maximum(diff, 0.0)[None] * log_gamma[:, None, None]), 0.0)
    scores = jnp.einsum('bhcnd,bhcmd->bhcnm', qc, kc) * decay[:, None]
    inner = jnp.einsum('bhcnm,bhcme->bhcne', scores, vc)
    zeta = jnp.exp((C - 1 - idx)[None, :] * log_gamma[:, None])
    xi = jnp.exp((idx + 1)[None, :] * log_gamma[:, None])
    chunk_decay = jnp.exp(C * log_gamma)

    def step(R, kv):
        kt, vt = kv
        R_new = chunk_decay[None, :, None, None] * R + jnp.einsum('bhmd,bhme->bhde', kt * zeta[None, :, :, None], vt)
        return R_new, R

    R0 = jnp.zeros((B, H, d, d), f32)
    _, R_prev = lax.scan(step, R0, (kc.transpose(2, 0, 1, 3, 4), vc.transpose(2, 0, 1, 3, 4)))
    cross = jnp.einsum('bhcnd,cbhde->bhcne', qc, R_prev) * xi[None, :, None, :, None]
    o = (inner + cross).transpose(0, 2, 3, 1, 4).reshape(B, T, H, d)
    o = rms_norm(o) * jax.nn.silu(g.astype(f32))
    return o.reshape(B, T, H * d).astype(q.dtype)


def nsa(q, kv, gates, q_gain, k_gain, cmp_pe, cmp_w1, cmp_w2, rel_bias):
    B, T, _, d = q.shape
    out_dtype = q.dtype
    q = rms_norm(q, q_gain) * d ** -0.5
    q = q.reshape(B, T, N_KV, HPG, d).transpose(0, 2, 3, 1, 4)
    kv = kv.transpose(2, 3, 0, 4, 1, 5)

    n_cmp = (T - CMP_LEN) // CMP_STRIDE + 1
    blk_idx = np.arange(n_cmp)[:, None] * CMP_STRIDE + np.arange(CMP_LEN)[None, :]

    def compress(a, j):
        blocks = a[:, :, blk_idx] + cmp_pe[j]
        flat = blocks.reshape(B, N_KV, n_cmp, CMP_LEN * d)
        return jax.nn.gelu(flat @ cmp_w1[j]) @ cmp_w2[j]

    k_cmp = rms_norm(compress(kv[0, 0], 0), k_gain[0])
    v_cmp = compress(kv[0, 1], 1)
    cmp_end = jnp.asarray(blk_idx[:, -1], jnp.int32)

    n_slc = T // SEL_BLOCK
    n_sel = min(N_SEL, n_slc)
    k_sel = rms_norm(kv[1, 0], k_gain[1]).reshape(B, N_KV, n_slc, SEL_BLOCK, d)
    v_sel = kv[1, 1].reshape(B, N_KV, n_slc, SEL_BLOCK, d)
    cs = np.arange(n_cmp)[:, None] * CMP_STRIDE
    ss = np.arange(n_slc)[None, :] * SEL_BLOCK
    overlap = np.clip(np.minimum(cs + CMP_LEN, ss + SEL_BLOCK) - np.maximum(cs, ss), 0, None) // CMP_STRIDE
    overlap = jnp.asarray(overlap, jnp.float32)

    pad = ((0, 0), (0, 0), (WINDOW, 0), (0, 0))
    k_win = jnp.pad(rms_norm(kv[2, 0], k_gain[2]), pad)
    v_win = jnp.pad(kv[2, 1], pad)

    gates_t = jax.nn.sigmoid(gates.astype(jnp.float32)).reshape(B, T, N_KV, HPG, 3).transpose(0, 2, 3, 1, 4)
    bias_tab = rel_bias.reshape(N_BUCKETS, N_KV, HPG)
    bi = jnp.arange(B)[:, None, None, None]
    gi = jnp.arange(N_KV)[None, :, None, None]
    K_SEL = n_sel * SEL_BLOCK

    def block(i):
        q0 = i * Q_BLOCK
        qb = lax.dynamic_slice_in_dim(q, q0, Q_BLOCK, axis=3)
        t = q0 + jnp.arange(Q_BLOCK, dtype=jnp.int32)

        dist_c = t[:, None] - cmp_end[None, :]
        bias_c = bias_tab[t5_bucket(dist_c)].transpose(2, 3, 0, 1)
        s_c = jnp.einsum('bghqd,bgnd->bghqn', qb, k_cmp) + bias_c
        p_c = masked_softmax(s_c, dist_c >= 0)
        o_c = jnp.einsum('bghqn,bgnd->bghqd', p_c, v_cmp)

        imp = jnp.einsum('bghqn,nj->bgqj', p_c, overlap)
        cur = (t // SEL_BLOCK)[:, None]
        j = jnp.arange(n_slc)[None, :]
        imp = jnp.where((j == 0) | (j == cur) | (j == cur - 1), BIG, imp)
        imp = jnp.where(j > cur, -BIG, imp)
        _, sel = lax.top_k(imp, n_sel)
        ks = k_sel[bi, gi, sel].reshape(B, N_KV, Q_BLOCK, K_SEL, d)
        vs = v_sel[bi, gi, sel].reshape(B, N_KV, Q_BLOCK, K_SEL, d)
        pos = sel[..., None] * SEL_BLOCK + jnp.arange(SEL_BLOCK, dtype=jnp.int32)
        dist_s = (t[None, None, :, None, None] - pos).reshape(B, N_KV, Q_BLOCK, K_SEL)
        bias_s = bias_tab[t5_bucket(dist_s), gi].transpose(0, 1, 4, 2, 3)
        s_s = jnp.einsum('bghqd,bgqkd->bghqk', qb, ks) + bias_s
        p_s = masked_softmax(s_s, (dist_s >= 0)[:, :, None])
        o_s = jnp.einsum('bghqk,bgqkd->bghqd', p_s, vs)

        kw = lax.dynamic_slice_in_dim(k_win, q0, WINDOW + Q_BLOCK, axis=2)
        vw = lax.dynamic_slice_in_dim(v_win, q0, WINDOW + Q_BLOCK, axis=2)
        s_pos = q0 - WINDOW + jnp.arange(WINDOW + Q_BLOCK, dtype=jnp.int32)
        dist_w = t[:, None] - s_pos[None, :]
        mask_w = (dist_w >= 0) & (dist_w < WINDOW) & (s_pos[None, :] >= 0)
        bias_w = bias_tab[t5_bucket(dist_w)].transpose(2, 3, 0, 1)
        s_w = jnp.einsum('bghqd,bgkd->bghqk', qb, kw) + bias_w
        p_w = masked_softmax(s_w, mask_w)
        o_w = jnp.einsum('bghqk,bgkd->bghqd', p_w, vw)

        g = lax.dynamic_slice_in_dim(gates_t, q0, Q_BLOCK, axis=3)
        return g[..., 0:1] * o_c + g[..., 1:2] * o_s + g[..., 2:3] * o_w

    out = lax.map(block, jnp.arange(T // Q_BLOCK, dtype=jnp.int32))
    return out.transpose(1, 0, 4, 2, 3, 5).reshape(B, T, N_NSA * d).astype(out_dtype)


def spatial_gating(z, ws, b):
    B, T, _ = z.shape
    nch = T // GM_CHUNK
    z = jax.nn.gelu(z)
    u, v = jnp.split(z, 2, axis=-1)
    v = rms_norm(v.reshape(B, nch, GM_CHUNK, N_GM, GM_DIM))
    causal = jnp.tril(jnp.ones((GM_CHUNK, GM_CHUNK), ws.dtype))
    sv = jnp.einsum('gts,bcsgd->bctgd', ws * causal, v) + b.T[None, None, :, :, None]
    return (u.reshape(B, nch, GM_CHUNK, N_GM, GM_DIM) * sv).reshape(B, T, GM_W)


def setup_inputs(seed: int = 0) -> dict:
    key = jax.random.key(seed)
    ks = jax.random.split(key, 16)
    f32 = jnp.float32
    nrm = lambda k, shape, scale: jax.random.normal(k, shape, f32) * scale
    return {
        "x": nrm(ks[0], (BATCH, SEQ, D_MODEL), 1.0),
        "attn_norm": 1.0 + nrm(ks[1], (DEPTH, D_MODEL), 0.02),
        "w_in": nrm(ks[2], (DEPTH, D_MODEL, IN_W), D_MODEL ** -0.5),
        "w_out": nrm(ks[3], (DEPTH, MIX_W, D_MODEL), MIX_W ** -0.5),
        "nsa_q_gain": 1.0 + nrm(ks[4], (DEPTH, HEAD_DIM), 0.02),
        "nsa_k_gain": 1.0 + nrm(ks[5], (DEPTH, 3, HEAD_DIM), 0.02),
        "cmp_pe": nrm(ks[6], (DEPTH, 2, CMP_LEN, HEAD_DIM), 0.1),
        "cmp_w1": nrm(ks[7], (DEPTH, 2, CMP_LEN * HEAD_DIM, HEAD_DIM), (CMP_LEN * HEAD_DIM) ** -0.5),
        "cmp_w2": nrm(ks[8], (DEPTH, 2, HEAD_DIM, HEAD_DIM), HEAD_DIM ** -0.5),
        "gm_ws": nrm(ks[9], (DEPTH, N_GM, GM_CHUNK, GM_CHUNK), GM_CHUNK ** -0.5),
        "gm_b": 1.0 + nrm(ks[10], (DEPTH, N_GM, GM_CHUNK), 0.02),
        "ffn_norm": 1.0 + nrm(ks[11], (DEPTH, D_MODEL), 0.02),
        "w_gate_up": nrm(ks[12], (DEPTH, D_MODEL, 2 * D_FF), D_MODEL ** -0.5),
        "w_down": nrm(ks[13], (DEPTH, D_FF, D_MODEL), D_FF ** -0.5),
        "rel_bias": nrm(ks[14], (N_BUCKETS, N_NSA), 0.5),
    }


def reference(x, attn_norm, w_in, w_out, nsa_q_gain, nsa_k_gain, cmp_pe, cmp_w1, cmp_w2,
              gm_ws, gm_b, ffn_norm, w_gate_up, w_down, rel_bias):
    B, T, _ = x.shape
    pos = jnp.arange(T, dtype=jnp.int32)
    split_points = [int(s) for s in np.cumsum(SPLIT_SIZES)[:-1]]
    for l in range(DEPTH):
        h = rms_norm(x, attn_norm[l])
        proj = h @ w_in[l]
        r_q, r_k, r_v, r_g, n_q, n_kv, n_g, gm_z = jnp.split(proj, split_points, axis=-1)
        hs = (B, T, N_RET, HEAD_DIM)
        ret_o = retention(rope(r_q.reshape(hs), pos), rope(r_k.reshape(hs), pos),
                          r_v.reshape(hs), r_g.reshape(hs))
        nsa_o = nsa(n_q.reshape(B, T, N_NSA, HEAD_DIM),
                    n_kv.reshape(B, T, 3, 2, N_KV, HEAD_DIM),
                    n_g.reshape(B, T, N_NSA, 3),
                    nsa_q_gain[l], nsa_k_gain[l], cmp_pe[l], cmp_w1[l], cmp_w2[l], rel_bias)
        gm_o = spatial_gating(gm_z, gm_ws[l], gm_b[l])
        mix = jnp.concatenate([ret_o, nsa_o.astype(ret_o.dtype), gm_o.astype(ret_o.dtype)], axis=-1)
        x = x + (mix @ w_out[l]).astype(x.dtype)
        h = rms_norm(x, ffn_norm[l])
        gate, up = jnp.split(h @ w_gate_up[l], 2, axis=-1)
        x = x + ((jax.nn.silu(gate) * up) @ w_down[l]).astype(x.dtype)
    return x
```

```python
import math
import ml_dtypes
from contextlib import ExitStack
import numpy as np
import concourse.bass as bass
import concourse.mybir as mybir
from concourse.bass_utils import run_bass_kernel_spmd

F32 = mybir.dt.float32
BF16 = mybir.dt.bfloat16
AF = mybir.ActivationFunctionType
ALU = mybir.AluOpType
AX = mybir.AxisListType

N_DMA_SEMS = 24


class Sched:
    ENGS = ("pe", "act", "dve", "pool", "sp")

    def __init__(self, nc, stack):
        self.nc = nc
        self.stack = stack
        self.sem = {e: stack.enter_context(nc.semaphore("s_" + e)) for e in self.ENGS}
        self.dsem = [stack.enter_context(nc.semaphore("d%d" % i)) for i in range(N_DMA_SEMS)]
        self.dcnt = [0] * N_DMA_SEMS
        self.dnext = 0
        self.cnt = {e: 0 for e in self.ENGS}
        self.clock = {e: {} for e in self.ENGS}
        self.streams = {e: [] for e in self.ENGS}
        self.last_w = {}
        self.reads = {}
        self.n_waits = 0

    def _need(self, eng, ev, waits):
        key, val, snap = ev
        ck = self.clock[eng]
        if ck.get(key, 0) >= val:
            return
        if key == eng and eng == "pe":
            return
        waits[key] = max(waits.get(key, 0), val)

    def _apply(self, eng, evs, waits):
        ck = self.clock[eng]
        for key, val, snap in evs:
            if snap:
                for k2, v2 in snap.items():
                    if ck.get(k2, 0) < v2:
                        ck[k2] = v2
        for key, val in waits.items():
            if ck.get(key, 0) < val:
                ck[key] = val

    def _deps(self, eng, reads, writes):
        waits = {}
        evs = []
        for k in reads:
            ev = self.last_w.get(k)
            if ev is not None:
                self._need(eng, ev, waits)
                evs.append(ev)
        for k in writes:
            ev = self.last_w.get(k)
            if ev is not None:
                self._need(eng, ev, waits)
                evs.append(ev)
            for ev in self.reads.get(k, {}).values():
                self._need(eng, ev, waits)
                evs.append(ev)
        self._apply(eng, [e for e in evs if e[0] in waits and waits[e[0]] >= e[1]], waits)
        return waits

    def _commit(self, ev, reads, writes):
        for k in reads:
            self.reads.setdefault(k, {})[ev[0]] = ev
        for k in writes:
            self.last_w[k] = ev
            self.reads[k] = {}

    def op(self, eng, fn, reads=(), writes=()):
        waits = self._deps(eng, reads, writes)
        self.cnt[eng] += 1
        val = self.cnt[eng]
        snap = dict(self.clock[eng])
        ev = (eng, val, snap)
        self.streams[eng].append((waits, fn, ("c", eng, 1)))
        self.n_waits += len(waits)
        self._commit(ev, reads, writes)

    def dma(self, q, out, in_, reads=(), writes=(), **kw):
        if not hasattr(self, "dnq"):
            self.dnq = {"sp": 0, "pool": 0, "act": 0}
        if q == "pool":
            lo, n = 0, 8
        else:
            lo, n = 8, N_DMA_SEMS - 8
        i = lo + self.dnq[q] % n
        self.dnq[q] += 1
        waits = self._deps(q, reads, writes)
        key = ("d", i)
        prev = self.dcnt[i]
        if prev > 0 and self.clock[q].get(key, 0) < prev:
            waits[key] = max(waits.get(key, 0), prev)
            self.clock[q][key] = prev
        self.dcnt[i] += 16
        ev = (key, self.dcnt[i], dict(self.clock[q]))
        fn = lambda e, out=out, in_=in_, kw=kw: e.dma_start(out=out, in_=in_, **kw)
        self.streams[q].append((waits, fn, ("d", i, 16)))
        self._commit(ev, reads, writes)

    def dma_custom(self, q, fn, reads=(), writes=()):
        if not hasattr(self, "dnq"):
            self.dnq = {"sp": 0, "pool": 0, "act": 0}
        if q == "pool":
            lo, n = 0, 8
        else:
            lo, n = 8, N_DMA_SEMS - 8
        i = lo + self.dnq[q] % n
        self.dnq[q] += 1
        waits = self._deps(q, reads, writes)
        key = ("d", i)
        prev = self.dcnt[i]
        if prev > 0 and self.clock[q].get(key, 0) < prev:
            waits[key] = max(waits.get(key, 0), prev)
            self.clock[q][key] = prev
        self.dcnt[i] += 16
        ev = (key, self.dcnt[i], dict(self.clock[q]))
        self.streams[q].append((waits, fn, ("d", i, 16)))
        self._commit(ev, reads, writes)

    def coll(self, fn, reads=(), writes=()):
        if not hasattr(self, "csem"):
            self.csem = self.stack.enter_context(self.nc.semaphore("s_cc"))
            self.ccnt = 0
        waits = self._deps("pool", reads, writes)
        self.ccnt += 1
        ev = ("cc", self.ccnt, dict(self.clock["pool"]))
        self.streams["pool"].append((waits, fn, ("cc", None, 1)))
        self._commit(ev, reads, writes)

    def fence(self):
        tgt = {}
        if getattr(self, "ccnt", 0) > 0:
            tgt["cc"] = self.ccnt
        for e in self.ENGS:
            if self.cnt[e] > 0:
                tgt[e] = self.cnt[e]
        for i in range(N_DMA_SEMS):
            if self.dcnt[i] > 0:
                tgt[("d", i)] = self.dcnt[i]
        for e in self.ENGS:
            waits = {}
            for k, v in tgt.items():
                if k == e:
                    continue
                if self.clock[e].get(k, 0) < v:
                    waits[k] = v
                    self.clock[e][k] = v
            if waits:
                self.streams[e].append((waits, None, None))
        for e in self.ENGS:
            for k, v in tgt.items():
                if k != e and self.clock[e].get(k, 0) < v:
                    self.clock[e][k] = v

    def wait_all(self, eng):
        waits = {}
        for e in self.ENGS:
            if e != eng and self.cnt[e] > self.clock[eng].get(e, 0):
                waits[e] = self.cnt[e]
        for i in range(N_DMA_SEMS):
            if self.dcnt[i] > self.clock[eng].get(("d", i), 0):
                waits[("d", i)] = self.dcnt[i]
        if getattr(self, "ccnt", 0) > self.clock[eng].get("cc", 0):
            waits["cc"] = self.ccnt
        self.streams[eng].append((waits, None, None))

    def _semof(self, key):
        if isinstance(key, tuple):
            return self.dsem[key[1]]
        if key == "cc":
            return self.csem
        return self.sem[key]

    def emit(self):
        nc = self.nc
        with nc.Block() as block:
            def replay(name):
                def run(e):
                    for waits, fn, inc in self.streams[name]:
                        for key, val in waits.items():
                            e.wait_ge(self._semof(key), val)
                        if fn is None:
                            continue
                        ins = fn(e)
                        if inc[0] == "c":
                            ins.then_inc(self.sem[inc[1]], 1)
                        elif inc[0] == "cc":
                            ins.then_inc(self.csem)
                        else:
                            ins.then_inc(self.dsem[inc[1]], 16)
                return run
            block.tensor(replay("pe"))
            block.scalar(replay("act"))
            block.vector(replay("dve"))
            block.gpsimd(replay("pool"))
            block.sync(replay("sp"))


NPBF = ml_dtypes.bfloat16
EPS = 1e-6
D = 1024
IN_W = 3218
DFF = 2816
SLABS = [(0, 384), (384, 768), (768, 1152), (1152, 1536), (1536, 1920), (1920, 2432), (2432, 2706), (2706, 3218)]


def mm(S, out, lhsT, rhs, start=True, stop=True, r=(), w=(), sgc=False):
    if sgc:
        S.op("pe", lambda e: e.matmul(out, lhsT, rhs, start=start, stop=stop, skip_group_check=True), reads=r, writes=w)
    else:
        S.op("pe", lambda e: e.matmul(out, lhsT, rhs, start=start, stop=stop), reads=r, writes=w)

def tr(S, out, in_, ident, r=(), w=()):
    S.op("pe", lambda e: e.transpose(out, in_, ident), reads=r, writes=w)

def act(S, out, in_, func, r=(), w=(), **kw):
    S.op("act", lambda e: e.activation(out, in_, func, **kw), reads=r, writes=w)

def cp(S, eng, out, in_, r=(), w=()):
    if eng == "act":
        S.op("act", lambda e: e.copy(out, in_), reads=r, writes=w)
    else:
        S.op(eng, lambda e: e.tensor_copy(out, in_), reads=r, writes=w)

def tt(S, eng, out, a, b, op, r=(), w=()):
    S.op(eng, lambda e: e.tensor_tensor(out, a, b, op), reads=r, writes=w)

def ts(S, eng, out, a, s1, s2, op0, op1=None, r=(), w=()):
    if op1 is None:
        S.op(eng, lambda e: e.tensor_scalar(out, a, s1, s2, op0), reads=r, writes=w)
    else:
        S.op(eng, lambda e: e.tensor_scalar(out, a, s1, s2, op0, op1), reads=r, writes=w)

def stt(S, eng, out, in0, scalar, in1, op0, op1, r=(), w=()):
    S.op(eng, lambda e: e.scalar_tensor_tensor(out, in0, scalar, in1, op0, op1), reads=r, writes=w)

def red(S, eng, out, in_, r=(), w=()):
    S.op(eng, lambda e: e.tensor_reduce(out, in_, AX.X, ALU.add), reads=r, writes=w)

def rsq(S, out, tmp, in_, scale, bias, r=(), w=(), wt=()):
    S.op("act", lambda e: e.activation(tmp, in_, AF.Sqrt, bias=bias, scale=scale), reads=r, writes=wt)
    S.op("dve", lambda e: e.reciprocal(out, tmp), reads=wt, writes=w)

def mset(S, eng, ap, val, w=()):
    S.op(eng, lambda e: e.memset(ap, val), writes=w)


def host_consts_A(b, r, NQ):
    c = {}
    inv = (10000.0 ** (-np.arange(32, dtype=np.float32) / 32)).astype(np.float32)
    gam = 1.0 - 2.0 ** (-5.0 - np.arange(6, dtype=np.float64))
    n = np.arange(128, dtype=np.float64)
    gq = gam[None, :] ** n[:, None]
    gk = gam[None, :] ** (-n[:, None]) / 8.0
    rq = np.zeros((NQ, 128, 2, 6, 32), np.float32)
    rk = np.zeros((NQ, 128, 2, 6, 32), np.float32)
    for j in range(NQ):
        i = 4 * j + r
        t = (128 * i + np.arange(128)).astype(np.float32)
        ang = t[:, None] * inv[None, :]
        cs = np.stack([np.cos(ang), np.sin(ang)], 1).astype(np.float64)
        rq[j] = (cs[:, :, None, :] * gq[:, None, :, None]).astype(np.float32)
        rk[j] = (cs[:, :, None, :] * gk[:, None, :, None]).astype(np.float32)
    c["ropeq"] = rq.reshape(NQ, 128, 384)
    c["ropek"] = rk.reshape(NQ, 128, 384)
    m = np.arange(128)
    c["cmask"] = (m[:, None] <= m[None, :]).astype(np.float32)
    c["idn"] = np.eye(128, dtype=np.float32)
    return c


def layer_inputs_A(inp, l):
    d = {}
    d["w_in"] = np.ascontiguousarray(inp["w_in"][l])
    d["anorm"] = np.ascontiguousarray(np.broadcast_to(inp["attn_norm"][l][None, :], (128, 1024)))
    d["gq"] = np.ascontiguousarray(np.broadcast_to(np.tile(inp["nsa_q_gain"][l], 6)[None, :], (128, 384)))
    kg = inp["nsa_k_gain"][l]
    d["kg"] = np.ascontiguousarray(np.broadcast_to(np.concatenate([kg[1], kg[1], kg[2], kg[2]])[None, :], (128, 256)))
    d["wsT"] = np.ascontiguousarray(inp["gm_ws"][l].transpose(2, 0, 1))
    d["gbT"] = np.ascontiguousarray(inp["gm_b"][l].T)
    d["w1"] = np.ascontiguousarray(inp["cmp_w1"][l])
    return d


def declare_A(nc, NQ, xin_kind="ExternalInput", out_kind="ExternalOutput"):
    T = {}
    def di(name, shape, dt=F32):
        T[name] = nc.dram_tensor(name, list(shape), dt, kind="ExternalInput").ap()
    def do(name, shape, dt=F32):
        T[name] = nc.dram_tensor(name, list(shape), dt, kind=out_kind).ap()
    T["xin"] = nc.dram_tensor("xin", [NQ, 128, 1024], F32, kind=xin_kind).ap()
    di("w_in", [1024, IN_W]); di("anorm", [128, 1024]); di("gq", [128, 384]); di("kg", [128, 256])
    di("wsT", [128, 4, 128]); di("gbT", [128, 4]); di("w1", [2, 2048, 64])
    di("ropeq", [NQ, 128, 384]); di("ropek", [NQ, 128, 384]); di("cmask", [128, 128]); di("idn", [128, 128])
    do("o_ks", [2, 2, 64, NQ, 128], BF16)
    do("o_v", [2, NQ, 128, 128], BF16)
    do("o_U", [NQ, 64, 384])
    do("o_cA", [4, 2, 64, NQ * 8])
    do("o_rqT", [NQ, 64, 768], BF16)
    do("o_inner", [NQ, 128, 384])
    do("o_sg", [NQ, 128, 384], BF16)
    do("o_qnT", [NQ, 64, 768], BF16)
    do("o_gates", [NQ, 128, 18])
    do("o_gmT", [NQ, 128, 256], BF16)
    return T


def emit_A(nc, S, T, NQ):
    with ExitStack() as st:
        sb = lambda name, shape, dt: st.enter_context(nc.sbuf_tensor("A_" + T.get("tag", "") + name, shape, dt))
        ps = lambda name, shape, dt: st.enter_context(nc.psum_tensor("A_" + T.get("tag", "") + name, shape, dt))
        w_sb = sb("w_sb", [128, 8, IN_W], BF16)
        gain = sb("gain", [128, 1024], F32)
        gq = sb("gq", [128, 384], F32)
        kg = sb("kg", [128, 256], F32)
        wc = sb("wc", [128, 4, 128], BF16)
        wcf = sb("wcf", [128, 4, 128], F32)
        gbT = sb("gbT", [128, 4], F32)
        w1 = sb("w1", [64, 2, 32, 64], BF16)
        cmask = sb("cmask", [128, 128], F32)
        ident = sb("ident", [128, 128], BF16)
        acT = sb("acT", [64, 4, NQ * 128], BF16)
        x_t = [sb("x%d" % p, [128, 1024], F32) for p in range(2)]
        rq_t = [sb("rq%d" % p, [128, 384], F32) for p in range(2)]
        rk_t = [sb("rk%d" % p, [128, 384], F32) for p in range(2)]
        junk = sb("junk", [128, 1024], BF16)
        ss = sb("ss", [128, 40], F32)
        h = sb("h", [128, 1024], BF16)
        hTs = [sb("hT%d" % i, [128, 1024], BF16) for i in range(2)]
        rf = sb("rf", [128, 384], F32)
        t1 = sb("t1", [128, 192], F32); t2 = sb("t2", [128, 192], F32)
        t3 = sb("t3", [128, 192], F32); t4 = sb("t4", [128, 192], F32)
        qp = sb("qp", [128, 384], BF16)
        kp = sb("kp", [128, 384], BF16)
        qpT = sb("qpT", [64, 768], BF16)
        kpT = sb("kpT", [64, 768], BF16)
        v = sb("v", [128, 384], BF16)
        scT = sb("scT", [128, 768], BF16)
        inner = sb("inner", [128, 384], F32)
        U = sb("U", [64, 384], F32)
        sg = sb("sg", [128, 384], BF16)
        sq = sb("sq", [128, 512], F32)
        qn = sb("qn", [128, 384], F32)
        qnb = sb("qnb", [128, 384], BF16)
        qnT = sb("qnT", [64, 768], BF16)
        ac = sb("ac", [128, 256], BF16)
        kns = [sb("kn%d" % i, [128, 128], F32) for i in range(2)]
        knbs = [sb("knb%d" % i, [128, 128], BF16) for i in range(2)]
        ksT = sb("ksT", [64, 256], BF16)
        vsb = sb("vsb", [128, 130], BF16)
        gates = sb("gates", [128, 18], F32)
        zf = sb("zf", [128, 512], F32)
        z2 = sb("z2", [128, 512], F32)
        zg = sb("zg", [128, 512], F32)
        vn = sb("vn", [128, 256], BF16)
        gm1 = sb("gm1", [128, 256], F32)
        gmb = sb("gmb", [128, 256], BF16)
        gmT = sb("gmT", [128, 256], BF16)
        cA = sb("cA", [64, NQ * 8], F32)
        pT = ps("pT", [128, 1024], BF16)
        pP = [ps("pP%d" % p, [128, 512], F32) for p in range(2)]
        pTr = ps("pTr", [128, 1024], BF16)
        pSc = ps("pSc", [128, 1024], F32)
        pIn = ps("pIn", [128, 512], F32)
        pU = ps("pU", [128, 512], F32)

        wv = T["w_in"].rearrange("(c p) f -> p c f", p=128)
        for c in range(8):
            S.dma("pool", w_sb[:, c, :], wv[:, c, :], writes=["w_sb"])
        S.dma("sp", gain[:], T["anorm"], writes=["gain"])
        S.dma("sp", gq[:], T["gq"], writes=["gq"])
        S.dma("sp", kg[:], T["kg"], writes=["kg"])
        S.dma("sp", wcf[:], T["wsT"], writes=["wcf"])
        S.dma("sp", gbT[:], T["gbT"], writes=["gbT"])
        S.dma("sp", cmask[:], T["cmask"], writes=["cmask"])
        S.dma("pool", ident[:], T["idn"], writes=["ident"])
        S.dma("pool", w1[:], T["w1"].rearrange("k (l d) o -> d k l o", d=64), writes=["w1"])
        mset(S, "pool", vsb[:], 1.0, w=["vsb", "vsb1"])
        tt(S, "dve", wc[:], wcf[:], cmask[:].unsqueeze(1).to_broadcast([128, 4, 128]), ALU.mult, r=["wcf", "cmask"], w=["wc"])

        defer = []
        def run_deferred(keep=0):
            while len(defer) > keep:
                defer.pop(0)()
        v3 = lambda a: a[:].rearrange("p (h d) -> p h d", h=6)

        def head(j):
            p = j % 2
            X = "x%d" % p
            S.dma("sp", x_t[p][:], T["xin"][j], writes=[X])
            S.dma("sp", rq_t[p][:], T["ropeq"][j], writes=["rq%d" % p])
            S.dma("sp", rk_t[p][:], T["ropek"][j], writes=["rk%d" % p])
            act(S, junk[:], x_t[p][:], AF.Square, r=[X], w=["junk", "ss0"], accum_out=ss[:, 0:1])
            rsq(S, ss[:, 2:3], ss[:, 1:2], ss[:, 0:1], 1.0 / 1024, EPS, r=["ss0"], wt=["ss1"], w=["ss2"])
            stt(S, "dve", h[:], x_t[p][:], ss[:, 2:3], gain[:], ALU.mult, ALU.mult, r=[X, "ss2", "gain"], w=["h"])

        def head_pe(j):
            p = j % 2
            for c in range(8):
                tr(S, pT[:, c * 128:(c + 1) * 128], h[:, c * 128:(c + 1) * 128], ident[:], r=["h", "ident"], w=["pT"])
            cp(S, "act", hTs[p][:], pT[:], r=["pT"], w=["hT%d" % p])

        head(0)
        head_pe(0)
        for j in range(NQ):
            p = j % 2
            hT = hTs[p]; HT = "hT%d" % p
            for s, (c0, c1) in enumerate(SLABS):
                wd = c1 - c0
                pp = pP[s % 2]
                PP = "pP%d" % (s % 2)
                for c in range(8):
                    mm(S, pp[:, 0:wd], hT[:, c * 128:(c + 1) * 128], w_sb[:, c, c0:c1], start=(c == 0), stop=(c == 7),
                       r=[HT, "w_sb"], w=[PP])
                run_deferred(2)
                if s == 4 and j + 1 < NQ:
                    head(j + 1)
                if s == 6 and j + 1 < NQ:
                    head_pe(j + 1)
                if s in (0, 1):
                    tab = rq_t[p] if s == 0 else rk_t[p]
                    TAB = ("rq%d" if s == 0 else "rk%d") % p
                    dst = qp if s == 0 else kp
                    DST = "qp" if s == 0 else "kp"
                    dT = qpT if s == 0 else kpT
                    DT = "qpT" if s == 0 else "kpT"
                    cp(S, "act", rf[:], pp[:, 0:384], r=[PP], w=["rf"])
                    xv = rf[:].rearrange("p (h t d) -> p h t d", h=6, t=2)
                    tv = tab[:].rearrange("p (t h d) -> p t h d", t=2, h=6)
                    x1, x2 = xv[:, :, 0, :], xv[:, :, 1, :]
                    co, si = tv[:, 0], tv[:, 1]
                    tt(S, "dve", v3(t1), x1, co, ALU.mult, r=["rf", TAB], w=["t1"])
                    tt(S, "pool", v3(t2), x2, si, ALU.mult, r=["rf", TAB], w=["t2"])
                    tt(S, "dve", v3(t3), x2, co, ALU.mult, r=["rf", TAB], w=["t3"])
                    tt(S, "pool", v3(t4), x1, si, ALU.mult, r=["rf", TAB], w=["t4"])
                    dv = dst[:].rearrange("p (h t d) -> p h t d", h=6, t=2)
                    tt(S, "dve", dv[:, :, 0, :], v3(t1), v3(t2), ALU.subtract, r=["t1", "t2"], w=[DST])
                    tt(S, "pool", dv[:, :, 1, :], v3(t3), v3(t4), ALU.add, r=["t3", "t4"], w=[DST])
                    def st2(j=j, s=s, dst=dst, DST=DST, dT=dT, DT=DT):
                        for hh in range(6):
                            tr(S, pTr[0:64, hh * 128:(hh + 1) * 128], dst[:, hh * 64:(hh + 1) * 64], ident[:], r=[DST, "ident"], w=["pTr"])
                        cp(S, "act", dT[:], pTr[0:64, 0:768], r=["pTr"], w=[DT])
                        if s == 0:
                            S.dma("sp", T["o_rqT"][j], qpT[:], reads=["qpT"])
                    defer.append(st2)
                elif s == 2:
                    cp(S, "act", v[:], pp[:, 0:384], r=[PP], w=["v"])
                    def st2(j=j):
                        for hh in range(6):
                            mm(S, pSc[:, hh * 128:(hh + 1) * 128], kpT[:, hh * 128:(hh + 1) * 128], qpT[:, hh * 128:(hh + 1) * 128],
                               r=["kpT", "qpT"], w=["pSc"])
                        tt(S, "dve", scT[:].rearrange("p (h n) -> p h n", h=6), pSc[:, 0:768].rearrange("p (h n) -> p h n", h=6),
                           cmask[:].unsqueeze(1).to_broadcast([128, 6, 128]), ALU.mult, r=["pSc", "cmask"], w=["scT"])
                        for hh in range(6):
                            mm(S, pU[0:64, hh * 64:(hh + 1) * 64], kp[:, hh * 64:(hh + 1) * 64], v[:, hh * 64:(hh + 1) * 64],
                               r=["kp", "v"], w=["pU"])
                        cp(S, "act", U[:], pU[0:64, 0:384], r=["pU"], w=["U"])
                        S.dma("sp", T["bU%d" % (j // ((NQ + 1) // 2))].rearrange("(j d) e -> j d e", d=64)[j % ((NQ + 1) // 2)], U[:], reads=["U"])
                        for hh in range(6):
                            mm(S, pIn[:, hh * 64:(hh + 1) * 64], scT[:, hh * 128:(hh + 1) * 128], v[:, hh * 64:(hh + 1) * 64],
                               r=["scT", "v"], w=["pIn"])
                        cp(S, "act", inner[:], pIn[:, 0:384], r=["pIn"], w=["inner"])
                        S.dma("sp", T["o_inner"][j], inner[:], reads=["inner"])
                    defer.append(st2)
                elif s == 3:
                    act(S, sg[:], pp[:, 0:384], AF.Silu, r=[PP], w=["sg"])
                    S.dma("sp", T["o_sg"][j], sg[:], reads=["sg"])
                elif s == 4:
                    act(S, sq[:, 0:384], pp[:, 0:384], AF.Square, r=[PP], w=["sq"])
                    red(S, "dve", ss[:, 3:9], sq[:, 0:384].rearrange("p (h d) -> p h d", h=6), r=["sq"], w=["sqa"])
                    rsq(S, ss[:, 9:15], ss[:, 33:39], ss[:, 3:9], 1.0, 64 * EPS, r=["sqa"], wt=["sqt"], w=["sqb"])
                    tt(S, "dve", qn[:].rearrange("p (h d) -> p h d", h=6), pp[:, 0:384].rearrange("p (h d) -> p h d", h=6),
                       ss[:, 9:15].unsqueeze(2).to_broadcast([128, 6, 64]), ALU.mult, r=[PP, "sqb"], w=["qn"])
                    tt(S, "pool", qnb[:], qn[:], gq[:], ALU.mult, r=["qn", "gq"], w=["qnb"])
                    def st2(j=j):
                        for hh in range(6):
                            tr(S, pTr[0:64, hh * 128:(hh + 1) * 128], qnb[:, hh * 64:(hh + 1) * 64], ident[:], r=["qnb", "ident"], w=["pTr"])
                        cp(S, "act", qnT[:], pTr[0:64, 0:768], r=["pTr"], w=["qnT"])
                        S.dma("sp", T["o_qnT"][j], qnT[:], reads=["qnT"])
                    defer.append(st2)
                elif s in (5, 6):
                    if s == 5:
                        cp(S, "act", ac[:], pp[:, 0:256], r=[PP], w=["ac"])
                        ko, vo, br = 256, 384, 0
                    else:
                        ko, vo, br = 0, 128, 1
                        act(S, gates[:], pp[:, 256:274], AF.Sigmoid, r=[PP], w=["gates"])
                        S.dma("sp", T["o_gates"][j], gates[:], reads=["gates"])
                    act(S, sq[:, 0:128], pp[:, ko:ko + 128], AF.Square, r=[PP], w=["sq"])
                    red(S, "dve", ss[:, 15:17], sq[:, 0:128].rearrange("p (h d) -> p h d", h=2), r=["sq"], w=["ska"])
                    rsq(S, ss[:, 19:21], ss[:, 17:19], ss[:, 15:17], 1.0 / 64, EPS, r=["ska"], wt=["skb"], w=["skc"])
                    kn = kns[br]; knb = knbs[br]
                    tt(S, "dve", kn[:].rearrange("p (h d) -> p h d", h=2), pp[:, ko:ko + 128].rearrange("p (h d) -> p h d", h=2),
                       ss[:, 19:21].unsqueeze(2).to_broadcast([128, 2, 64]), ALU.mult, r=[PP, "skc"], w=["kn%d" % br])
                    tt(S, "pool", knb[:], kn[:], kg[:, br * 128:(br + 1) * 128], ALU.mult, r=["kn%d" % br, "kg"], w=["knb%d" % br])
                    cp(S, "act", vsb[:].rearrange("p (g e) -> p g e", g=2)[:, :, 0:64], pp[:, vo:vo + 128].rearrange("p (g d) -> p g d", g=2), r=[PP], w=["vsb"])
                    S.dma("sp", T["bv%d" % br].rearrange("(j n) c -> j n c", j=NQ)[j], vsb[:], reads=["vsb", "vsb1"])
                    def st2(j=j, s=s, br=br, knb=knb):
                        if s == 5:
                            for sl in range(4):
                                tr(S, pTr[0:64, sl * 128:(sl + 1) * 128], ac[:, sl * 64:(sl + 1) * 64], ident[:], r=["ac", "ident"], w=["pTr"])
                            cp(S, "act", acT[:, :, j * 128:(j + 1) * 128], pTr[0:64, 0:512].rearrange("p (s n) -> p s n", s=4), r=["pTr"], w=["acT"])
                        for g in range(2):
                            tr(S, pTr[0:64, g * 128:(g + 1) * 128], knb[:, g * 64:(g + 1) * 64], ident[:], r=["knb%d" % br, "ident"], w=["pTr"])
                        cp(S, "act", ksT[:], pTr[0:64, 0:256], r=["pTr"], w=["ksT"])
                        bks4 = T["bks%d" % br].rearrange("(g d) (j n) -> g d j n", g=2, j=NQ)
                        S.dma("sp", bks4[:, :, j, :].rearrange("g d n -> d g n"), ksT[:].rearrange("p (g n) -> p g n", g=2), reads=["ksT"])
                    defer.append(st2)
                else:
                    cp(S, "act", zf[:], pp[:, 0:512], r=[PP], w=["zf"])
                    tt(S, "pool", z2[:], zf[:], zf[:], ALU.mult, r=["zf"], w=["z2"])
                    ts(S, "dve", z2[:], z2[:], 0.044715, 1.0, ALU.mult, ALU.add, r=["z2"], w=["z2"])
                    tt(S, "pool", z2[:], z2[:], zf[:], ALU.mult, r=["z2", "zf"], w=["z2"])
                    act(S, zg[:], z2[:], AF.Sigmoid, r=["z2"], w=["zg"], scale=1.5957691216057308)
                    tt(S, "dve", zg[:], zg[:], zf[:], ALU.mult, r=["zg", "zf"], w=["zg"])
                    act(S, sq[:, 0:256], zg[:, 256:512], AF.Square, r=["zg"], w=["sq"])
                    red(S, "dve", ss[:, 21:25], sq[:, 0:256].rearrange("p (h d) -> p h d", h=4), r=["sq"], w=["sga"])
                    rsq(S, ss[:, 29:33], ss[:, 25:29], ss[:, 21:25], 1.0 / 64, EPS, r=["sga"], wt=["sgb"], w=["sgc"])
                    tt(S, "dve", vn[:].rearrange("p (h d) -> p h d", h=4), zg[:, 256:512].rearrange("p (h d) -> p h d", h=4),
                       ss[:, 29:33].unsqueeze(2).to_broadcast([128, 4, 64]), ALU.mult, r=["zg", "sgc"], w=["vn"])
                    def st2(j=j):
                        for g in range(4):
                            mm(S, pU[:, g * 64:(g + 1) * 64], wc[:, g, :], vn[:, g * 64:(g + 1) * 64], r=["wc", "vn"], w=["pU"])
                        tt(S, "dve", gm1[:].rearrange("p (h d) -> p h d", h=4), pU[:, 0:256].rearrange("p (h d) -> p h d", h=4),
                           gbT[:].unsqueeze(2).to_broadcast([128, 4, 64]), ALU.add, r=["pU", "gbT"], w=["gm1"])
                        tt(S, "pool", gmb[:], gm1[:], zg[:, 0:256], ALU.mult, r=["gm1", "zg"], w=["gmb"])
                        for c in range(2):
                            tr(S, pTr[:, c * 128:(c + 1) * 128], gmb[:, c * 128:(c + 1) * 128], ident[:], r=["gmb", "ident"], w=["pTr"])
                        cp(S, "act", gmT[:], pTr[:, 0:256], r=["pTr"], w=["gmT"])
                        S.dma("sp", T["o_gmT"][j], gmT[:], reads=["gmT"])
                    defer.append(st2)
        run_deferred()
        NG = NQ * 8
        for sl in range(4):
            kv = sl // 2
            for half in range(2):
                for lp in range(16):
                    mm(S, pU[0:64, 0:NG], w1[:, kv, half * 16 + lp, :], acT[:, sl, lp::16], start=(lp == 0), stop=(lp == 15),
                       r=["w1", "acT"], w=["pU"])
                cp(S, "act", cA[:], pU[0:64, 0:NG], r=["pU"], w=["cA"])
                S.dma("sp", T["bcA"].rearrange("(s h o) c -> s h o c", s=4, h=2)[sl, half], cA[:], reads=["cA"])


import math

def t5_bucket_np(dist):
    n = np.maximum(dist, 0)
    nf = np.maximum(n, 1).astype(np.float32)
    large = 16 + (np.log(nf / np.float32(16)) / np.float32(math.log(8.0)) * np.float32(16)).astype(np.int32)
    large = np.minimum(large, 31)
    return np.where(n < 16, n, large)


def host_consts_B(NQ):
    NB = 4 * NQ
    c = {}
    c["idn"] = np.eye(128, dtype=np.float32)
    E = np.zeros((128, NB, 128), np.float32)
    for kb in range(NB):
        if 2 * kb < 128:
            E[2 * kb, kb, 0:64] = 1
            E[2 * kb + 1, kb, 64:128] = 1
    c["E"] = E
    def ov(cs, ssv):
        return np.clip(np.minimum(cs + 32, ssv + 64) - np.maximum(cs, ssv), 0, None) // 16
    n = np.arange(512)
    j = np.arange(128)
    OV = ov(n[:, None] * 16, j[None, :] * 64).astype(np.float32)
    c["OVF"] = np.ascontiguousarray(OV.reshape(4, 128, 128).transpose(1, 0, 2))
    npr = np.arange(16)
    dl = np.arange(256) - 128
    c["OVB"] = ov((-128 + 16 * npr)[:, None], (64 * dl)[None, :]).astype(np.float32)
    Dt = np.zeros((128, 256), np.float32)
    for ql in range(128):
        for idx in range(256):
            d = idx - 128
            if ql < 64:
                v = 1e9 if d == -1 else 2e9 if d == 0 else -1e9 if d >= 1 else 0.0
            else:
                v = 1e9 if d == 0 else 2e9 if d == 1 else -1e9 if d >= 2 else 0.0
            Dt[ql, idx] = v
    c["Dt"] = Dt
    gam = 1.0 - 2.0 ** (-5.0 - np.arange(6, dtype=np.float64))
    rdec = np.stack([gam ** 128, gam ** 127, gam], 0)
    c["rdec"] = np.ascontiguousarray(np.broadcast_to(rdec[None, :, :, None], (64, 3, 6, 64)).reshape(64, 3, 384)).astype(np.float32)
    kl = np.arange(128)[:, None]; ql = np.arange(128)[None, :]
    m0 = (ql >= kl).astype(np.float32)
    c["mask0"] = np.ascontiguousarray(np.broadcast_to(m0[:, None, :], (128, 3, 128)).reshape(128, 384))
    c["neg0"] = (c["mask0"] - 1.0) * 30000.0
    m4 = (kl > ql).astype(np.float32)
    c["B4"] = np.ascontiguousarray(np.broadcast_to(((m4 - 1.0) * 30000.0)[:, None, :], (128, 3, 128)).reshape(128, 384))
    distc = ql + 97 - 16 * np.arange(16)[:, None]
    mc = (distc >= 0).astype(np.float32)
    mc0 = mc * (np.arange(16)[:, None] >= 8)
    bc3 = lambda a: np.ascontiguousarray(np.broadcast_to(a[:, None, :], (16, 3, 128)).reshape(16, 384))
    c["maskc"] = np.stack([bc3(mc), bc3(mc0)], 0)
    c["negc"] = (c["maskc"] - 1.0) * 30000.0
    return c


def bias_tabs(rel_bias):
    kl = np.arange(128)[:, None]; ql = np.arange(128)[None, :]
    b0 = t5_bucket_np(ql - kl); b1 = t5_bucket_np(128 + ql - kl)
    bcn = t5_bucket_np(ql + 97 - 16 * np.arange(16)[:, None])
    rb = rel_bias.reshape(32, 2, 3)
    d = {}
    d["tab0"] = np.ascontiguousarray(rb[b0].transpose(2, 0, 3, 1).reshape(2, 128, 384))
    d["tab1"] = np.ascontiguousarray(rb[b1].transpose(2, 0, 3, 1).reshape(2, 128, 384))
    d["tabc"] = np.ascontiguousarray(rb[bcn].transpose(2, 0, 3, 1).reshape(2, 16, 384))
    t31 = rb[31]
    d["tab31"] = np.ascontiguousarray(np.broadcast_to(t31[:, None, :, None], (2, 128, 3, 128)).reshape(2, 128, 384))
    return d


def layer_inputs_B(inp, l):
    d = {}
    d["w_out"] = np.ascontiguousarray(inp["w_out"][l])
    d["fnorm"] = np.ascontiguousarray(np.broadcast_to(inp["ffn_norm"][l][None, :], (128, 1024)))
    d["w_gu"] = np.ascontiguousarray(inp["w_gate_up"][l])
    d["w_dn"] = np.ascontiguousarray(inp["w_down"][l])
    d["pe"] = np.ascontiguousarray(inp["cmp_pe"][l].reshape(2, 16, 128).transpose(2, 0, 1))
    d["w1f"] = np.ascontiguousarray(inp["cmp_w1"][l])
    d["w2"] = np.ascontiguousarray(inp["cmp_w2"][l].transpose(1, 0, 2))
    d["kg0"] = np.ascontiguousarray(np.broadcast_to(inp["nsa_k_gain"][l][0][None, :], (128, 64)))
    return d


def declare_B(nc, NQ, in_kind="ExternalInput", out_kind="ExternalOutput", decl_x=True):
    NB = 4 * NQ
    T = {}
    def di(name, shape, dt=F32, kind="ExternalInput"):
        T[name] = nc.dram_tensor(name, list(shape), dt, kind=kind).ap()
    if decl_x:
        di("xin", [NQ, 128, 1024])
    di("g_ks", [4, 2, 2, 64, NQ, 128], BF16, in_kind)
    di("g_v", [4, 2, NQ, 128, 128], BF16, in_kind)
    di("g_U", [4, NQ, 64, 384], F32, in_kind)
    di("g_cA", [4, 4, 2, 64, NQ * 8], F32, in_kind)
    for nm, shp, dt in (("o_rqT", [NQ, 64, 768], BF16), ("o_inner", [NQ, 128, 384], F32), ("o_sg", [NQ, 128, 384], BF16),
                        ("o_qnT", [NQ, 64, 768], BF16), ("o_gates", [NQ, 128, 18], F32), ("o_gmT", [NQ, 128, 256], BF16)):
        di(nm, shp, dt, in_kind)
    di("w_out", [1024, 1024]); di("fnorm", [128, 1024]); di("w_gu", [1024, 2 * DFF]); di("w_dn", [DFF, 1024])
    di("pe", [128, 2, 16]); di("w1f", [2, 2048, 64]); di("w2", [64, 2, 64]); di("kg0", [128, 64])
    di("idn", [128, 128]); di("E", [128, NB, 128]); di("OVF", [128, 4, 128]); di("OVB", [16, 256]); di("Dt", [128, 256])
    di("rdec", [64, 3, 384]); di("mask0", [128, 384]); di("neg0", [128, 384]); di("B4", [128, 384])
    di("maskc", [2, 16, 384]); di("negc", [2, 16, 384])
    di("tab0", [2, 128, 384]); di("tab1", [2, 128, 384]); di("tabc", [2, 16, 384]); di("tab31", [2, 128, 384])
    T["xout"] = nc.dram_tensor("xout", [NQ, 128, 1024], F32, kind=out_kind).ap()
    return T


def gelu_ops(S, out_bf, x, tmp, tmp2, keys):
    kx, kt, kt2, ko = keys
    tt(S, "pool", tmp, x, x, ALU.mult, r=[kx], w=[kt])
    ts(S, "dve", tmp, tmp, 0.044715, 1.0, ALU.mult, ALU.add, r=[kt], w=[kt])
    tt(S, "pool", tmp, tmp, x, ALU.mult, r=[kt, kx], w=[kt])
    act(S, tmp2, tmp, AF.Sigmoid, r=[kt], w=[kt2], scale=1.5957691216057308)
    tt(S, "dve", out_bf, tmp2, x, ALU.mult, r=[kt2, kx], w=[ko])


GATH = ["gks0", "gv0", "gU0", "gks1", "gv1", "gU1", "gcA"]
ZS = 5
ZW = 8


def host_consts_Bu(NQ, r):
    NB = 4 * NQ
    NC = 8 * NB - 1
    CW = 8 * NB + 40
    NCHF = (32 * (NQ - 1) + 24 + 127) // 128
    P = 32 - 8 * r
    c0 = host_consts_B(NQ)
    c = {"idn": c0["idn"], "E": c0["E"], "rdec": c0["rdec"]}
    OVBr = np.zeros((16, 264), np.float32); OVBr[:, 2 * r:2 * r + 256] = c0["OVB"]
    Dtr = np.zeros((128, 264), np.float32); Dtr[:, 2 * r:2 * r + 256] = c0["Dt"]; Dtr[:, 2 * r + 256:] = -1e9
    c["OVBr"] = OVBr; c["Dtr"] = Dtr
    n = np.arange(128 * NCHF) - P
    j = np.arange(128)
    cs = n[:, None] * 16; ssv = j[None, :] * 64
    OV = (np.clip(np.minimum(cs + 32, ssv + 64) - np.maximum(cs, ssv), 0, None) // 16).astype(np.float32)
    OV[(n < 0) | (n >= NC)] = 0
    c["OVFr"] = np.ascontiguousarray(OV.reshape(NCHF, 128, 128).transpose(1, 0, 2))
    sel = np.zeros((64, 4), np.float32); sel[:, r] = 1
    c["selr"] = sel
    gam_ = 1.0 - 2.0 ** (-5.0 - np.arange(6, dtype=np.float64))
    dec_, c127_ = gam_ ** 128, gam_ ** 127
    pf = np.zeros((10, 6), np.float64)
    for k in range(4):
        pf[k] = dec_ ** (3 - k) * c127_
        pf[4 + k] = gam_ * c127_ * dec_ ** (r - 1 - k) if r > k else 0.0
    pf[8] = gam_ * dec_ ** r
    pf[9] = dec_ ** 4
    c["pfx"] = np.ascontiguousarray(np.broadcast_to(pf[None, :, :, None], (64, 10, 6, 64)).reshape(64, 10, 384)).astype(np.float32)
    zf0 = np.zeros((128, 1), np.float32); zf0[:P] = -30000.0
    c["zf0"] = zf0
    zmask = np.zeros((ZS + ZW, 128, 384), np.float32); zneg = np.zeros((ZS + ZW, 128, 384), np.float32)
    kinds = zone_kinds(r)
    for s_, kd in enumerate(kinds):
        if kd == "B0":
            zmask[s_] = c0["mask0"]; zneg[s_] = c0["neg0"]
        elif kd == "B1":
            zmask[s_] = 1.0
        elif kd == "NEG":
            zneg[s_] = -30000.0
        elif kd == "B4":
            zneg[s_] = c0["B4"]
    c["zmask"] = zmask; c["zneg"] = zneg
    cm = np.stack([c0["maskc"][0], c0["maskc"][1] if r == 0 else c0["maskc"][0]], 0)
    c["cmask"] = cm; c["cneg"] = (cm - 1.0) * 30000.0
    return c


def zone_kinds(r):
    kinds = []
    for s_ in range(ZS):
        dl = r + 1 - s_
        kinds.append("NEG" if dl < 0 else "B0" if dl == 0 else "B1" if dl == 1 else "Z")
    for s_ in range(ZW):
        dl = r + 4 - s_
        kinds.append("NEG" if (dl < 0 or dl > 4) else "B0" if dl == 0 else "B1" if dl == 1 else "B4" if dl == 4 else "Z")
    return kinds


def bias_tabs_u(rel_bias, r):
    t = bias_tabs(rel_bias)
    kinds = zone_kinds(r)
    ztab = np.zeros((ZS + ZW, 2, 128, 384), np.float32)
    for s_, kd in enumerate(kinds):
        ztab[s_] = t["tab0"] if kd == "B0" else t["tab1"] if kd == "B1" else t["tab31"]
    return {"ztab": ztab, "zt31": t["tab31"], "ctab": t["tabc"]}


def declare_Bu(nc, NQ, in_kind="ExternalInput", out_kind="ExternalOutput"):
    NB = 4 * NQ
    CW = 8 * NB + 40
    NCHF = (32 * (NQ - 1) + 24 + 127) // 128
    T = {}
    def di(name, shape, dt=F32, kind="ExternalInput"):
        T[name] = nc.dram_tensor(name, list(shape), dt, kind=kind).ap()
    di("xin", [NQ, 128, 1024])
    di("ks2", [128, 2, NB * 128], BF16, in_kind)
    di("vs2", [128, 2 * NB * 2, 65], BF16, in_kind)
    di("U2", [NB, 64, 384], F32, in_kind)
    di("cAsh", [64, 8, CW], F32, in_kind)
    for nm, shp, dt in (("o_rqT", [NQ, 64, 768], BF16), ("o_inner", [NQ, 128, 384], F32), ("o_sg", [NQ, 128, 384], BF16),
                        ("o_qnT", [NQ, 64, 768], BF16), ("o_gates", [NQ, 128, 18], F32), ("o_gmT", [NQ, 128, 256], BF16)):
        di(nm, shp, dt, in_kind)
    di("w_out", [1024, 1024]); di("fnorm", [128, 1024]); di("w_gu", [1024, 2 * DFF]); di("w_dn", [DFF, 1024])
    di("pe", [128, 2, 16]); di("w1f", [2, 2048, 64]); di("w2", [64, 2, 64]); di("kg0", [128, 64])
    di("idn", [128, 128]); di("E", [128, NB, 128]); di("OVFr", [128, NCHF, 128]); di("OVBr", [16, 264]); di("Dtr", [128, 264])
    di("rdec", [64, 3, 384]); di("selr", [64, 4]); di("pfx", [64, 10, 384]); di("zf0", [128, 1])
    di("zmask", [ZS + ZW, 128, 384]); di("zneg", [ZS + ZW, 128, 384]); di("ztab", [ZS + ZW, 2, 128, 384]); di("zt31", [2, 128, 384])
    di("ctab", [2, 16, 384]); di("cmask", [2, 16, 384]); di("cneg", [2, 16, 384])
    T["xout"] = nc.dram_tensor("xout", [NQ, 128, 1024], F32, kind=out_kind).ap()
    return T


def emit_B(nc, S, T, NQ):
    NB = 4 * NQ
    CW = 8 * NB + 40
    NCHF = (32 * (NQ - 1) + 24 + 127) // 128
    NCHK = (CW + 127) // 128
    NZ = ZS + ZW
    with ExitStack() as st0:
        sb0 = lambda name, shape, dt: st0.enter_context(nc.sbuf_tensor("B_" + T.get("tag", "") + name, shape, dt))
        ident = sb0("ident", [128, 128], BF16)
        S.dma("pool", ident[:], T["idn"], writes=["ident"])
        with ExitStack() as st:
            sb = lambda name, shape, dt: st.enter_context(nc.sbuf_tensor("B1_" + T.get("tag", "") + name, shape, dt))
            ps = lambda name, shape, dt: st.enter_context(nc.psum_tensor("B1_" + T.get("tag", "") + name, shape, dt))
            ksT = sb("ksT", [128, 2, NB * 128], BF16)
            vs = sb("vs", [128, 2 * NB * 2, 65], BF16)
            kcT = sb("kcT", [128, CW], BF16)
            gT = sb("gT", [64, 4, CW], BF16)
            CV = sb("CV", [128, 2 * NCHF, 193], BF16)
            Gs = sb("Gs", [64, NQ, 384], BF16)
            OVB = sb("OVB", [16, 264], BF16)
            Dt = sb("Dt", [128, 264], F32)
            selr = sb("selr", [64, 4], F32)
            Z = sb("Z", [128, NZ * 2, 384], BF16)
            zf0 = sb("zf0", [128, 1], F32)
            Bc = sb("Bc", [16, 4, 384], BF16)
            w2 = sb("w2", [64, 2, 64], BF16)
            kg0 = sb("kg0", [128, 64], F32)
            pS = [ps("pS%d" % i, [128, 512], F32) for i in range(3)]
            pC = ps("pC", [128, 4, 256], F32)
            pO = ps("pO", [128, 2, 3, 65], F32)
            pR = ps("pR", [128, 512], F32)
            pTr = ps("pTr", [128, 1024], BF16)
            pM = pS[2]

            S.dma("pool", OVB[:], T["OVBr"], writes=["OVB"])
            S.dma("pool", CV[:, 0:NCHF, 65:193], T["OVFr"], writes=["CVo"])
            S.dma("pool", CV[:, NCHF:2 * NCHF, 65:193], T["OVFr"], writes=["CVo"])
            mset(S, "pool", CV[:, :, 64:65], 1.0, w=["CV1"])
            S.dma("sp", Dt[:], T["Dtr"], writes=["Dt"])
            S.dma("sp", selr[:], T["selr"], writes=["selr"])
            S.dma("sp", zf0[:], T["zf0"], writes=["zf0"])
            S.dma("pool", w2[:], T["w2"], writes=["w2"])
            S.dma("sp", kg0[:], T["kg0"], writes=["kg0"])
            import os as _os
            STOP = _os.environ.get("STOPB", "")
            if STOP == "loads":
                S.fence(); return
            stm = ExitStack()
            stm.__enter__()
            if True:
                sbb = lambda name, shape, dt: stm.enter_context(nc.sbuf_tensor("Bb_" + T.get("tag", "") + name, shape, dt))
                zt = [sbb("zt%d" % i, [128, 2, 384], F32) for i in range(1)]
                zm = [sbb("zm%d" % i, [128, 2, 384], F32) for i in range(1)]
                t31 = sbb("t31", [128, 2, 384], F32)
                tc_ = sbb("tc", [16, 2, 384], F32)
                mc = sbb("mc", [16, 4, 384], F32)
                Bcf = sbb("Bcf", [16, 4, 384], F32)
                S.dma("sp", t31[:], T["zt31"].rearrange("g p f -> p g f"), writes=["t31"])
                for s_ in range(NZ):
                    q = 0
                    S.dma("sp", zt[q][:], T["ztab"][s_].rearrange("g p f -> p g f"), writes=["zt%d" % q])
                    S.dma("sp", zm[q][:, 0, :], T["zmask"][s_], writes=["zm%d" % q])
                    S.dma("sp", zm[q][:, 1, :], T["zneg"][s_], writes=["zm%d" % q])
                    tt(S, "dve", zt[q][:], zt[q][:], t31[:], ALU.subtract, r=["zt%d" % q, "t31"], w=["zt%d" % q])
                    tt(S, "dve", zt[q][:], zt[q][:], zm[q][:, 0:1, :].to_broadcast([128, 2, 384]), ALU.mult, r=["zt%d" % q, "zm%d" % q], w=["zt%d" % q])
                    tt(S, "dve", zt[q][:], zt[q][:], zm[q][:, 1:2, :].to_broadcast([128, 2, 384]), ALU.add,
                       r=["zt%d" % q, "zm%d" % q], w=["zt%d" % q])
                    act(S, Z[:, 2 * s_:2 * s_ + 2, :], zt[q][:], AF.Exp, r=["zt%d" % q], w=["Z"])
                S.dma("sp", tc_[:], T["ctab"].rearrange("g p f -> p g f"), writes=["tc"])
                S.dma("sp", mc[:, 0:2, :], T["cmask"].rearrange("v p f -> p v f"), writes=["mc"])
                S.dma("sp", mc[:, 2:4, :], T["cneg"].rearrange("v p f -> p v f"), writes=["mc"])
                tt(S, "dve", tc_[:], tc_[:], t31[0:16, :, :], ALU.subtract, r=["tc", "t31"], w=["tc"])
                for var in range(2):
                    tt(S, "dve", Bcf[:, 2 * var:2 * var + 2, :], tc_[:], mc[:, var:var + 1, :].to_broadcast([16, 2, 384]), ALU.mult, r=["tc", "mc"], w=["Bcf"])
                    tt(S, "dve", Bcf[:, 2 * var:2 * var + 2, :], Bcf[:, 2 * var:2 * var + 2, :], mc[:, 2 + var:3 + var, :].to_broadcast([16, 2, 384]), ALU.add,
                       r=["Bcf", "mc"], w=["Bcf"])
                    act(S, Bc[:, 2 * var:2 * var + 2, :], Bcf[:, 2 * var:2 * var + 2, :], AF.Exp, r=["Bcf"], w=["Bc"])
            gks = [T["gks%d" % b_].rearrange("(q r) c -> q r c", q=4) for b_ in range(2)]
            gv = [T["gv%d" % b_].rearrange("(q j n) c -> q j n c", q=4, j=NQ) for b_ in range(2)]
            for rr in range(4):
                for br in range(2):
                    for g in range(2):
                        dst = ksT[g * 64:(g + 1) * 64, br, :].rearrange("p (j q n) -> p j q n", q=4, n=128)[:, :, rr, :]
                        src = gks[br][rr, g * 64:(g + 1) * 64, :].rearrange("d (j n) -> d j n", n=128)
                        S.dma("sp", dst, src, reads=GATH, writes=["ksT"])
                    dstv = vs[:, br * NB * 2:(br + 1) * NB * 2, :].rearrange("p (j q g) e -> p j q (g e)", q=4, g=2)[:, :, rr, :]
                    S.dma("sp", dstv, gv[br][rr].rearrange("j n c -> n j c"), reads=GATH, writes=["vs", "vs1"])
            if True:
                sbc = lambda name, shape, dt: stm.enter_context(nc.sbuf_tensor("Bc_" + T.get("tag", "") + name, shape, dt))
                cAs = sbc("cAs", [64, 8, CW + 24], F32)
                w1c = sbc("w1c", [128, 2, 16, 64], BF16)
                pec = sbc("pec", [128, 2, 16], BF16)
                cvec = sbc("cvec", [64, 2], F32)
                kcb = sbc("kcb", [128, 128], BF16)
                sqc = sbc("sqc", [128, 64], F32)
                ssc = sbc("ssc", [128, 4], F32)
                cAu = sbc("cAu", [64, 8, 4, NQ * 8], F32)
                gcA = T["gcA"].rearrange("(q s o) c -> q o s c", q=4, s=8)
                for rr in range(4):
                    S.dma("sp", cAu[:, :, rr, :], gcA[rr], reads=GATH, writes=["cAu"])
                mset(S, "pool", cAs[:], 0.0, w=["cAs"])
                for rp in range(4):
                    for rr in range(4):
                        off = 32 - 8 * rp + 8 * rr
                        for s8 in range(8):
                            dstc = cAs[:, s8, off:off + 32 * NQ].rearrange("p (j x) -> p j x", x=32)[:, :, 0:8]
                            srcc = cAu[:, s8, rr, :].rearrange("p (j g) -> p j g", g=8)
                            stt(S, "dve", dstc, srcc, selr[:, rp:rp + 1], dstc, ALU.mult, ALU.add, r=["cAu", "selr", "cAs"], w=["cAs"])
                S.dma("pool", w1c[:], T["w1f"].rearrange("k (c p) o -> p k c o", p=128), writes=["w1c"])
                S.dma("pool", pec[:], T["pe"], writes=["pec"])
                for kv in range(2):
                    for c in range(16):
                        mm(S, pM[0:64, kv:kv + 1], w1c[:, kv, c, :], pec[:, kv, c:c + 1], start=(c == 0), stop=(c == 15),
                           r=["w1c", "pec"], w=["pS2"])
                cp(S, "act", cvec[:], pM[0:64, 0:2], r=["pS2"], w=["cvec"])
                pre = cAu[:].rearrange("p s q c -> p (s q c)")[:, 0:4 * CW].rearrange("p (s c) -> p s c", s=4)
                mset(S, "pool", pre[:], 0.0, w=["pre", "cAu"])
                cv4 = cAs[:, :, 0:CW].rearrange("p (s h) c -> p s h c", h=2)
                tt(S, "dve", pre[:, :, 0:CW - 1], cv4[:, :, 0, 0:CW - 1], cv4[:, :, 1, 1:CW], ALU.add, r=["cAs", "pre", "cAu"], w=["pre", "cAu"])
                for sl in range(4):
                    ts(S, "dve", pre[:, sl, :], pre[:, sl, :], cvec[:, sl // 2:sl // 2 + 1], None, ALU.add, r=["pre", "cvec"], w=["pre"])
                gelu_ops(S, gT[:], pre[:], cAs[:, 0:4, 0:CW], cAs[:, 4:8, 0:CW], ("pre", "cAs", "cAs", "gT"))
                for c in range(NCHK):
                    rows = min(128, CW - 128 * c)
                    for g in range(2):
                        mm(S, pM[0:rows, 64:128], gT[:, g, 128 * c:128 * c + rows], w2[:, 0, :], r=["gT", "w2"], w=["pS2"])
                        act(S, sqc[0:rows, :], pM[0:rows, 64:128], AF.Square, r=["pS2"], w=["sqc", "ssc0"], accum_out=ssc[0:rows, 0:1])
                        rsq(S, ssc[0:rows, 2:3], ssc[0:rows, 1:2], ssc[0:rows, 0:1], 1.0 / 64, EPS, r=["ssc0"], wt=["ssc1"], w=["ssc2"])
                        stt(S, "dve", kcb[0:rows, g * 64:(g + 1) * 64], pM[0:rows, 64:128], ssc[0:rows, 2:3], kg0[0:rows, :], ALU.mult, ALU.mult,
                            r=["pS2", "ssc2", "kg0"], w=["kcb"])
                        if c < NCHF:
                            mm(S, pM[0:rows, 128:192], gT[:, 2 + g, 128 * c:128 * c + rows], w2[:, 1, :], r=["gT", "w2"], w=["pS2"])
                            cp(S, "act", CV[0:rows, g * NCHF + c, 0:64], pM[0:rows, 128:192], r=["pS2"], w=["CVv"])
                    tr(S, pTr[:, 0:rows], kcb[0:rows, :], ident[0:rows, 0:rows], r=["kcb", "ident"], w=["pTr"])
                    cp(S, "act", kcT[:, 128 * c:128 * c + rows], pTr[:, 0:rows], r=["pTr"], w=["kcT"])
            if True:
                sbp = lambda name, shape, dt: stm.enter_context(nc.sbuf_tensor("Bp_" + T.get("tag", "") + name, shape, dt))
                Rst = sbp("Rst", [64, 384], F32)
                pfx = sbp("pfx", [64, 10, 384], F32)
                if NQ >= 12:
                    cAu_f = cAu[:].rearrange("p s q c -> p (s q c)")
                    cAs_f = cAs[:].rearrange("p s c -> p (s c)")
                    Ut = [cAu_f[:, i * 1536:(i + 1) * 1536].rearrange("p (k e) -> p k e", k=4) for i in range(2)]
                    tW = cAs_f[:, 0:1536].rearrange("p (k e) -> p k e", k=4)
                    tQ = cAs_f[:, 1536:3072].rearrange("p (k e) -> p k e", k=4)
                else:
                    Ut = [sbp("Ut%d" % i, [64, 4, 384], F32)[:] for i in range(2)]
                    tW = sbp("tW", [64, 4, 384], F32)[:]
                    tQ = sbp("tQ", [64, 4, 384], F32)[:]
                Wt = sbp("Wt", [64, 384], F32)
                Qt = sbp("Qt", [64, 384], F32)
                tG = sbp("tG", [64, 384], F32)
                S.dma("sp", pfx[:], T["pfx"], writes=["pfx"])
                mset(S, "pool", Rst[:], 0.0, w=["Rst"])
                for jj in range(NQ):
                    u = jj % 2
                    UT = "Ut%d" % u
                    extra_w = ["cAu", "pre"] if jj < 2 else []
                    S.dma("sp", Ut[u], T["gU%d" % (jj // ((NQ + 1) // 2))].rearrange("(q j d) e -> d q j e", q=4, d=64)[:, :, jj % ((NQ + 1) // 2), :], reads=GATH, writes=[UT] + extra_w)
                    tt(S, "pool", tQ, Ut[u], pfx[:, 4:8, :], ALU.mult, r=[UT, "pfx"], w=["tQ"] + (["cAs"] if jj == 0 else []))
                    red(S, "dve", Qt[:], tQ.rearrange("p k e -> p e k"), r=["tQ"], w=["Qt"])
                    tt(S, "dve", tG[:], Rst[:], pfx[:, 8, :], ALU.mult, r=["Rst", "pfx"], w=["tG"])
                    tt(S, "pool", Gs[:, jj, :], tG[:], Qt[:], ALU.add, r=["tG", "Qt"], w=["Gs%d" % jj])
                    if jj == NQ - 1:
                        break
                    tt(S, "pool", tW, Ut[u], pfx[:, 0:4, :], ALU.mult, r=[UT, "pfx"], w=["tW"] + (["cAs"] if jj == 0 else []))
                    red(S, "dve", Wt[:], tW.rearrange("p k e -> p e k"), r=["tW"], w=["Wt"])
                    tt(S, "dve", Rst[:], Rst[:], pfx[:, 9, :], ALU.mult, r=["Rst", "pfx"], w=["Rst"])
                    tt(S, "dve", Rst[:], Rst[:], Wt[:], ALU.add, r=["Rst", "Wt"], w=["Rst"])
            S.fence()
            stm.__exit__(None, None, None)

            mixTs = [sb("mixT%d" % i, [128, 8, 128], BF16) for i in range(2)]
            wo = sb("wo", [128, 8, 1024], BF16)
            S.dma("pool", wo[:], T["w_out"].rearrange("(c p) f -> p c f", p=128), writes=["wo"])
            xres = [sb("xres%d" % i, [128, 1024], F32) for i in range(1)] * 2
            x1o = [sb("x1o%d" % i, [128, 1024], F32) for i in range(1)] * 2
            rqT = [sb("rqT%d" % p, [64, 768], BF16) for p in range(2)]
            inn = [sb("inn%d" % p, [128, 384], F32) for p in range(2)]
            sgt = [sb("sgt%d" % p, [128, 384], BF16) for p in range(2)]
            qnT = [sb("qnT%d" % p, [128, 768], BF16) for p in range(2)]
            for p_ in range(2):
                mset(S, "pool", qnT[p_][:], 0.0, w=["qnT%d" % p_])
            gat = [sb("gat%d" % p, [128, 18], F32) for p in range(2)]
            ro = sb("ro", [128, 384], F32)
            rsqv = sb("rsqv", [128, 384], F32)
            rss = sb("rss", [128, 24], F32)
            ron = sb("ron", [128, 384], F32)
            rob = sb("rob", [128, 384], BF16)
            eS = [sb("eS%d" % i, [128, 384], BF16) for i in range(4)]
            eC = [sb("eC%d" % i, [128, 384], BF16) for i in range(2)]
            eN = sb("eN", [16, 384], BF16)
            imp = sb("imp", [128, 128], F32)
            imw = sb("imw", [128, 128], F32)
            top = sb("top", [128, 16], F32)
            rcs = sb("rcs", [128, 12], F32)
            Mn = sb("Mn", [128, 128], BF16)
            MnT = sb("MnT", [128, 128], BF16)
            sc = sb("sc", [128, 9], F32)
            oa = sb("oa", [128, 192], F32)
            ob = sb("ob", [128, 192], F32)
            nsab = sb("nsab", [128, 384], BF16)
            MTs = [sb("MT%d" % i, [128, NB, 128], BF16) for i in range(2)]
            cSs = [sb("cS%d" % i, [128, 3, 193], F32) for i in range(2)]
            vnrs = [sb("vnr%d" % i, [16, 65], BF16) for i in range(2)]
            for i_ in range(2):
                mset(S, "pool", vnrs[i_][:, 64:65], 1.0, w=["vnr%d" % i_])
            cnt = {"ne": 0, "nce": 0, "nps": 0}
            deferred = []
            early = []

            def next_ps():
                k = cnt["nps"] % 3; cnt["nps"] += 1
                return pS[k], "pS%d" % k

            def loads(j):
                p = j % 2
                S.dma("sp", rqT[p][:], T["o_rqT"][j], writes=["rqT%d" % p])
                S.dma("sp", inn[p][:], T["o_inner"][j], writes=["inn%d" % p])
                S.dma("sp", sgt[p][:], T["o_sg"][j], writes=["sgt%d" % p])
                S.dma("sp", qnT[p][0:64, 0:384], T["o_qnT"][j][:, 0:384], writes=["qnT%d" % p])
                S.dma("sp", qnT[p][64:128, 384:768], T["o_qnT"][j][:, 384:768], writes=["qnT%d" % p])
                S.dma("sp", gat[p][:], T["o_gates"][j], writes=["gat%d" % p])

            def ret(j):
                p = j % 2
                mixT = mixTs[p]; MX = "mixT%d" % p
                for hh in range(6):
                    mm(S, pR[:, hh * 64:(hh + 1) * 64], rqT[p][:, hh * 128:(hh + 1) * 128], Gs[:, j, hh * 64:(hh + 1) * 64],
                       r=["rqT%d" % p, "Gs%d" % j], w=["pR"])
                tt(S, "dve", ro[:], pR[:, 0:384], inn[p][:], ALU.add, r=["pR", "inn%d" % p], w=["ro"])
                act(S, rsqv[:], ro[:], AF.Square, r=["ro"], w=["rsqv"])
                red(S, "dve", rss[:, 0:6], rsqv[:].rearrange("p (h d) -> p h d", h=6), r=["rsqv"], w=["rss0"])
                rsq(S, rss[:, 12:18], rss[:, 6:12], rss[:, 0:6], 1.0 / 64, EPS, r=["rss0"], wt=["rss1"], w=["rss2"])
                tt(S, "dve", ron[:].rearrange("p (h d) -> p h d", h=6), ro[:].rearrange("p (h d) -> p h d", h=6),
                   rss[:, 12:18].unsqueeze(2).to_broadcast([128, 6, 64]), ALU.mult, r=["ro", "rss2"], w=["ron"])
                tt(S, "pool", rob[:], ron[:], sgt[p][:], ALU.mult, r=["ron", "sgt%d" % p], w=["rob"])
                def ret2(mixT=mixT, MX=MX, j=j):
                    S.dma("sp", mixT[:, 6:8, :], T["o_gmT"][j].rearrange("p (c n) -> p c n", c=2), writes=[MX])
                    for c in range(3):
                        tr(S, pTr[:, c * 128:(c + 1) * 128], rob[:, c * 128:(c + 1) * 128], ident[:], r=["rob", "ident"], w=["pTr"])
                    cp(S, "act", mixT[:, 0:3, :], pTr[:, 0:384].rearrange("p (c n) -> p c n", c=3), r=["pTr"], w=[MX])
                early.append(ret2)

            def pre_round(j, g):
                nb = (2 * j + g) % 2
                p = j % 2
                imax = 4 * j + 3
                QK = "qnT%d" % p
                gp = slice(g * 64, (g + 1) * 64)
                qg = qnT[p][gp, g * 384:(g + 1) * 384]
                nf = 32 * j + 24
                nfc = (nf + 127) // 128
                var = 1 if j == 0 else 0
                vnr = vnrs[nb]; VN = "vnr%d" % nb
                cS = cSs[nb]; CS = "cS%d" % nb
                MT = MTs[nb]; MTK = "MT%d" % nb
                mm(S, pR[0:16, 448:512], gT[:, 2 + g, nf:nf + 16], w2[:, 1, :], r=["gT", "w2"], w=["pR"])
                cp(S, "act", vnr[:, 0:64], pR[0:16, 448:512], r=["pR"], w=[VN])
                pend = []
                def flush(keep=0):
                    while len(pend) > keep:
                        pend.pop(0)()
                def pv_far(ec, EC, rows, c):
                    for hh in range(3):
                        mm(S, pC[:, hh, 0:193], ec[0:rows, hh * 128:(hh + 1) * 128], CV[0:rows, g * NCHF + c, :], start=(c == 0 and hh != 1), stop=False,
                           r=[EC, "CVv", "CVo", "CV1"], w=["pC"], sgc=True)
                def pv_near():
                    for hh in range(3):
                        mm(S, pC[:, hh, 0:65], eN[:, hh * 128:(hh + 1) * 128], vnr[:, :], start=False, stop=True,
                           r=["eN", VN], w=["pC"], sgc=True)
                        mm(S, pC[:, hh, 65:193], eN[:, hh * 128:(hh + 1) * 128], OVB[:, 128 - 8 * j:256 - 8 * j], start=False, stop=True,
                           r=["eN", "OVB"], w=["pC"], sgc=True)
                for c in range(nfc):
                    rows = min(128, nf - 128 * c)
                    pp, PP = next_ps()
                    mm(S, pp[0:rows, 0:384], kcT[gp, 128 * c:128 * c + rows], qg, start=True, stop=True, r=["kcT", QK], w=[PP])
                    k_ = cnt["nce"] % 2; cnt["nce"] += 1
                    ec = eC[k_]; EC = "eC%d" % k_
                    if c == 0:
                        act(S, ec[0:rows, :], pp[0:rows, 0:384], AF.Exp, r=[PP, "zf0"], w=[EC], bias=zf0[0:rows, 0:1])
                    else:
                        act(S, ec[0:rows, :], pp[0:rows, 0:384], AF.Exp, r=[PP], w=[EC])
                    flush()
                    pend.append(lambda ec=ec, EC=EC, rows=rows, c=c: pv_far(ec, EC, rows, c))
                pp, PP = next_ps()
                mm(S, pp[0:16, 0:384], kcT[gp, nf:nf + 16], qg, start=True, stop=True, r=["kcT", QK], w=[PP])
                act(S, eN[:], pp[0:16, 0:384], AF.Exp, r=[PP], w=["eN"])
                tt(S, "dve", eN[:], eN[:], Bc[:, 2 * var + g, :], ALU.mult, r=["eN", "Bc"], w=["eN"])
                flush()
                pv_near()
                cp(S, "act", cS[:], pC[:, 0:3, 0:193], r=["pC"], w=[CS])
                if j >= 2:
                    ts(S, "dve", rcs[:, 0:3], cS[:, :, 64], 1e-30, None, ALU.max, r=[CS], w=["rcs"])
                    S.op("dve", lambda e: e.reciprocal(rcs[:, 0:3], rcs[:, 0:3]), reads=["rcs"], writes=["rcs"])
                    ts(S, "dve", imp[:], cS[:, 0, 65:193], rcs[:, 0:1], None, ALU.mult, r=[CS, "rcs"], w=["imp"])
                    stt(S, "dve", imp[:], cS[:, 1, 65:193], rcs[:, 1:2], imp[:], ALU.mult, ALU.add, r=[CS, "rcs", "imp"], w=["imp"])
                    stt(S, "dve", imp[:], cS[:, 2, 65:193], rcs[:, 2:3], imp[:], ALU.mult, ALU.add, r=[CS, "rcs", "imp"], w=["imp"])
                    tt(S, "dve", imp[:], imp[:], Dt[:, 128 - 8 * j:256 - 8 * j], ALU.add, r=["imp", "Dt"], w=["imp"])
                    mset(S, "dve", imp[:, 0:1], 3e9, w=["imp"])
                    S.op("dve", lambda e: e.max(out=top[:, 0:8], in_=imp[:]), reads=["imp"], writes=["top"])
                    S.op("dve", lambda e: e.match_replace(out=imw[:], in_to_replace=top[:, 0:8], in_values=imp[:], imm_value=-3e9),
                         reads=["imp", "top"], writes=["imw"])
                    S.op("dve", lambda e: e.max(out=top[:, 8:16], in_=imw[:]), reads=["imw"], writes=["top"])
                    ts(S, "dve", Mn[:], imp[:], top[:, 15:16], None, ALU.is_ge, r=["imp", "top"], w=["Mn"])
                    def fin():
                        tr(S, pTr[:, 512:640], Mn[:], ident[:], r=["Mn", "ident"], w=["pTr"])
                        cp(S, "dve", MnT[:], pTr[:, 512:640], r=["pTr"], w=["MnT"])
                        msl = T["mscr"][(2 * j + g) % 4]
                        MS = "mscr%d" % ((2 * j + g) % 4)
                        S.dma("sp", msl, MnT[:], reads=["MnT"], writes=[MS])
                        nkb = imax + 1
                        for hf in range(2):
                            srcm = msl.rearrange("(k two) q -> two k q", two=2)[hf, 0:nkb, :].unsqueeze(0).to_broadcast([64, nkb, 128])
                            S.dma("sp", MT[hf * 64:(hf + 1) * 64, 0:nkb, :], srcm, reads=[MS], writes=[MTK])
                    deferred.append(fin)

            def main_round(j, g):
                nb = (2 * j + g) % 2
                p = j % 2
                imax = 4 * j + 3
                QK = "qnT%d" % p
                gp = slice(g * 64, (g + 1) * 64)
                qg = qnT[p][gp, g * 384:(g + 1) * 384]
                use_sel = j >= 2
                cS = cSs[nb]; CS = "cS%d" % nb
                MT = MTs[nb]; MTK = "MT%d" % nb
                pend = []
                def flush(keep=0):
                    while len(pend) > keep:
                        pend.pop(0)()
                def pv_sw(es, ES, br, kb, first, last):
                    for hh in range(3):
                        mm(S, pO[:, br, hh, :], es[:, hh * 128:(hh + 1) * 128], vs[:, (br * NB + kb) * 2 + g, :],
                           start=(first and hh == 0), stop=last, r=[ES, "vs", "vs1"], w=["pO"], sgc=True)
                steps = [(1, kb) for kb in range(max(0, 4 * j - 4), imax + 1)] + [(0, kb) for kb in range(0, imax + 1)]
                for si, (br, kb) in enumerate(steps):
                    pp, PP = next_ps()
                    mm(S, pp[:, 0:384], ksT[:, br, kb * 128:(kb + 1) * 128], qnT[p][:, g * 384:(g + 1) * 384], start=True, stop=True, r=["ksT", QK], w=[PP])
                    k_ = cnt["ne"] % 4; cnt["ne"] += 1
                    es = eS[k_]; ES = "eS%d" % k_
                    act(S, es[:], pp[:, 0:384], AF.Exp, r=[PP], w=[ES])
                    if br == 0 and use_sel:
                        tt(S, "dve", es[:].rearrange("p (h n) -> p h n", h=3), es[:].rearrange("p (h n) -> p h n", h=3),
                           MT[:, kb, :].unsqueeze(1).to_broadcast([128, 3, 128]), ALU.mult, r=[ES, MTK], w=[ES])
                    if br == 0 and kb >= 4 * j - 1:
                        tt(S, "dve", es[:], es[:], Z[:, 2 * (kb - 4 * j + 1) + g, :], ALU.mult, r=[ES, "Z"], w=[ES])
                    if br == 1:
                        tt(S, "dve", es[:], es[:], Z[:, 2 * (ZS + kb - 4 * j + 4) + g, :], ALU.mult, r=[ES, "Z"], w=[ES])
                    flush(1)
                    pend.append(lambda es=es, ES=ES, br=br, kb=kb, first=(si == 0), last=(kb == imax): pv_sw(es, ES, br, kb, first, last))
                    if si == 1:
                        while early:
                            early.pop(0)()
                    if si == 6:
                        while deferred:
                            deferred.pop(0)()
                flush()
                while early:
                    early.pop(0)()
                while deferred:
                    deferred.pop(0)()
                ts(S, "dve", rcs[:, 3:6], cS[:, :, 64], 1e-30, None, ALU.max, r=[CS], w=["rcs2"])
                S.op("dve", lambda e: e.reciprocal(rcs[:, 3:6], rcs[:, 3:6]), reads=["rcs2"], writes=["rcs2"])
                S.op("dve", lambda e: e.reciprocal(rcs[:, 6:12].rearrange("p (b h) -> p b h", b=2), pO[:, :, :, 64]), reads=["pO"], writes=["rcs2"])
                gv = gat[p][:, g * 9:(g + 1) * 9].rearrange("p (h b) -> p b h", b=3)
                tt(S, "dve", sc[:].rearrange("p (b h) -> p b h", b=3), rcs[:, 3:12].rearrange("p (b h) -> p b h", b=3), gv, ALU.mult,
                   r=["rcs2", "gat%d" % p], w=["sc"])
                o3 = lambda a: a[:].rearrange("p (h d) -> p h d", h=3)
                tt(S, "dve", o3(oa), cS[:, :, 0:64], sc[:, 0:3].unsqueeze(2).to_broadcast([128, 3, 64]), ALU.mult, r=[CS, "sc"], w=["oa"])
                tt(S, "dve", o3(ob), pO[:, 0, :, 0:64], sc[:, 3:6].unsqueeze(2).to_broadcast([128, 3, 64]), ALU.mult, r=["pO", "sc"], w=["ob"])
                tt(S, "pool", oa[:], oa[:], ob[:], ALU.add, r=["oa", "ob"], w=["oa"])
                tt(S, "dve", o3(ob), pO[:, 1, :, 0:64], sc[:, 6:9].unsqueeze(2).to_broadcast([128, 3, 64]), ALU.mult, r=["pO", "sc"], w=["ob"])
                tt(S, "pool", nsab[:, g * 192:(g + 1) * 192], oa[:], ob[:], ALU.add, r=["oa", "ob"], w=["nsab"])

            def finish(j):
                p = j % 2
                mixT = mixTs[p]; MX = "mixT%d" % p
                S.dma("sp", xres[p][:], T["xin"][j], writes=["xres0"])
                for c in range(3):
                    tr(S, pTr[:, c * 128:(c + 1) * 128], nsab[:, c * 128:(c + 1) * 128], ident[:], r=["nsab", "ident"], w=["pTr"])
                cp(S, "act", mixT[:, 3:6, :], pTr[:, 0:384].rearrange("p (c n) -> p c n", c=3), r=["pTr"], w=[MX])
                if "dbg_mix" in T:
                    S.dma("sp", T["dbg_mix"][j], mixT[:], reads=[MX])
                for half in range(2):
                    for c in range(8):
                        mm(S, pR[:, 0:512], mixT[:, c, :], wo[:, c, half * 512:(half + 1) * 512], start=(c == 0), stop=(c == 7),
                           r=[MX, "wo"], w=["pR"])
                    tt(S, "dve", x1o[p][:, half * 512:(half + 1) * 512], pR[:, 0:512], xres[p][:, half * 512:(half + 1) * 512], ALU.add,
                       r=["pR", "xres0"], w=["x1o0"])
                S.dma("sp", T["xout"][j], x1o[p][:], reads=["x1o0"], writes=["xout%d" % j])

            loads(0)
            pre_round(0, 0)
            while deferred:
                deferred.pop(0)()
            for j in range(NQ):
                if j + 1 < NQ:
                    loads(j + 1)
                ret(j)
                pre_round(j, 1)
                main_round(j, 0)
                if j + 1 < NQ:
                    pre_round(j + 1, 0)
                main_round(j, 1)
                early.append(lambda j=j: finish(j))
            while early:
                early.pop(0)()
            S.fence()
            if STOP == "b1":
                return
        emit_B2(nc, S, T, NQ, ident)
        S.fence()


def emit_B2(nc, S, T, NQ, ident):
    NT = (NQ + 3) // 4
    with ExitStack() as st:
        sb = lambda name, shape, dt: st.enter_context(nc.sbuf_tensor("B2_" + T.get("tag", "") + name, shape, dt))
        ps = lambda name, shape, dt: st.enter_context(nc.psum_tensor("B2_" + T.get("tag", "") + name, shape, dt))
        wgu = sb("wgu", [128, 8, 2 * DFF], BF16)
        wd = sb("wd", [128, 22, 1024], BF16)
        gain2 = sb("gain2", [128, 1024], F32)
        x1 = [sb("x1_%d" % i, [128, 1024], F32) for i in range(4)]
        junk = sb("junk", [128, 1024], BF16)
        ss = sb("ss", [128, 4], F32)
        h2 = sb("h2", [128, 1024], BF16)
        h2T = sb("h2T", [128, 8, 512], BF16)
        sgt = [sb("sgt%d" % i, [128, 512], BF16) for i in range(2)]
        aT = sb("aT", [128, 22, 512], BF16)
        xo = [sb("xo%d" % i, [128, 1024], F32) for i in range(2)]
        pX = ps("pX", [128, 1024], F32)
        pT2 = ps("pT2", [128, 1024], BF16)
        pG = [ps("pG%d" % i, [128, 512], F32) for i in range(2)]
        pUp = [ps("pUp%d" % i, [128, 512], F32) for i in range(2)]
        wgv = T["w_gu"].rearrange("(c p) f -> p c f", p=128)
        wdv = T["w_dn"].rearrange("(c p) f -> p c f", p=128)
        for c in range(8):
            S.dma("pool", wgu[:, c, :], wgv[:, c, :], writes=["wgu"])
        for c in range(0, 22, 2):
            S.dma("pool", wd[:, c:c + 2, :], wdv[:, c:c + 2, :], writes=["wd"])
        S.dma("sp", gain2[:], T["fnorm"], writes=["gain2"])
        nx = 0
        for t in range(NT):
            nb = min(4, NQ - 4 * t)
            W = nb * 128
            for jj in range(nb):
                j = 4 * t + jj
                X1 = "x1_%d" % jj
                S.dma("sp", x1[jj][:], T["xout"][j], reads=["xout%d" % j], writes=[X1])
                act(S, junk[:], x1[jj][:], AF.Square, r=[X1], w=["junk2", "fs0"], accum_out=ss[:, 0:1])
                rsq(S, ss[:, 2:3], ss[:, 1:2], ss[:, 0:1], 1.0 / 1024, EPS, r=["fs0"], wt=["fs1"], w=["fs2"])
                stt(S, "dve", h2[:], x1[jj][:], ss[:, 2:3], gain2[:], ALU.mult, ALU.mult, r=[X1, "fs2", "gain2"], w=["h2"])
                for c in range(8):
                    tr(S, pT2[:, c * 128:(c + 1) * 128], h2[:, c * 128:(c + 1) * 128], ident[:], r=["h2", "ident"], w=["pT2"])
                cp(S, "act", h2T[:, :, jj * 128:(jj + 1) * 128], pT2[:].rearrange("p (c n) -> p c n", c=8), r=["pT2"], w=["h2T"])
            for fc in range(22):
                q = fc % 2
                for c in range(8):
                    mm(S, pG[q][:, 0:W], wgu[:, c, fc * 128:(fc + 1) * 128], h2T[:, c, 0:W], start=(c == 0), stop=(c == 7),
                       r=["wgu", "h2T"], w=["pG%d" % q])
                for c in range(8):
                    mm(S, pUp[q][:, 0:W], wgu[:, c, DFF + fc * 128:DFF + (fc + 1) * 128], h2T[:, c, 0:W], start=(c == 0), stop=(c == 7),
                       r=["wgu", "h2T"], w=["pUp%d" % q])
                act(S, sgt[q][:, 0:W], pG[q][:, 0:W], AF.Silu, r=["pG%d" % q], w=["sgt%d" % q])
                tt(S, "dve", aT[:, fc, 0:W], pUp[q][:, 0:W], sgt[q][:, 0:W], ALU.mult, r=["pUp%d" % q, "sgt%d" % q], w=["aT"])
            for jj in range(nb):
                j = 4 * t + jj
                for half in range(2):
                    for fc in range(22):
                        mm(S, pX[:, half * 512:(half + 1) * 512], aT[:, fc, jj * 128:(jj + 1) * 128], wd[:, fc, half * 512:(half + 1) * 512],
                           start=(fc == 0), stop=(fc == 21), r=["aT", "wd"], w=["pX"])
                o = xo[jj % 2]; OK = "xo%d" % (jj % 2)
                tt(S, "dve", o[:], pX[:], x1[jj][:], ALU.add, r=["pX", "x1_%d" % jj], w=[OK])
                S.dma("sp", T["xout"][j], o[:], reads=[OK], writes=["xout%d" % j])


def host_gather_B(Ao, r, NQ):
    NB = 4 * NQ
    CW = 8 * NB + 40
    P = 32 - 8 * r
    ks = np.stack([np.asarray(Ao[rr]["o_ks"]) for rr in range(4)])
    ks2 = ks.transpose(2, 3, 1, 4, 0, 5).reshape(128, 2, NB * 128)
    v = np.stack([np.asarray(Ao[rr]["o_v"]) for rr in range(4)])
    vs2 = np.ones((128, 2 * NB * 2, 65), v.dtype)
    vs2[:, :, 0:64] = v.reshape(4, 2, NQ, 128, 2, 64).transpose(3, 1, 2, 0, 4, 5).reshape(128, 2 * NB * 2, 64)
    U = np.stack([np.asarray(Ao[rr]["o_U"]) for rr in range(4)])
    U2 = U.transpose(1, 0, 2, 3).reshape(NB, 64, 384)
    cA = np.stack([np.asarray(Ao[rr]["o_cA"]) for rr in range(4)])
    cAg = cA.reshape(4, 4, 2, 64, NQ, 8).transpose(3, 1, 2, 4, 0, 5).reshape(64, 8, NB * 8)
    cAsh = np.zeros((64, 8, CW), np.float32)
    cAsh[:, :, P:P + NB * 8] = cAg
    return {"ks2": np.ascontiguousarray(ks2), "vs2": np.ascontiguousarray(vs2), "U2": np.ascontiguousarray(U2), "cAsh": cAsh}


NQ_FULL = 16
RG = [[0, 1, 2, 3], [4, 5, 6, 7]]


def stack_layers(inp, fn):
    L = inp["w_in"].shape[0]
    per = [fn(inp, l) for l in range(L)]
    return {k: np.ascontiguousarray(np.stack([p[k] for p in per])) for k in per[0]}


def build_fused(NQ, L=2):
    NB = 4 * NQ
    NCHF = (32 * (NQ - 1) + 24 + 127) // 128
    nc = bass.Bass("TRN2", target_bir_lowering=False)
    X = {}
    def di(name, shape, dt=F32):
        X[name] = nc.dram_tensor(name, list(shape), dt, kind="ExternalInput").ap()
    def dn(name, shape, dt=F32):
        t = nc.dram_tensor(name, list(shape), dt)
        X[name] = t.ap()
        return t
    di("xin", [NQ, 128, 1024])
    di("w_in", [L, 1024, IN_W]); di("anorm", [L, 128, 1024]); di("gq", [L, 128, 384]); di("kg", [L, 128, 256])
    di("wsT", [L, 128, 4, 128]); di("gbT", [L, 128, 4]); di("w1", [L, 2, 2048, 64])
    di("w_out", [L, 1024, 1024]); di("fnorm", [L, 128, 1024]); di("w_gu", [L, 1024, 2 * DFF]); di("w_dn", [L, DFF, 1024])
    di("pe", [L, 128, 2, 16]); di("w2", [L, 64, 2, 64]); di("kg0", [L, 128, 64])
    di("ropeq", [NQ, 128, 384]); di("ropek", [NQ, 128, 384]); di("cmask", [128, 128]); di("idn", [128, 128])
    di("OVFr", [128, NCHF, 128]); di("OVBr", [16, 264]); di("Dtr", [128, 264])
    di("rdec", [64, 3, 384]); di("selr", [64, 4]); di("pfx", [64, 10, 384]); di("zf0", [128, 1])
    di("zmask", [ZS + ZW, 128, 384]); di("zneg", [ZS + ZW, 128, 384]); di("ztab", [ZS + ZW, 2, 128, 384]); di("zt31", [2, 128, 384])
    di("ctab", [2, 16, 384]); di("cmaskc", [2, 16, 384]); di("cneg", [2, 16, 384])
    X["xout"] = nc.dram_tensor("xout", [NQ, 128, 1024], F32, kind="ExternalOutput").ap()
    dn("xbuf", [NQ, 128, 1024])
    dn("mscr", [4, 128, 128], BF16)
    bt = {}
    JH = (NQ + 1) // 2
    pairs = []
    for b_ in range(2):
        bt["bks%d" % b_] = dn("bks%d" % b_, [128, NQ * 128], BF16); bt["gks%d" % b_] = dn("gks%d" % b_, [4 * 128, NQ * 128], BF16)
        bt["bv%d" % b_] = dn("bv%d" % b_, [NQ * 128, 130], BF16); bt["gv%d" % b_] = dn("gv%d" % b_, [4 * NQ * 128, 130], BF16)
        bt["bU%d" % b_] = dn("bU%d" % b_, [JH * 64, 384]); bt["gU%d" % b_] = dn("gU%d" % b_, [4 * JH * 64, 384])
        pairs += [("bks%d" % b_, "gks%d" % b_), ("bv%d" % b_, "gv%d" % b_), ("bU%d" % b_, "gU%d" % b_)]
    bt["bcA"] = dn("bcA", [512, NQ * 8]); bt["gcA"] = dn("gcA", [4 * 512, NQ * 8])
    pairs.append(("bcA", "gcA"))
    dn("o_rqT", [NQ, 64, 768], BF16); dn("o_inner", [NQ, 128, 384]); dn("o_sg", [NQ, 128, 384], BF16)
    dn("o_qnT", [NQ, 64, 768], BF16); dn("o_gates", [NQ, 128, 18]); dn("o_gmT", [NQ, 128, 256], BF16)
    with ExitStack() as st:
        S = Sched(nc, st)
        for l in range(L):
            TA = dict(X)
            for k in ("w_in", "anorm", "gq", "kg", "wsT", "gbT", "w1"):
                TA[k] = X[k][l]
            TA["xin"] = X["xin"] if l == 0 else X["xbuf"]
            TA["tag"] = "L%d_" % l
            emit_A(nc, S, TA, NQ)
            S.fence()
            for bn, gn in pairs:
                def mk_fn(bn=bn, gn=gn):
                    return lambda e: e.collective_compute("AllGather", ALU.bypass, replica_groups=RG,
                                                          ins=[bt[bn].ap().opt()], outs=[bt[gn].ap().opt()])
                S.coll(mk_fn(), reads=[bn], writes=[gn])
            TB = dict(X)
            for k in ("w_out", "fnorm", "w_gu", "w_dn", "pe", "w2", "kg0"):
                TB[k] = X[k][l]
            TB["w1f"] = X["w1"][l]
            TB["cmask"] = X["cmaskc"]
            TB["tag"] = "L%d_" % l
            TB["xin"] = X["xin"] if l == 0 else X["xbuf"]
            TB["xout"] = X["xbuf"] if l < L - 1 else X["xout"]
            emit_B(nc, S, TB, NQ)
            S.fence()
        S.wait_all("sp")
        S.emit()
    return nc


def make_in_maps(inp, NQ):
    x = inp["x"].astype(np.float32)
    B_, T_, D_ = x.shape
    cores = [(b, r) for b in range(2) for r in range(4)]
    LA = stack_layers(inp, layer_inputs_A)
    LB = stack_layers(inp, layer_inputs_B)
    maps = []
    for (b, r) in cores:
        m = {}
        m["xin"] = np.ascontiguousarray(x[b].reshape(T_ // 128, 128, D_)[r::4][:NQ])
        cA = host_consts_A(b, r, NQ)
        cB = host_consts_Bu(NQ, r)
        btab = bias_tabs_u(inp["rel_bias"].astype(np.float32), r)
        for k in ("w_in", "anorm", "gq", "kg", "wsT", "gbT", "w1"):
            m[k] = LA[k]
        for k in ("w_out", "fnorm", "w_gu", "w_dn", "pe", "w2", "kg0"):
            m[k] = LB[k]
        for k in ("ropeq", "ropek", "cmask", "idn"):
            m[k] = cA[k]
        for k in ("OVFr", "OVBr", "Dtr", "rdec", "selr", "pfx", "zf0", "zmask", "zneg", "cneg"):
            m[k] = cB[k]
        m["cmaskc"] = cB["cmask"]
        for k in ("ztab", "zt31", "ctab"):
            m[k] = btab[k]
        maps.append({k: np.ascontiguousarray(v, dtype=np.float32) for k, v in m.items()})
    return cores, maps


def kernel(**inp):
    inp = {k: np.asarray(v) for k, v in inp.items()}
    NQ = NQ_FULL
    B_, T_, D_ = inp["x"].shape
    nc = build_fused(NQ, L=inp["w_in"].shape[0])
    cores, maps = make_in_maps(inp, NQ)
    res = run_bass_kernel_spmd(nc, maps, core_ids=list(range(8))).results
    out = np.zeros((B_, T_ // 128, 128, D_), np.float32)
    for ci, (b, r) in enumerate(cores):
        out[b, r::4] = np.asarray(res[ci]["xout"], dtype=np.float32)
    return out.reshape(B_, T_, D_)
```

```python
import math
import ml_dtypes
from contextlib import ExitStack
import numpy as np
import concourse.bass as bass
import concourse.mybir as mybir
from concourse.bass_utils import run_bass_kernel_spmd

F32 = mybir.dt.float32
BF16 = mybir.dt.bfloat16
AF = mybir.ActivationFunctionType
ALU = mybir.AluOpType
AX = mybir.AxisListType

N_DMA_SEMS = 24


class Sched:
    ENGS = ("pe", "act", "dve", "pool", "sp")

    def __init__(self, nc, stack):
        self.nc = nc
        self.stack = stack
        self.sem = {e: stack.enter_context(nc.semaphore("s_" + e)) for e in self.ENGS}
        self.dsem = [stack.enter_context(nc.semaphore("d%d" % i)) for i in range(N_DMA_SEMS)]
        self.dcnt = [0] * N_DMA_SEMS
        self.dnext = 0
        self.cnt = {e: 0 for e in self.ENGS}
        self.clock = {e: {} for e in self.ENGS}
        self.streams = {e: [] for e in self.ENGS}
        self.last_w = {}
        self.reads = {}
        self.n_waits = 0

    def _need(self, eng, ev, waits):
        key, val, snap = ev
        ck = self.clock[eng]
        if ck.get(key, 0) >= val:
            return
        if key == eng and eng == "pe":
            return
        waits[key] = max(waits.get(key, 0), val)

    def _apply(self, eng, evs, waits):
        ck = self.clock[eng]
        for key, val, snap in evs:
            if snap:
                for k2, v2 in snap.items():
                    if ck.get(k2, 0) < v2:
                        ck[k2] = v2
        for key, val in waits.items():
            if ck.get(key, 0) < val:
                ck[key] = val

    def _deps(self, eng, reads, writes):
        waits = {}
        evs = []
        for k in reads:
            ev = self.last_w.get(k)
            if ev is not None:
                self._need(eng, ev, waits)
                evs.append(ev)
        for k in writes:
            ev = self.last_w.get(k)
            if ev is not None:
                self._need(eng, ev, waits)
                evs.append(ev)
            for ev in self.reads.get(k, {}).values():
                self._need(eng, ev, waits)
                evs.append(ev)
        self._apply(eng, [e for e in evs if e[0] in waits and waits[e[0]] >= e[1]], waits)
        return waits

    def _commit(self, ev, reads, writes):
        for k in reads:
            self.reads.setdefault(k, {})[ev[0]] = ev
        for k in writes:
            self.last_w[k] = ev
            self.reads[k] = {}

    def op(self, eng, fn, reads=(), writes=()):
        waits = self._deps(eng, reads, writes)
        self.cnt[eng] += 1
        val = self.cnt[eng]
        snap = dict(self.clock[eng])
        ev = (eng, val, snap)
        self.streams[eng].append((waits, fn, ("c", eng, 1)))
        self.n_waits += len(waits)
        self._commit(ev, reads, writes)

    def dma(self, q, out, in_, reads=(), writes=(), **kw):
        if not hasattr(self, "dnq"):
            self.dnq = {"sp": 0, "pool": 0, "act": 0}
        if q == "pool":
            lo, n = 0, 8
        else:
            lo, n = 8, N_DMA_SEMS - 8
        i = lo + self.dnq[q] % n
        self.dnq[q] += 1
        waits = self._deps(q, reads, writes)
        key = ("d", i)
        prev = self.dcnt[i]
        if prev > 0 and self.clock[q].get(key, 0) < prev:
            waits[key] = max(waits.get(key, 0), prev)
            self.clock[q][key] = prev
        self.dcnt[i] += 16
        ev = (key, self.dcnt[i], dict(self.clock[q]))
        fn = lambda e, out=out, in_=in_, kw=kw: e.dma_start(out=out, in_=in_, **kw)
        self.streams[q].append((waits, fn, ("d", i, 16)))
        self._commit(ev, reads, writes)

    def dma_custom(self, q, fn, reads=(), writes=()):
        if not hasattr(self, "dnq"):
            self.dnq = {"sp": 0, "pool": 0, "act": 0}
        if q == "pool":
            lo, n = 0, 8
        else:
            lo, n = 8, N_DMA_SEMS - 8
        i = lo + self.dnq[q] % n
        self.dnq[q] += 1
        waits = self._deps(q, reads, writes)
        key = ("d", i)
        prev = self.dcnt[i]
        if prev > 0 and self.clock[q].get(key, 0) < prev:
            waits[key] = max(waits.get(key, 0), prev)
            self.clock[q][key] = prev
        self.dcnt[i] += 16
        ev = (key, self.dcnt[i], dict(self.clock[q]))
        self.streams[q].append((waits, fn, ("d", i, 16)))
        self._commit(ev, reads, writes)

    def coll(self, fn, reads=(), writes=()):
        if not hasattr(self, "csem"):
            self.csem = self.stack.enter_context(self.nc.semaphore("s_cc"))
            self.ccnt = 0
        waits = self._deps("pool", reads, writes)
        self.ccnt += 1
        ev = ("cc", self.ccnt, dict(self.clock["pool"]))
        self.streams["pool"].append((waits, fn, ("cc", None, 1)))
        self._commit(ev, reads, writes)

    def fence(self):
        tgt = {}
        if getattr(self, "ccnt", 0) > 0:
            tgt["cc"] = self.ccnt
        for e in self.ENGS:
            if self.cnt[e] > 0:
                tgt[e] = self.cnt[e]
        for i in range(N_DMA_SEMS):
            if self.dcnt[i] > 0:
                tgt[("d", i)] = self.dcnt[i]
        for e in self.ENGS:
            waits = {}
            for k, v in tgt.items():
                if k == e:
                    continue
                if self.clock[e].get(k, 0) < v:
                    waits[k] = v
                    self.clock[e][k] = v
            if waits:
                self.streams[e].append((waits, None, None))
        for e in self.ENGS:
            for k, v in tgt.items():
                if k != e and self.clock[e].get(k, 0) < v:
                    self.clock[e][k] = v

    def wait_all(self, eng):
        waits = {}
        for e in self.ENGS:
            if e != eng and self.cnt[e] > self.clock[eng].get(e, 0):
                waits[e] = self.cnt[e]
        for i in range(N_DMA_SEMS):
            if self.dcnt[i] > self.clock[eng].get(("d", i), 0):
                waits[("d", i)] = self.dcnt[i]
        if getattr(self, "ccnt", 0) > self.clock[eng].get("cc", 0):
            waits["cc"] = self.ccnt
        self.streams[eng].append((waits, None, None))

    def _semof(self, key):
        if isinstance(key, tuple):
            return self.dsem[key[1]]
        if key == "cc":
            return self.csem
        return self.sem[key]

    def emit(self):
        nc = self.nc
        with nc.Block() as block:
            def replay(name):
                def run(e):
                    for waits, fn, inc in self.streams[name]:
                        for key, val in waits.items():
                            e.wait_ge(self._semof(key), val)
                        if fn is None:
                            continue
                        ins = fn(e)
                        if inc[0] == "c":
                            ins.then_inc(self.sem[inc[1]], 1)
                        elif inc[0] == "cc":
                            ins.then_inc(self.csem)
                        else:
                            ins.then_inc(self.dsem[inc[1]], 16)
                return run
            block.tensor(replay("pe"))
            block.scalar(replay("act"))
            block.vector(replay("dve"))
            block.gpsimd(replay("pool"))
            block.sync(replay("sp"))


NPBF = ml_dtypes.bfloat16
EPS = 1e-6
D = 1024
IN_W = 3218
DFF = 2816
SLABS = [(0, 384), (384, 768), (768, 1152), (1152, 1536), (1536, 1920), (1920, 2432), (2432, 2706), (2706, 3218)]


def mm(S, out, lhsT, rhs, start=True, stop=True, r=(), w=(), sgc=False):
    if sgc:
        S.op("pe", lambda e: e.matmul(out, lhsT, rhs, start=start, stop=stop, skip_group_check=True), reads=r, writes=w)
    else:
        S.op("pe", lambda e: e.matmul(out, lhsT, rhs, start=start, stop=stop), reads=r, writes=w)

def tr(S, out, in_, ident, r=(), w=()):
    S.op("pe", lambda e: e.transpose(out, in_, ident), reads=r, writes=w)

def act(S, out, in_, func, r=(), w=(), **kw):
    S.op("act", lambda e: e.activation(out, in_, func, **kw), reads=r, writes=w)

def cp(S, eng, out, in_, r=(), w=()):
    if eng == "act":
        S.op("act", lambda e: e.copy(out, in_), reads=r, writes=w)
    else:
        S.op(eng, lambda e: e.tensor_copy(out, in_), reads=r, writes=w)

def tt(S, eng, out, a, b, op, r=(), w=()):
    S.op(eng, lambda e: e.tensor_tensor(out, a, b, op), reads=r, writes=w)

def ts(S, eng, out, a, s1, s2, op0, op1=None, r=(), w=()):
    if op1 is None:
        S.op(eng, lambda e: e.tensor_scalar(out, a, s1, s2, op0), reads=r, writes=w)
    else:
        S.op(eng, lambda e: e.tensor_scalar(out, a, s1, s2, op0, op1), reads=r, writes=w)

def stt(S, eng, out, in0, scalar, in1, op0, op1, r=(), w=()):
    S.op(eng, lambda e: e.scalar_tensor_tensor(out, in0, scalar, in1, op0, op1), reads=r, writes=w)

def red(S, eng, out, in_, r=(), w=()):
    S.op(eng, lambda e: e.tensor_reduce(out, in_, AX.X, ALU.add), reads=r, writes=w)

def rsq(S, out, tmp, in_, scale, bias, r=(), w=(), wt=()):
    S.op("act", lambda e: e.activation(tmp, in_, AF.Sqrt, bias=bias, scale=scale), reads=r, writes=wt)
    S.op("dve", lambda e: e.reciprocal(out, tmp), reads=wt, writes=w)

def mset(S, eng, ap, val, w=()):
    S.op(eng, lambda e: e.memset(ap, val), writes=w)


def host_consts_A(b, r, NQ):
    c = {}
    inv = (10000.0 ** (-np.arange(32, dtype=np.float32) / 32)).astype(np.float32)
    gam = 1.0 - 2.0 ** (-5.0 - np.arange(6, dtype=np.float64))
    n = np.arange(128, dtype=np.float64)
    gq = gam[None, :] ** n[:, None]
    gk = gam[None, :] ** (-n[:, None]) / 8.0
    rq = np.zeros((NQ, 128, 2, 6, 32), np.float32)
    rk = np.zeros((NQ, 128, 2, 6, 32), np.float32)
    for j in range(NQ):
        i = 4 * j + r
        t = (128 * i + np.arange(128)).astype(np.float32)
        ang = t[:, None] * inv[None, :]
        cs = np.stack([np.cos(ang), np.sin(ang)], 1).astype(np.float64)
        rq[j] = (cs[:, :, None, :] * gq[:, None, :, None]).astype(np.float32)
        rk[j] = (cs[:, :, None, :] * gk[:, None, :, None]).astype(np.float32)
    c["ropeq"] = rq.reshape(NQ, 128, 384)
    c["ropek"] = rk.reshape(NQ, 128, 384)
    m = np.arange(128)
    c["cmask"] = (m[:, None] <= m[None, :]).astype(np.float32)
    c["idn"] = np.eye(128, dtype=np.float32)
    return c


def layer_inputs_A(inp, l):
    d = {}
    d["w_in"] = np.ascontiguousarray(inp["w_in"][l])
    d["anorm"] = np.ascontiguousarray(np.broadcast_to(inp["attn_norm"][l][None, :], (128, 1024)))
    d["gq"] = np.ascontiguousarray(np.broadcast_to(np.tile(inp["nsa_q_gain"][l], 6)[None, :], (128, 384)))
    kg = inp["nsa_k_gain"][l]
    d["kg"] = np.ascontiguousarray(np.broadcast_to(np.concatenate([kg[1], kg[1], kg[2], kg[2]])[None, :], (128, 256)))
    d["wsT"] = np.ascontiguousarray(inp["gm_ws"][l].transpose(2, 0, 1))
    d["gbT"] = np.ascontiguousarray(inp["gm_b"][l].T)
    d["w1"] = np.ascontiguousarray(inp["cmp_w1"][l])
    return d


def declare_A(nc, NQ, xin_kind="ExternalInput", out_kind="ExternalOutput"):
    T = {}
    def di(name, shape, dt=F32):
        T[name] = nc.dram_tensor(name, list(shape), dt, kind="ExternalInput").ap()
    def do(name, shape, dt=F32):
        T[name] = nc.dram_tensor(name, list(shape), dt, kind=out_kind).ap()
    T["xin"] = nc.dram_tensor("xin", [NQ, 128, 1024], F32, kind=xin_kind).ap()
    di("w_in", [1024, IN_W]); di("anorm", [128, 1024]); di("gq", [128, 384]); di("kg", [128, 256])
    di("wsT", [128, 4, 128]); di("gbT", [128, 4]); di("w1", [2, 2048, 64])
    di("ropeq", [NQ, 128, 384]); di("ropek", [NQ, 128, 384]); di("cmask", [128, 128]); di("idn", [128, 128])
    do("o_ks", [2, 2, 64, NQ, 128], BF16)
    do("o_v", [2, NQ, 128, 128], BF16)
    do("o_U", [NQ, 64, 384])
    do("o_cA", [4, 2, 64, NQ * 8])
    do("o_rqT", [NQ, 64, 768], BF16)
    do("o_inner", [NQ, 128, 384])
    do("o_sg", [NQ, 128, 384], BF16)
    do("o_qnT", [NQ, 64, 768], BF16)
    do("o_gates", [NQ, 128, 18])
    do("o_gmT", [NQ, 128, 256], BF16)
    return T


def emit_A(nc, S, T, NQ):
    with ExitStack() as st:
        sb = lambda name, shape, dt: st.enter_context(nc.sbuf_tensor("A_" + T.get("tag", "") + name, shape, dt))
        ps = lambda name, shape, dt: st.enter_context(nc.psum_tensor("A_" + T.get("tag", "") + name, shape, dt))
        w_sb = sb("w_sb", [128, 8, IN_W], BF16)
        gain = sb("gain", [128, 1024], F32)
        gq = sb("gq", [128, 384], F32)
        kg = sb("kg", [128, 256], F32)
        wc = sb("wc", [128, 4, 128], BF16)
        wcf = sb("wcf", [128, 4, 128], F32)
        gbT = sb("gbT", [128, 4], F32)
        w1 = sb("w1", [64, 2, 32, 64], BF16)
        cmask = sb("cmask", [128, 128], F32)
        ident = sb("ident", [128, 128], BF16)
        acT = sb("acT", [64, 4, NQ * 128], BF16)
        x_t = [sb("x%d" % p, [128, 1024], F32) for p in range(2)]
        rq_t = [sb("rq%d" % p, [128, 384], F32) for p in range(2)]
        rk_t = [sb("rk%d" % p, [128, 384], F32) for p in range(2)]
        junk = sb("junk", [128, 1024], BF16)
        ss = sb("ss", [128, 40], F32)
        h = sb("h", [128, 1024], BF16)
        hTs = [sb("hT%d" % i, [128, 1024], BF16) for i in range(2)]
        rf = sb("rf", [128, 384], F32)
        t1 = sb("t1", [128, 192], F32); t2 = sb("t2", [128, 192], F32)
        t3 = sb("t3", [128, 192], F32); t4 = sb("t4", [128, 192], F32)
        qp = sb("qp", [128, 384], BF16)
        kp = sb("kp", [128, 384], BF16)
        qpT = sb("qpT", [64, 768], BF16)
        kpT = sb("kpT", [64, 768], BF16)
        v = sb("v", [128, 384], BF16)
        scT = sb("scT", [128, 768], BF16)
        inner = sb("inner", [128, 384], F32)
        U = sb("U", [64, 384], F32)
        sg = sb("sg", [128, 384], BF16)
        sq = sb("sq", [128, 512], F32)
        qn = sb("qn", [128, 384], F32)
        qnb = sb("qnb", [128, 384], BF16)
        qnT = sb("qnT", [64, 768], BF16)
        ac = sb("ac", [128, 256], BF16)
        kns = [sb("kn%d" % i, [128, 128], F32) for i in range(2)]
        knbs = [sb("knb%d" % i, [128, 128], BF16) for i in range(2)]
        ksT = sb("ksT", [64, 256], BF16)
        vsb = sb("vsb", [128, 130], BF16)
        gates = sb("gates", [128, 18], F32)
        zf = sb("zf", [128, 512], F32)
        z2 = sb("z2", [128, 512], F32)
        zg = sb("zg", [128, 512], F32)
        vn = sb("vn", [128, 256], BF16)
        gm1 = sb("gm1", [128, 256], F32)
        gmb = sb("gmb", [128, 256], BF16)
        gmT = sb("gmT", [128, 256], BF16)
        cA = sb("cA", [64, NQ * 8], F32)
        pT = ps("pT", [128, 1024], BF16)
        pP = [ps("pP%d" % p, [128, 512], F32) for p in range(2)]
        pTr = ps("pTr", [128, 1024], BF16)
        pSc = ps("pSc", [128, 1024], F32)
        pIn = ps("pIn", [128, 512], F32)
        pU = ps("pU", [128, 512], F32)

        wv = T["w_in"].rearrange("(c p) f -> p c f", p=128)
        for c in range(8):
            S.dma("pool", w_sb[:, c, :], wv[:, c, :], writes=["w_sb"])
        S.dma("sp", gain[:], T["anorm"], writes=["gain"])
        S.dma("sp", gq[:], T["gq"], writes=["gq"])
        S.dma("sp", kg[:], T["kg"], writes=["kg"])
        S.dma("sp", wcf[:], T["wsT"], writes=["wcf"])
        S.dma("sp", gbT[:], T["gbT"], writes=["gbT"])
        S.dma("sp", cmask[:], T["cmask"], writes=["cmask"])
        S.dma("pool", ident[:], T["idn"], writes=["ident"])
        S.dma("pool", w1[:], T["w1"].rearrange("k (l d) o -> d k l o", d=64), writes=["w1"])
        mset(S, "pool", vsb[:], 1.0, w=["vsb", "vsb1"])
        tt(S, "dve", wc[:], wcf[:], cmask[:].unsqueeze(1).to_broadcast([128, 4, 128]), ALU.mult, r=["wcf", "cmask"], w=["wc"])

        defer = []
        def run_deferred(keep=0):
            while len(defer) > keep:
                defer.pop(0)()
        v3 = lambda a: a[:].rearrange("p (h d) -> p h d", h=6)

        def head(j):
            p = j % 2
            X = "x%d" % p
            S.dma("sp", x_t[p][:], T["xin"][j], writes=[X])
            S.dma("sp", rq_t[p][:], T["ropeq"][j], writes=["rq%d" % p])
            S.dma("sp", rk_t[p][:], T["ropek"][j], writes=["rk%d" % p])
            act(S, junk[:], x_t[p][:], AF.Square, r=[X], w=["junk", "ss0"], accum_out=ss[:, 0:1])
            rsq(S, ss[:, 2:3], ss[:, 1:2], ss[:, 0:1], 1.0 / 1024, EPS, r=["ss0"], wt=["ss1"], w=["ss2"])
            stt(S, "dve", h[:], x_t[p][:], ss[:, 2:3], gain[:], ALU.mult, ALU.mult, r=[X, "ss2", "gain"], w=["h"])

        def head_pe(j):
            p = j % 2
            for c in range(8):
                tr(S, pT[:, c * 128:(c + 1) * 128], h[:, c * 128:(c + 1) * 128], ident[:], r=["h", "ident"], w=["pT"])
            cp(S, "act", hTs[p][:], pT[:], r=["pT"], w=["hT%d" % p])

        head(0)
        head_pe(0)
        for j in range(NQ):
            p = j % 2
            hT = hTs[p]; HT = "hT%d" % p
            for s, (c0, c1) in enumerate(SLABS):
                wd = c1 - c0
                pp = pP[s % 2]
                PP = "pP%d" % (s % 2)
                for c in range(8):
                    mm(S, pp[:, 0:wd], hT[:, c * 128:(c + 1) * 128], w_sb[:, c, c0:c1], start=(c == 0), stop=(c == 7),
                       r=[HT, "w_sb"], w=[PP])
                run_deferred(1)
                if s == 4 and j + 1 < NQ:
                    head(j + 1)
                if s == 6 and j + 1 < NQ:
                    head_pe(j + 1)
                if s in (0, 1):
                    tab = rq_t[p] if s == 0 else rk_t[p]
                    TAB = ("rq%d" if s == 0 else "rk%d") % p
                    dst = qp if s == 0 else kp
                    DST = "qp" if s == 0 else "kp"
                    dT = qpT if s == 0 else kpT
                    DT = "qpT" if s == 0 else "kpT"
                    cp(S, "act", rf[:], pp[:, 0:384], r=[PP], w=["rf"])
                    xv = rf[:].rearrange("p (h t d) -> p h t d", h=6, t=2)
                    tv = tab[:].rearrange("p (t h d) -> p t h d", t=2, h=6)
                    x1, x2 = xv[:, :, 0, :], xv[:, :, 1, :]
                    co, si = tv[:, 0], tv[:, 1]
                    tt(S, "dve", v3(t1), x1, co, ALU.mult, r=["rf", TAB], w=["t1"])
                    tt(S, "pool", v3(t2), x2, si, ALU.mult, r=["rf", TAB], w=["t2"])
                    tt(S, "dve", v3(t3), x2, co, ALU.mult, r=["rf", TAB], w=["t3"])
                    tt(S, "pool", v3(t4), x1, si, ALU.mult, r=["rf", TAB], w=["t4"])
                    dv = dst[:].rearrange("p (h t d) -> p h t d", h=6, t=2)
                    tt(S, "dve", dv[:, :, 0, :], v3(t1), v3(t2), ALU.subtract, r=["t1", "t2"], w=[DST])
                    tt(S, "pool", dv[:, :, 1, :], v3(t3), v3(t4), ALU.add, r=["t3", "t4"], w=[DST])
                    def st2(j=j, s=s, dst=dst, DST=DST, dT=dT, DT=DT):
                        for hh in range(6):
                            tr(S, pTr[0:64, hh * 128:(hh + 1) * 128], dst[:, hh * 64:(hh + 1) * 64], ident[:], r=[DST, "ident"], w=["pTr"])
                        cp(S, "act", dT[:], pTr[0:64, 0:768], r=["pTr"], w=[DT])
                        if s == 0:
                            S.dma("sp", T["o_rqT"][j], qpT[:], reads=["qpT"])
                    defer.append(st2)
                elif s == 2:
                    cp(S, "act", v[:], pp[:, 0:384], r=[PP], w=["v"])
                    def st2(j=j):
                        for hh in range(6):
                            mm(S, pSc[:, hh * 128:(hh + 1) * 128], kpT[:, hh * 128:(hh + 1) * 128], qpT[:, hh * 128:(hh + 1) * 128],
                               r=["kpT", "qpT"], w=["pSc"])
                        tt(S, "dve", scT[:].rearrange("p (h n) -> p h n", h=6), pSc[:, 0:768].rearrange("p (h n) -> p h n", h=6),
                           cmask[:].unsqueeze(1).to_broadcast([128, 6, 128]), ALU.mult, r=["pSc", "cmask"], w=["scT"])
                        for hh in range(6):
                            mm(S, pU[0:64, hh * 64:(hh + 1) * 64], kp[:, hh * 64:(hh + 1) * 64], v[:, hh * 64:(hh + 1) * 64],
                               r=["kp", "v"], w=["pU"])
                        cp(S, "act", U[:], pU[0:64, 0:384], r=["pU"], w=["U"])
                        S.dma("sp", T["bU%d" % (j // ((NQ + 1) // 2))].rearrange("(j d) e -> j d e", d=64)[j % ((NQ + 1) // 2)], U[:], reads=["U"])
                        for hh in range(6):
                            mm(S, pIn[:, hh * 64:(hh + 1) * 64], scT[:, hh * 128:(hh + 1) * 128], v[:, hh * 64:(hh + 1) * 64],
                               r=["scT", "v"], w=["pIn"])
                        cp(S, "act", inner[:], pIn[:, 0:384], r=["pIn"], w=["inner"])
                        S.dma("sp", T["o_inner"][j], inner[:], reads=["inner"])
                    defer.append(st2)
                elif s == 3:
                    act(S, sg[:], pp[:, 0:384], AF.Silu, r=[PP], w=["sg"])
                    S.dma("sp", T["o_sg"][j], sg[:], reads=["sg"])
                elif s == 4:
                    act(S, sq[:, 0:384], pp[:, 0:384], AF.Square, r=[PP], w=["sq"])
                    red(S, "dve", ss[:, 3:9], sq[:, 0:384].rearrange("p (h d) -> p h d", h=6), r=["sq"], w=["sqa"])
                    rsq(S, ss[:, 9:15], ss[:, 33:39], ss[:, 3:9], 1.0, 64 * EPS, r=["sqa"], wt=["sqt"], w=["sqb"])
                    tt(S, "dve", qn[:].rearrange("p (h d) -> p h d", h=6), pp[:, 0:384].rearrange("p (h d) -> p h d", h=6),
                       ss[:, 9:15].unsqueeze(2).to_broadcast([128, 6, 64]), ALU.mult, r=[PP, "sqb"], w=["qn"])
                    tt(S, "pool", qnb[:], qn[:], gq[:], ALU.mult, r=["qn", "gq"], w=["qnb"])
                    def st2(j=j):
                        for hh in range(6):
                            tr(S, pTr[0:64, hh * 128:(hh + 1) * 128], qnb[:, hh * 64:(hh + 1) * 64], ident[:], r=["qnb", "ident"], w=["pTr"])
                        cp(S, "act", qnT[:], pTr[0:64, 0:768], r=["pTr"], w=["qnT"])
                        S.dma("sp", T["o_qnT"][j], qnT[:], reads=["qnT"])
                    defer.append(st2)
                elif s in (5, 6):
                    if s == 5:
                        cp(S, "act", ac[:], pp[:, 0:256], r=[PP], w=["ac"])
                        ko, vo, br = 256, 384, 0
                    else:
                        ko, vo, br = 0, 128, 1
                        act(S, gates[:], pp[:, 256:274], AF.Sigmoid, r=[PP], w=["gates"])
                        S.dma("sp", T["o_gates"][j], gates[:], reads=["gates"])
                    act(S, sq[:, 0:128], pp[:, ko:ko + 128], AF.Square, r=[PP], w=["sq"])
                    red(S, "dve", ss[:, 15:17], sq[:, 0:128].rearrange("p (h d) -> p h d", h=2), r=["sq"], w=["ska"])
                    rsq(S, ss[:, 19:21], ss[:, 17:19], ss[:, 15:17], 1.0 / 64, EPS, r=["ska"], wt=["skb"], w=["skc"])
                    kn = kns[br]; knb = knbs[br]
                    tt(S, "dve", kn[:].rearrange("p (h d) -> p h d", h=2), pp[:, ko:ko + 128].rearrange("p (h d) -> p h d", h=2),
                       ss[:, 19:21].unsqueeze(2).to_broadcast([128, 2, 64]), ALU.mult, r=[PP, "skc"], w=["kn%d" % br])
                    tt(S, "pool", knb[:], kn[:], kg[:, br * 128:(br + 1) * 128], ALU.mult, r=["kn%d" % br, "kg"], w=["knb%d" % br])
                    cp(S, "act", vsb[:].rearrange("p (g e) -> p g e", g=2)[:, :, 0:64], pp[:, vo:vo + 128].rearrange("p (g d) -> p g d", g=2), r=[PP], w=["vsb"])
                    S.dma("sp", T["bv%d" % br].rearrange("(j n) c -> j n c", j=NQ)[j], vsb[:], reads=["vsb", "vsb1"])
                    def st2(j=j, s=s, br=br, knb=knb):
                        if s == 5:
                            for sl in range(4):
                                tr(S, pTr[0:64, sl * 128:(sl + 1) * 128], ac[:, sl * 64:(sl + 1) * 64], ident[:], r=["ac", "ident"], w=["pTr"])
                            cp(S, "act", acT[:, :, j * 128:(j + 1) * 128], pTr[0:64, 0:512].rearrange("p (s n) -> p s n", s=4), r=["pTr"], w=["acT"])
                        for g in range(2):
                            tr(S, pTr[0:64, g * 128:(g + 1) * 128], knb[:, g * 64:(g + 1) * 64], ident[:], r=["knb%d" % br, "ident"], w=["pTr"])
                        cp(S, "act", ksT[:], pTr[0:64, 0:256], r=["pTr"], w=["ksT"])
                        bks4 = T["bks%d" % br].rearrange("(g d) (j n) -> g d j n", g=2, j=NQ)
                        S.dma("sp", bks4[:, :, j, :].rearrange("g d n -> d g n"), ksT[:].rearrange("p (g n) -> p g n", g=2), reads=["ksT"])
                    defer.append(st2)
                else:
                    cp(S, "act", zf[:], pp[:, 0:512], r=[PP], w=["zf"])
                    tt(S, "pool", z2[:], zf[:], zf[:], ALU.mult, r=["zf"], w=["z2"])
                    ts(S, "dve", z2[:], z2[:], 0.044715, 1.0, ALU.mult, ALU.add, r=["z2"], w=["z2"])
                    tt(S, "pool", z2[:], z2[:], zf[:], ALU.mult, r=["z2", "zf"], w=["z2"])
                    act(S, zg[:], z2[:], AF.Sigmoid, r=["z2"], w=["zg"], scale=1.5957691216057308)
                    tt(S, "dve", zg[:], zg[:], zf[:], ALU.mult, r=["zg", "zf"], w=["zg"])
                    act(S, sq[:, 0:256], zg[:, 256:512], AF.Square, r=["zg"], w=["sq"])
                    red(S, "dve", ss[:, 21:25], sq[:, 0:256].rearrange("p (h d) -> p h d", h=4), r=["sq"], w=["sga"])
                    rsq(S, ss[:, 29:33], ss[:, 25:29], ss[:, 21:25], 1.0 / 64, EPS, r=["sga"], wt=["sgb"], w=["sgc"])
                    tt(S, "dve", vn[:].rearrange("p (h d) -> p h d", h=4), zg[:, 256:512].rearrange("p (h d) -> p h d", h=4),
                       ss[:, 29:33].unsqueeze(2).to_broadcast([128, 4, 64]), ALU.mult, r=["zg", "sgc"], w=["vn"])
                    def st2(j=j):
                        for g in range(4):
                            mm(S, pU[:, g * 64:(g + 1) * 64], wc[:, g, :], vn[:, g * 64:(g + 1) * 64], r=["wc", "vn"], w=["pU"])
                        tt(S, "dve", gm1[:].rearrange("p (h d) -> p h d", h=4), pU[:, 0:256].rearrange("p (h d) -> p h d", h=4),
                           gbT[:].unsqueeze(2).to_broadcast([128, 4, 64]), ALU.add, r=["pU", "gbT"], w=["gm1"])
                        tt(S, "pool", gmb[:], gm1[:], zg[:, 0:256], ALU.mult, r=["gm1", "zg"], w=["gmb"])
                        for c in range(2):
                            tr(S, pTr[:, c * 128:(c + 1) * 128], gmb[:, c * 128:(c + 1) * 128], ident[:], r=["gmb", "ident"], w=["pTr"])
                        cp(S, "act", gmT[:], pTr[:, 0:256], r=["pTr"], w=["gmT"])
                        S.dma("sp", T["o_gmT"][j], gmT[:], reads=["gmT"])
                    defer.append(st2)
        run_deferred()
        NG = NQ * 8
        for sl in range(4):
            kv = sl // 2
            for half in range(2):
                for lp in range(16):
                    mm(S, pU[0:64, 0:NG], w1[:, kv, half * 16 + lp, :], acT[:, sl, lp::16], start=(lp == 0), stop=(lp == 15),
                       r=["w1", "acT"], w=["pU"])
                cp(S, "act", cA[:], pU[0:64, 0:NG], r=["pU"], w=["cA"])
                S.dma("sp", T["bcA"].rearrange("(s h o) c -> s h o c", s=4, h=2)[sl, half], cA[:], reads=["cA"])


import math

def t5_bucket_np(dist):
    n = np.maximum(dist, 0)
    nf = np.maximum(n, 1).astype(np.float32)
    large = 16 + (np.log(nf / np.float32(16)) / np.float32(math.log(8.0)) * np.float32(16)).astype(np.int32)
    large = np.minimum(large, 31)
    return np.where(n < 16, n, large)


def host_consts_B(NQ):
    NB = 4 * NQ
    c = {}
    c["idn"] = np.eye(128, dtype=np.float32)
    E = np.zeros((128, NB, 128), np.float32)
    for kb in range(NB):
        if 2 * kb < 128:
            E[2 * kb, kb, 0:64] = 1
            E[2 * kb + 1, kb, 64:128] = 1
    c["E"] = E
    def ov(cs, ssv):
        return np.clip(np.minimum(cs + 32, ssv + 64) - np.maximum(cs, ssv), 0, None) // 16
    n = np.arange(512)
    j = np.arange(128)
    OV = ov(n[:, None] * 16, j[None, :] * 64).astype(np.float32)
    c["OVF"] = np.ascontiguousarray(OV.reshape(4, 128, 128).transpose(1, 0, 2))
    npr = np.arange(16)
    dl = np.arange(256) - 128
    c["OVB"] = ov((-128 + 16 * npr)[:, None], (64 * dl)[None, :]).astype(np.float32)
    Dt = np.zeros((128, 256), np.float32)
    for ql in range(128):
        for idx in range(256):
            d = idx - 128
            if ql < 64:
                v = 1e9 if d == -1 else 2e9 if d == 0 else -1e9 if d >= 1 else 0.0
            else:
                v = 1e9 if d == 0 else 2e9 if d == 1 else -1e9 if d >= 2 else 0.0
            Dt[ql, idx] = v
    c["Dt"] = Dt
    gam = 1.0 - 2.0 ** (-5.0 - np.arange(6, dtype=np.float64))
    rdec = np.stack([gam ** 128, gam ** 127, gam], 0)
    c["rdec"] = np.ascontiguousarray(np.broadcast_to(rdec[None, :, :, None], (64, 3, 6, 64)).reshape(64, 3, 384)).astype(np.float32)
    kl = np.arange(128)[:, None]; ql = np.arange(128)[None, :]
    m0 = (ql >= kl).astype(np.float32)
    c["mask0"] = np.ascontiguousarray(np.broadcast_to(m0[:, None, :], (128, 3, 128)).reshape(128, 384))
    c["neg0"] = (c["mask0"] - 1.0) * 30000.0
    m4 = (kl > ql).astype(np.float32)
    c["B4"] = np.ascontiguousarray(np.broadcast_to(((m4 - 1.0) * 30000.0)[:, None, :], (128, 3, 128)).reshape(128, 384))
    distc = ql + 97 - 16 * np.arange(16)[:, None]
    mc = (distc >= 0).astype(np.float32)
    mc0 = mc * (np.arange(16)[:, None] >= 8)
    bc3 = lambda a: np.ascontiguousarray(np.broadcast_to(a[:, None, :], (16, 3, 128)).reshape(16, 384))
    c["maskc"] = np.stack([bc3(mc), bc3(mc0)], 0)
    c["negc"] = (c["maskc"] - 1.0) * 30000.0
    return c


def bias_tabs(rel_bias):
    kl = np.arange(128)[:, None]; ql = np.arange(128)[None, :]
    b0 = t5_bucket_np(ql - kl); b1 = t5_bucket_np(128 + ql - kl)
    bcn = t5_bucket_np(ql + 97 - 16 * np.arange(16)[:, None])
    rb = rel_bias.reshape(32, 2, 3)
    d = {}
    d["tab0"] = np.ascontiguousarray(rb[b0].transpose(2, 0, 3, 1).reshape(2, 128, 384))
    d["tab1"] = np.ascontiguousarray(rb[b1].transpose(2, 0, 3, 1).reshape(2, 128, 384))
    d["tabc"] = np.ascontiguousarray(rb[bcn].transpose(2, 0, 3, 1).reshape(2, 16, 384))
    t31 = rb[31]
    d["tab31"] = np.ascontiguousarray(np.broadcast_to(t31[:, None, :, None], (2, 128, 3, 128)).reshape(2, 128, 384))
    return d


def layer_inputs_B(inp, l):
    d = {}
    d["w_out"] = np.ascontiguousarray(inp["w_out"][l])
    d["fnorm"] = np.ascontiguousarray(np.broadcast_to(inp["ffn_norm"][l][None, :], (128, 1024)))
    d["w_gu"] = np.ascontiguousarray(inp["w_gate_up"][l])
    d["w_dn"] = np.ascontiguousarray(inp["w_down"][l])
    d["pe"] = np.ascontiguousarray(inp["cmp_pe"][l].reshape(2, 16, 128).transpose(2, 0, 1))
    d["w1f"] = np.ascontiguousarray(inp["cmp_w1"][l])
    d["w2"] = np.ascontiguousarray(inp["cmp_w2"][l].transpose(1, 0, 2))
    d["kg0"] = np.ascontiguousarray(np.broadcast_to(inp["nsa_k_gain"][l][0][None, :], (128, 64)))
    return d


def declare_B(nc, NQ, in_kind="ExternalInput", out_kind="ExternalOutput", decl_x=True):
    NB = 4 * NQ
    T = {}
    def di(name, shape, dt=F32, kind="ExternalInput"):
        T[name] = nc.dram_tensor(name, list(shape), dt, kind=kind).ap()
    if decl_x:
        di("xin", [NQ, 128, 1024])
    di("g_ks", [4, 2, 2, 64, NQ, 128], BF16, in_kind)
    di("g_v", [4, 2, NQ, 128, 128], BF16, in_kind)
    di("g_U", [4, NQ, 64, 384], F32, in_kind)
    di("g_cA", [4, 4, 2, 64, NQ * 8], F32, in_kind)
    for nm, shp, dt in (("o_rqT", [NQ, 64, 768], BF16), ("o_inner", [NQ, 128, 384], F32), ("o_sg", [NQ, 128, 384], BF16),
                        ("o_qnT", [NQ, 64, 768], BF16), ("o_gates", [NQ, 128, 18], F32), ("o_gmT", [NQ, 128, 256], BF16)):
        di(nm, shp, dt, in_kind)
    di("w_out", [1024, 1024]); di("fnorm", [128, 1024]); di("w_gu", [1024, 2 * DFF]); di("w_dn", [DFF, 1024])
    di("pe", [128, 2, 16]); di("w1f", [2, 2048, 64]); di("w2", [64, 2, 64]); di("kg0", [128, 64])
    di("idn", [128, 128]); di("E", [128, NB, 128]); di("OVF", [128, 4, 128]); di("OVB", [16, 256]); di("Dt", [128, 256])
    di("rdec", [64, 3, 384]); di("mask0", [128, 384]); di("neg0", [128, 384]); di("B4", [128, 384])
    di("maskc", [2, 16, 384]); di("negc", [2, 16, 384])
    di("tab0", [2, 128, 384]); di("tab1", [2, 128, 384]); di("tabc", [2, 16, 384]); di("tab31", [2, 128, 384])
    T["xout"] = nc.dram_tensor("xout", [NQ, 128, 1024], F32, kind=out_kind).ap()
    return T


def gelu_ops(S, out_bf, x, tmp, tmp2, keys):
    kx, kt, kt2, ko = keys
    tt(S, "pool", tmp, x, x, ALU.mult, r=[kx], w=[kt])
    ts(S, "dve", tmp, tmp, 0.044715, 1.0, ALU.mult, ALU.add, r=[kt], w=[kt])
    tt(S, "pool", tmp, tmp, x, ALU.mult, r=[kt, kx], w=[kt])
    act(S, tmp2, tmp, AF.Sigmoid, r=[kt], w=[kt2], scale=1.5957691216057308)
    tt(S, "dve", out_bf, tmp2, x, ALU.mult, r=[kt2, kx], w=[ko])


GATH = ["gks0", "gv0", "gU0", "gks1", "gv1", "gU1", "gcA"]
ZS = 5
ZW = 8


def host_consts_Bu(NQ, r):
    NB = 4 * NQ
    NC = 8 * NB - 1
    CW = 8 * NB + 40
    NCHF = (32 * (NQ - 1) + 24 + 127) // 128
    P = 32 - 8 * r
    c0 = host_consts_B(NQ)
    c = {"idn": c0["idn"], "E": c0["E"], "rdec": c0["rdec"]}
    OVBr = np.zeros((16, 264), np.float32); OVBr[:, 2 * r:2 * r + 256] = c0["OVB"]
    Dtr = np.zeros((128, 264), np.float32); Dtr[:, 2 * r:2 * r + 256] = c0["Dt"]; Dtr[:, 2 * r + 256:] = -1e9
    c["OVBr"] = OVBr; c["Dtr"] = Dtr
    n = np.arange(128 * NCHF) - P
    j = np.arange(128)
    cs = n[:, None] * 16; ssv = j[None, :] * 64
    OV = (np.clip(np.minimum(cs + 32, ssv + 64) - np.maximum(cs, ssv), 0, None) // 16).astype(np.float32)
    OV[(n < 0) | (n >= NC)] = 0
    c["OVFr"] = np.ascontiguousarray(OV.reshape(NCHF, 128, 128).transpose(1, 0, 2))
    sel = np.zeros((64, 4), np.float32); sel[:, r] = 1
    c["selr"] = sel
    gam_ = 1.0 - 2.0 ** (-5.0 - np.arange(6, dtype=np.float64))
    dec_, c127_ = gam_ ** 128, gam_ ** 127
    pf = np.zeros((10, 6), np.float64)
    for k in range(4):
        pf[k] = dec_ ** (3 - k) * c127_
        pf[4 + k] = gam_ * c127_ * dec_ ** (r - 1 - k) if r > k else 0.0
    pf[8] = gam_ * dec_ ** r
    pf[9] = dec_ ** 4
    c["pfx"] = np.ascontiguousarray(np.broadcast_to(pf[None, :, :, None], (64, 10, 6, 64)).reshape(64, 10, 384)).astype(np.float32)
    zf0 = np.zeros((128, 1), np.float32); zf0[:P] = -30000.0
    c["zf0"] = zf0
    zmask = np.zeros((ZS + ZW, 128, 384), np.float32); zneg = np.zeros((ZS + ZW, 128, 384), np.float32)
    kinds = zone_kinds(r)
    for s_, kd in enumerate(kinds):
        if kd == "B0":
            zmask[s_] = c0["mask0"]; zneg[s_] = c0["neg0"]
        elif kd == "B1":
            zmask[s_] = 1.0
        elif kd == "NEG":
            zneg[s_] = -30000.0
        elif kd == "B4":
            zneg[s_] = c0["B4"]
    c["zmask"] = zmask; c["zneg"] = zneg
    cm = np.stack([c0["maskc"][0], c0["maskc"][1] if r == 0 else c0["maskc"][0]], 0)
    c["cmask"] = cm; c["cneg"] = (cm - 1.0) * 30000.0
    return c


def zone_kinds(r):
    kinds = []
    for s_ in range(ZS):
        dl = r + 1 - s_
        kinds.append("NEG" if dl < 0 else "B0" if dl == 0 else "B1" if dl == 1 else "Z")
    for s_ in range(ZW):
        dl = r + 4 - s_
        kinds.append("NEG" if (dl < 0 or dl > 4) else "B0" if dl == 0 else "B1" if dl == 1 else "B4" if dl == 4 else "Z")
    return kinds


def bias_tabs_u(rel_bias, r):
    t = bias_tabs(rel_bias)
    kinds = zone_kinds(r)
    ztab = np.zeros((ZS + ZW, 2, 128, 384), np.float32)
    for s_, kd in enumerate(kinds):
        ztab[s_] = t["tab0"] if kd == "B0" else t["tab1"] if kd == "B1" else t["tab31"]
    return {"ztab": ztab, "zt31": t["tab31"], "ctab": t["tabc"]}


def declare_Bu(nc, NQ, in_kind="ExternalInput", out_kind="ExternalOutput"):
    NB = 4 * NQ
    CW = 8 * NB + 40
    NCHF = (32 * (NQ - 1) + 24 + 127) // 128
    T = {}
    def di(name, shape, dt=F32, kind="ExternalInput"):
        T[name] = nc.dram_tensor(name, list(shape), dt, kind=kind).ap()
    di("xin", [NQ, 128, 1024])
    di("ks2", [128, 2, NB * 128], BF16, in_kind)
    di("vs2", [128, 2 * NB * 2, 65], BF16, in_kind)
    di("U2", [NB, 64, 384], F32, in_kind)
    di("cAsh", [64, 8, CW], F32, in_kind)
    for nm, shp, dt in (("o_rqT", [NQ, 64, 768], BF16), ("o_inner", [NQ, 128, 384], F32), ("o_sg", [NQ, 128, 384], BF16),
                        ("o_qnT", [NQ, 64, 768], BF16), ("o_gates", [NQ, 128, 18], F32), ("o_gmT", [NQ, 128, 256], BF16)):
        di(nm, shp, dt, in_kind)
    di("w_out", [1024, 1024]); di("fnorm", [128, 1024]); di("w_gu", [1024, 2 * DFF]); di("w_dn", [DFF, 1024])
    di("pe", [128, 2, 16]); di("w1f", [2, 2048, 64]); di("w2", [64, 2, 64]); di("kg0", [128, 64])
    di("idn", [128, 128]); di("E", [128, NB, 128]); di("OVFr", [128, NCHF, 128]); di("OVBr", [16, 264]); di("Dtr", [128, 264])
    di("rdec", [64, 3, 384]); di("selr", [64, 4]); di("pfx", [64, 10, 384]); di("zf0", [128, 1])
    di("zmask", [ZS + ZW, 128, 384]); di("zneg", [ZS + ZW, 128, 384]); di("ztab", [ZS + ZW, 2, 128, 384]); di("zt31", [2, 128, 384])
    di("ctab", [2, 16, 384]); di("cmask", [2, 16, 384]); di("cneg", [2, 16, 384])
    T["xout"] = nc.dram_tensor("xout", [NQ, 128, 1024], F32, kind=out_kind).ap()
    return T


def emit_B(nc, S, T, NQ):
    NB = 4 * NQ
    CW = 8 * NB + 40
    NCHF = (32 * (NQ - 1) + 24 + 127) // 128
    NCHK = (CW + 127) // 128
    NZ = ZS + ZW
    with ExitStack() as st0:
        sb0 = lambda name, shape, dt: st0.enter_context(nc.sbuf_tensor("B_" + T.get("tag", "") + name, shape, dt))
        ident = sb0("ident", [128, 128], BF16)
        S.dma("pool", ident[:], T["idn"], writes=["ident"])
        with ExitStack() as st:
            sb = lambda name, shape, dt: st.enter_context(nc.sbuf_tensor("B1_" + T.get("tag", "") + name, shape, dt))
            ps = lambda name, shape, dt: st.enter_context(nc.psum_tensor("B1_" + T.get("tag", "") + name, shape, dt))
            ksT = sb("ksT", [128, 2, NB * 128], BF16)
            vs = sb("vs", [128, 2 * NB * 2, 65], BF16)
            kcT = sb("kcT", [128, CW], BF16)
            gT = sb("gT", [64, 4, CW], BF16)
            CV = sb("CV", [128, 2 * NCHF, 193], BF16)
            Gs = sb("Gs", [64, NQ, 384], BF16)
            OVB = sb("OVB", [16, 264], BF16)
            Dt = sb("Dt", [128, 264], F32)
            selr = sb("selr", [64, 4], F32)
            Z = sb("Z", [128, NZ * 2, 384], BF16)
            zf0 = sb("zf0", [128, 1], F32)
            Bc = sb("Bc", [16, 4, 384], BF16)
            w2 = sb("w2", [64, 2, 64], BF16)
            kg0 = sb("kg0", [128, 64], F32)
            pS = [ps("pS%d" % i, [128, 512], F32) for i in range(3)]
            pC = ps("pC", [128, 4, 256], F32)
            pO = ps("pO", [128, 2, 3, 65], F32)
            pR = ps("pR", [128, 512], F32)
            pTr = ps("pTr", [128, 1024], BF16)
            pM = pS[2]

            S.dma("pool", OVB[:], T["OVBr"], writes=["OVB"])
            S.dma("pool", CV[:, 0:NCHF, 65:193], T["OVFr"], writes=["CVo"])
            S.dma("pool", CV[:, NCHF:2 * NCHF, 65:193], T["OVFr"], writes=["CVo"])
            mset(S, "pool", CV[:, :, 64:65], 1.0, w=["CV1"])
            S.dma("sp", Dt[:], T["Dtr"], writes=["Dt"])
            S.dma("sp", selr[:], T["selr"], writes=["selr"])
            S.dma("sp", zf0[:], T["zf0"], writes=["zf0"])
            S.dma("pool", w2[:], T["w2"], writes=["w2"])
            S.dma("sp", kg0[:], T["kg0"], writes=["kg0"])
            import os as _os
            STOP = _os.environ.get("STOPB", "")
            if STOP == "loads":
                S.fence(); return
            stm = ExitStack()
            stm.__enter__()
            if True:
                sbb = lambda name, shape, dt: stm.enter_context(nc.sbuf_tensor("Bb_" + T.get("tag", "") + name, shape, dt))
                zt = [sbb("zt%d" % i, [128, 2, 384], F32) for i in range(1)]
                zm = [sbb("zm%d" % i, [128, 2, 384], F32) for i in range(1)]
                t31 = sbb("t31", [128, 2, 384], F32)
                tc_ = sbb("tc", [16, 2, 384], F32)
                mc = sbb("mc", [16, 4, 384], F32)
                Bcf = sbb("Bcf", [16, 4, 384], F32)
                S.dma("sp", t31[:], T["zt31"].rearrange("g p f -> p g f"), writes=["t31"])
                for s_ in range(NZ):
                    q = 0
                    S.dma("sp", zt[q][:], T["ztab"][s_].rearrange("g p f -> p g f"), writes=["zt%d" % q])
                    S.dma("sp", zm[q][:, 0, :], T["zmask"][s_], writes=["zm%d" % q])
                    S.dma("sp", zm[q][:, 1, :], T["zneg"][s_], writes=["zm%d" % q])
                    tt(S, "dve", zt[q][:], zt[q][:], t31[:], ALU.subtract, r=["zt%d" % q, "t31"], w=["zt%d" % q])
                    tt(S, "dve", zt[q][:], zt[q][:], zm[q][:, 0:1, :].to_broadcast([128, 2, 384]), ALU.mult, r=["zt%d" % q, "zm%d" % q], w=["zt%d" % q])
                    tt(S, "dve", zt[q][:], zt[q][:], zm[q][:, 1:2, :].to_broadcast([128, 2, 384]), ALU.add,
                       r=["zt%d" % q, "zm%d" % q], w=["zt%d" % q])
                    act(S, Z[:, 2 * s_:2 * s_ + 2, :], zt[q][:], AF.Exp, r=["zt%d" % q], w=["Z"])
                S.dma("sp", tc_[:], T["ctab"].rearrange("g p f -> p g f"), writes=["tc"])
                S.dma("sp", mc[:, 0:2, :], T["cmask"].rearrange("v p f -> p v f"), writes=["mc"])
                S.dma("sp", mc[:, 2:4, :], T["cneg"].rearrange("v p f -> p v f"), writes=["mc"])
                tt(S, "dve", tc_[:], tc_[:], t31[0:16, :, :], ALU.subtract, r=["tc", "t31"], w=["tc"])
                for var in range(2):
                    tt(S, "dve", Bcf[:, 2 * var:2 * var + 2, :], tc_[:], mc[:, var:var + 1, :].to_broadcast([16, 2, 384]), ALU.mult, r=["tc", "mc"], w=["Bcf"])
                    tt(S, "dve", Bcf[:, 2 * var:2 * var + 2, :], Bcf[:, 2 * var:2 * var + 2, :], mc[:, 2 + var:3 + var, :].to_broadcast([16, 2, 384]), ALU.add,
                       r=["Bcf", "mc"], w=["Bcf"])
                    act(S, Bc[:, 2 * var:2 * var + 2, :], Bcf[:, 2 * var:2 * var + 2, :], AF.Exp, r=["Bcf"], w=["Bc"])
            if True:
                sbc = lambda name, shape, dt: stm.enter_context(nc.sbuf_tensor("Bc_" + T.get("tag", "") + name, shape, dt))
                cAs = sbc("cAs", [64, 8, CW + 24], F32)
                w1c = sbc("w1c", [128, 2, 16, 64], BF16)
                pec = sbc("pec", [128, 2, 16], BF16)
                cvec = sbc("cvec", [64, 2], F32)
                kcb = sbc("kcb", [128, 128], BF16)
                sqc = sbc("sqc", [128, 64], F32)
                ssc = sbc("ssc", [128, 4], F32)
                cAu = sbc("cAu", [64, 8, 4, NQ * 8], F32)
                gcA = T["gcA"].rearrange("(q s o) c -> q o s c", q=4, s=8)
                for rr in range(4):
                    S.dma("sp", cAu[:, :, rr, :], gcA[rr], reads=GATH, writes=["cAu"])
                mset(S, "pool", cAs[:], 0.0, w=["cAs"])
                for rp in range(4):
                    for rr in range(4):
                        off = 32 - 8 * rp + 8 * rr
                        for s8 in range(8):
                            dstc = cAs[:, s8, off:off + 32 * NQ].rearrange("p (j x) -> p j x", x=32)[:, :, 0:8]
                            srcc = cAu[:, s8, rr, :].rearrange("p (j g) -> p j g", g=8)
                            stt(S, "dve", dstc, srcc, selr[:, rp:rp + 1], dstc, ALU.mult, ALU.add, r=["cAu", "selr", "cAs"], w=["cAs"])
                S.dma("pool", w1c[:], T["w1f"].rearrange("k (c p) o -> p k c o", p=128), writes=["w1c"])
                S.dma("pool", pec[:], T["pe"], writes=["pec"])
                for kv in range(2):
                    for c in range(16):
                        mm(S, pM[0:64, kv:kv + 1], w1c[:, kv, c, :], pec[:, kv, c:c + 1], start=(c == 0), stop=(c == 15),
                           r=["w1c", "pec"], w=["pS2"])
                cp(S, "act", cvec[:], pM[0:64, 0:2], r=["pS2"], w=["cvec"])
                pre = cAu[:].rearrange("p s q c -> p (s q c)")[:, 0:4 * CW].rearrange("p (s c) -> p s c", s=4)
                mset(S, "pool", pre[:], 0.0, w=["pre", "cAu"])
                cv4 = cAs[:, :, 0:CW].rearrange("p (s h) c -> p s h c", h=2)
                tt(S, "dve", pre[:, :, 0:CW - 1], cv4[:, :, 0, 0:CW - 1], cv4[:, :, 1, 1:CW], ALU.add, r=["cAs", "pre", "cAu"], w=["pre", "cAu"])
                for sl in range(4):
                    ts(S, "dve", pre[:, sl, :], pre[:, sl, :], cvec[:, sl // 2:sl // 2 + 1], None, ALU.add, r=["pre", "cvec"], w=["pre"])
                gelu_ops(S, gT[:], pre[:], cAs[:, 0:4, 0:CW], cAs[:, 4:8, 0:CW], ("pre", "cAs", "cAs", "gT"))
                for c in range(NCHK):
                    rows = min(128, CW - 128 * c)
                    for g in range(2):
                        mm(S, pM[0:rows, 64:128], gT[:, g, 128 * c:128 * c + rows], w2[:, 0, :], r=["gT", "w2"], w=["pS2"])
                        act(S, sqc[0:rows, :], pM[0:rows, 64:128], AF.Square, r=["pS2"], w=["sqc", "ssc0"], accum_out=ssc[0:rows, 0:1])
                        rsq(S, ssc[0:rows, 2:3], ssc[0:rows, 1:2], ssc[0:rows, 0:1], 1.0 / 64, EPS, r=["ssc0"], wt=["ssc1"], w=["ssc2"])
                        stt(S, "dve", kcb[0:rows, g * 64:(g + 1) * 64], pM[0:rows, 64:128], ssc[0:rows, 2:3], kg0[0:rows, :], ALU.mult, ALU.mult,
                            r=["pS2", "ssc2", "kg0"], w=["kcb"])
                        if c < NCHF:
                            mm(S, pM[0:rows, 128:192], gT[:, 2 + g, 128 * c:128 * c + rows], w2[:, 1, :], r=["gT", "w2"], w=["pS2"])
                            cp(S, "act", CV[0:rows, g * NCHF + c, 0:64], pM[0:rows, 128:192], r=["pS2"], w=["CVv"])
                    tr(S, pTr[:, 0:rows], kcb[0:rows, :], ident[0:rows, 0:rows], r=["kcb", "ident"], w=["pTr"])
                    cp(S, "act", kcT[:, 128 * c:128 * c + rows], pTr[:, 0:rows], r=["pTr"], w=["kcT"])
            if True:
                sbp = lambda name, shape, dt: stm.enter_context(nc.sbuf_tensor("Bp_" + T.get("tag", "") + name, shape, dt))
                Rst = sbp("Rst", [64, 384], F32)
                pfx = sbp("pfx", [64, 10, 384], F32)
                if NQ >= 12:
                    cAu_f = cAu[:].rearrange("p s q c -> p (s q c)")
                    cAs_f = cAs[:].rearrange("p s c -> p (s c)")
                    Ut = [cAu_f[:, i * 1536:(i + 1) * 1536].rearrange("p (k e) -> p k e", k=4) for i in range(2)]
                    tW = cAs_f[:, 0:1536].rearrange("p (k e) -> p k e", k=4)
                    tQ = cAs_f[:, 1536:3072].rearrange("p (k e) -> p k e", k=4)
                else:
                    Ut = [sbp("Ut%d" % i, [64, 4, 384], F32)[:] for i in range(2)]
                    tW = sbp("tW", [64, 4, 384], F32)[:]
                    tQ = sbp("tQ", [64, 4, 384], F32)[:]
                Wt = sbp("Wt", [64, 384], F32)
                Qt = sbp("Qt", [64, 384], F32)
                tG = sbp("tG", [64, 384], F32)
                S.dma("sp", pfx[:], T["pfx"], writes=["pfx"])
                mset(S, "pool", Rst[:], 0.0, w=["Rst"])
                for jj in range(NQ):
                    u = jj % 2
                    UT = "Ut%d" % u
                    extra_w = ["cAu", "pre"] if jj < 2 else []
                    S.dma("sp", Ut[u], T["gU%d" % (jj // ((NQ + 1) // 2))].rearrange("(q j d) e -> d q j e", q=4, d=64)[:, :, jj % ((NQ + 1) // 2), :], reads=GATH, writes=[UT] + extra_w)
                    tt(S, "pool", tQ, Ut[u], pfx[:, 4:8, :], ALU.mult, r=[UT, "pfx"], w=["tQ"] + (["cAs"] if jj == 0 else []))
                    red(S, "dve", Qt[:], tQ.rearrange("p k e -> p e k"), r=["tQ"], w=["Qt"])
                    tt(S, "dve", tG[:], Rst[:], pfx[:, 8, :], ALU.mult, r=["Rst", "pfx"], w=["tG"])
                    tt(S, "pool", Gs[:, jj, :], tG[:], Qt[:], ALU.add, r=["tG", "Qt"], w=["Gs%d" % jj])
                    if jj == NQ - 1:
                        break
                    tt(S, "pool", tW, Ut[u], pfx[:, 0:4, :], ALU.mult, r=[UT, "pfx"], w=["tW"] + (["cAs"] if jj == 0 else []))
                    red(S, "dve", Wt[:], tW.rearrange("p k e -> p e k"), r=["tW"], w=["Wt"])
                    tt(S, "dve", Rst[:], Rst[:], pfx[:, 9, :], ALU.mult, r=["Rst", "pfx"], w=["Rst"])
                    tt(S, "dve", Rst[:], Rst[:], Wt[:], ALU.add, r=["Rst", "Wt"], w=["Rst"])
            gks = [T["gks%d" % b_].rearrange("(q r) c -> q r c", q=4) for b_ in range(2)]
            gv = [T["gv%d" % b_].rearrange("(q j n) c -> q j n c", q=4, j=NQ) for b_ in range(2)]
            for rr in range(4):
                for br in range(2):
                    for g in range(2):
                        dst = ksT[g * 64:(g + 1) * 64, br, :].rearrange("p (j q n) -> p j q n", q=4, n=128)[:, :, rr, :]
                        src = gks[br][rr, g * 64:(g + 1) * 64, :].rearrange("d (j n) -> d j n", n=128)
                        S.dma("sp", dst, src, reads=GATH, writes=["ksT"])
                    dstv = vs[:, br * NB * 2:(br + 1) * NB * 2, :].rearrange("p (j q g) e -> p j q (g e)", q=4, g=2)[:, :, rr, :]
                    S.dma("sp", dstv, gv[br][rr].rearrange("j n c -> n j c"), reads=GATH, writes=["vs", "vs1"])
            S.fence()
            stm.__exit__(None, None, None)

            mixTs = [sb("mixT%d" % i, [128, 8, 128], BF16) for i in range(2)]
            wo = sb("wo", [128, 8, 1024], BF16)
            S.dma("pool", wo[:], T["w_out"].rearrange("(c p) f -> p c f", p=128), writes=["wo"])
            xres = [sb("xres%d" % i, [128, 1024], F32) for i in range(1)] * 2
            x1o = [sb("x1o%d" % i, [128, 1024], F32) for i in range(1)] * 2
            rqT = [sb("rqT%d" % p, [64, 768], BF16) for p in range(2)]
            inn = [sb("inn%d" % p, [128, 384], F32) for p in range(2)]
            sgt = [sb("sgt%d" % p, [128, 384], BF16) for p in range(2)]
            qnT = [sb("qnT%d" % p, [128, 768], BF16) for p in range(2)]
            for p_ in range(2):
                mset(S, "pool", qnT[p_][:], 0.0, w=["qnT%d" % p_])
            gat = [sb("gat%d" % p, [128, 18], F32) for p in range(2)]
            ro = sb("ro", [128, 384], F32)
            rsqv = sb("rsqv", [128, 384], F32)
            rss = sb("rss", [128, 24], F32)
            ron = sb("ron", [128, 384], F32)
            rob = sb("rob", [128, 384], BF16)
            eS = [sb("eS%d" % i, [128, 384], BF16) for i in range(4)]
            eC = [sb("eC%d" % i, [128, 384], BF16) for i in range(2)]
            eN = sb("eN", [16, 384], BF16)
            imp = sb("imp", [128, 128], F32)
            imw = sb("imw", [128, 128], F32)
            top = sb("top", [128, 16], F32)
            rcs = sb("rcs", [128, 12], F32)
            Mn = sb("Mn", [128, 128], BF16)
            MnT = sb("MnT", [128, 128], BF16)
            sc = sb("sc", [128, 9], F32)
            oa = sb("oa", [128, 192], F32)
            ob = sb("ob", [128, 192], F32)
            nsab = sb("nsab", [128, 384], BF16)
            MTs = [sb("MT%d" % i, [128, NB, 128], BF16) for i in range(2)]
            cSs = [sb("cS%d" % i, [128, 3, 193], F32) for i in range(2)]
            vnrs = [sb("vnr%d" % i, [16, 65], BF16) for i in range(2)]
            for i_ in range(2):
                mset(S, "pool", vnrs[i_][:, 64:65], 1.0, w=["vnr%d" % i_])
            cnt = {"ne": 0, "nce": 0, "nps": 0}
            deferred = []
            early = []

            def next_ps():
                k = cnt["nps"] % 3; cnt["nps"] += 1
                return pS[k], "pS%d" % k

            def loads(j):
                p = j % 2
                S.dma("sp", rqT[p][:], T["o_rqT"][j], writes=["rqT%d" % p])
                S.dma("sp", inn[p][:], T["o_inner"][j], writes=["inn%d" % p])
                S.dma("sp", sgt[p][:], T["o_sg"][j], writes=["sgt%d" % p])
                S.dma("sp", qnT[p][0:64, 0:384], T["o_qnT"][j][:, 0:384], writes=["qnT%d" % p])
                S.dma("sp", qnT[p][64:128, 384:768], T["o_qnT"][j][:, 384:768], writes=["qnT%d" % p])
                S.dma("sp", gat[p][:], T["o_gates"][j], writes=["gat%d" % p])

            def ret(j):
                p = j % 2
                mixT = mixTs[p]; MX = "mixT%d" % p
                for hh in range(6):
                    mm(S, pR[:, hh * 64:(hh + 1) * 64], rqT[p][:, hh * 128:(hh + 1) * 128], Gs[:, j, hh * 64:(hh + 1) * 64],
                       r=["rqT%d" % p, "Gs%d" % j], w=["pR"])
                tt(S, "dve", ro[:], pR[:, 0:384], inn[p][:], ALU.add, r=["pR", "inn%d" % p], w=["ro"])
                act(S, rsqv[:], ro[:], AF.Square, r=["ro"], w=["rsqv"])
                red(S, "dve", rss[:, 0:6], rsqv[:].rearrange("p (h d) -> p h d", h=6), r=["rsqv"], w=["rss0"])
                rsq(S, rss[:, 12:18], rss[:, 6:12], rss[:, 0:6], 1.0 / 64, EPS, r=["rss0"], wt=["rss1"], w=["rss2"])
                tt(S, "dve", ron[:].rearrange("p (h d) -> p h d", h=6), ro[:].rearrange("p (h d) -> p h d", h=6),
                   rss[:, 12:18].unsqueeze(2).to_broadcast([128, 6, 64]), ALU.mult, r=["ro", "rss2"], w=["ron"])
                tt(S, "pool", rob[:], ron[:], sgt[p][:], ALU.mult, r=["ron", "sgt%d" % p], w=["rob"])
                def ret2(mixT=mixT, MX=MX, j=j):
                    S.dma("sp", mixT[:, 6:8, :], T["o_gmT"][j].rearrange("p (c n) -> p c n", c=2), writes=[MX])
                    for c in range(3):
                        tr(S, pTr[:, c * 128:(c + 1) * 128], rob[:, c * 128:(c + 1) * 128], ident[:], r=["rob", "ident"], w=["pTr"])
                    cp(S, "act", mixT[:, 0:3, :], pTr[:, 0:384].rearrange("p (c n) -> p c n", c=3), r=["pTr"], w=[MX])
                early.append(ret2)

            def pre_round(j, g):
                nb = (2 * j + g) % 2
                p = j % 2
                imax = 4 * j + 3
                QK = "qnT%d" % p
                gp = slice(g * 64, (g + 1) * 64)
                qg = qnT[p][gp, g * 384:(g + 1) * 384]
                nf = 32 * j + 24
                nfc = (nf + 127) // 128
                var = 1 if j == 0 else 0
                vnr = vnrs[nb]; VN = "vnr%d" % nb
                cS = cSs[nb]; CS = "cS%d" % nb
                MT = MTs[nb]; MTK = "MT%d" % nb
                mm(S, pR[0:16, 448:512], gT[:, 2 + g, nf:nf + 16], w2[:, 1, :], r=["gT", "w2"], w=["pR"])
                cp(S, "act", vnr[:, 0:64], pR[0:16, 448:512], r=["pR"], w=[VN])
                pend = []
                def flush(keep=0):
                    while len(pend) > keep:
                        pend.pop(0)()
                def pv_far(ec, EC, rows, c):
                    for hh in range(3):
                        mm(S, pC[:, hh, 0:193], ec[0:rows, hh * 128:(hh + 1) * 128], CV[0:rows, g * NCHF + c, :], start=(c == 0 and hh != 1), stop=False,
                           r=[EC, "CVv", "CVo", "CV1"], w=["pC"], sgc=True)
                def pv_near():
                    for hh in range(3):
                        mm(S, pC[:, hh, 0:65], eN[:, hh * 128:(hh + 1) * 128], vnr[:, :], start=False, stop=True,
                           r=["eN", VN], w=["pC"], sgc=True)
                        mm(S, pC[:, hh, 65:193], eN[:, hh * 128:(hh + 1) * 128], OVB[:, 128 - 8 * j:256 - 8 * j], start=False, stop=True,
                           r=["eN", "OVB"], w=["pC"], sgc=True)
                for c in range(nfc):
                    rows = min(128, nf - 128 * c)
                    pp, PP = next_ps()
                    mm(S, pp[0:rows, 0:384], kcT[gp, 128 * c:128 * c + rows], qg, start=True, stop=True, r=["kcT", QK], w=[PP])
                    k_ = cnt["nce"] % 2; cnt["nce"] += 1
                    ec = eC[k_]; EC = "eC%d" % k_
                    if c == 0:
                        act(S, ec[0:rows, :], pp[0:rows, 0:384], AF.Exp, r=[PP, "zf0"], w=[EC], bias=zf0[0:rows, 0:1])
                    else:
                        act(S, ec[0:rows, :], pp[0:rows, 0:384], AF.Exp, r=[PP], w=[EC])
                    flush()
                    pend.append(lambda ec=ec, EC=EC, rows=rows, c=c: pv_far(ec, EC, rows, c))
                pp, PP = next_ps()
                mm(S, pp[0:16, 0:384], kcT[gp, nf:nf + 16], qg, start=True, stop=True, r=["kcT", QK], w=[PP])
                act(S, eN[:], pp[0:16, 0:384], AF.Exp, r=[PP], w=["eN"])
                tt(S, "dve", eN[:], eN[:], Bc[:, 2 * var + g, :], ALU.mult, r=["eN", "Bc"], w=["eN"])
                flush()
                pv_near()
                cp(S, "act", cS[:], pC[:, 0:3, 0:193], r=["pC"], w=[CS])
                if j >= 2:
                    ts(S, "dve", rcs[:, 0:3], cS[:, :, 64], 1e-30, None, ALU.max, r=[CS], w=["rcs"])
                    S.op("dve", lambda e: e.reciprocal(rcs[:, 0:3], rcs[:, 0:3]), reads=["rcs"], writes=["rcs"])
                    ts(S, "dve", imp[:], cS[:, 0, 65:193], rcs[:, 0:1], None, ALU.mult, r=[CS, "rcs"], w=["imp"])
                    stt(S, "dve", imp[:], cS[:, 1, 65:193], rcs[:, 1:2], imp[:], ALU.mult, ALU.add, r=[CS, "rcs", "imp"], w=["imp"])
                    stt(S, "dve", imp[:], cS[:, 2, 65:193], rcs[:, 2:3], imp[:], ALU.mult, ALU.add, r=[CS, "rcs", "imp"], w=["imp"])
                    tt(S, "dve", imp[:], imp[:], Dt[:, 128 - 8 * j:256 - 8 * j], ALU.add, r=["imp", "Dt"], w=["imp"])
                    mset(S, "dve", imp[:, 0:1], 3e9, w=["imp"])
                    S.op("dve", lambda e: e.max(out=top[:, 0:8], in_=imp[:]), reads=["imp"], writes=["top"])
                    S.op("dve", lambda e: e.match_replace(out=imw[:], in_to_replace=top[:, 0:8], in_values=imp[:], imm_value=-3e9),
                         reads=["imp", "top"], writes=["imw"])
                    S.op("dve", lambda e: e.max(out=top[:, 8:16], in_=imw[:]), reads=["imw"], writes=["top"])
                    ts(S, "dve", Mn[:], imp[:], top[:, 15:16], None, ALU.is_ge, r=["imp", "top"], w=["Mn"])
                    def fin():
                        tr(S, pTr[:, 512:640], Mn[:], ident[:], r=["Mn", "ident"], w=["pTr"])
                        cp(S, "dve", MnT[:], pTr[:, 512:640], r=["pTr"], w=["MnT"])
                        msl = T["mscr"][(2 * j + g) % 4]
                        MS = "mscr%d" % ((2 * j + g) % 4)
                        S.dma("sp", msl, MnT[:], reads=["MnT"], writes=[MS])
                        nkb = imax + 1
                        for hf in range(2):
                            srcm = msl.rearrange("(k two) q -> two k q", two=2)[hf, 0:nkb, :].unsqueeze(0).to_broadcast([64, nkb, 128])
                            S.dma("sp", MT[hf * 64:(hf + 1) * 64, 0:nkb, :], srcm, reads=[MS], writes=[MTK])
                    deferred.append(fin)

            def main_round(j, g):
                nb = (2 * j + g) % 2
                p = j % 2
                imax = 4 * j + 3
                QK = "qnT%d" % p
                gp = slice(g * 64, (g + 1) * 64)
                qg = qnT[p][gp, g * 384:(g + 1) * 384]
                use_sel = j >= 2
                cS = cSs[nb]; CS = "cS%d" % nb
                MT = MTs[nb]; MTK = "MT%d" % nb
                pend = []
                def flush(keep=0):
                    while len(pend) > keep:
                        pend.pop(0)()
                def pv_sw(es, ES, br, kb, first, last):
                    for hh in range(3):
                        mm(S, pO[:, br, hh, :], es[:, hh * 128:(hh + 1) * 128], vs[:, (br * NB + kb) * 2 + g, :],
                           start=(first and hh == 0), stop=last, r=[ES, "vs", "vs1"], w=["pO"], sgc=True)
                steps = [(1, kb) for kb in range(max(0, 4 * j - 4), imax + 1)] + [(0, kb) for kb in range(0, imax + 1)]
                for si, (br, kb) in enumerate(steps):
                    pp, PP = next_ps()
                    mm(S, pp[:, 0:384], ksT[:, br, kb * 128:(kb + 1) * 128], qnT[p][:, g * 384:(g + 1) * 384], start=True, stop=True, r=["ksT", QK], w=[PP])
                    k_ = cnt["ne"] % 4; cnt["ne"] += 1
                    es = eS[k_]; ES = "eS%d" % k_
                    act(S, es[:], pp[:, 0:384], AF.Exp, r=[PP], w=[ES])
                    if br == 0 and use_sel:
                        tt(S, "dve", es[:].rearrange("p (h n) -> p h n", h=3), es[:].rearrange("p (h n) -> p h n", h=3),
                           MT[:, kb, :].unsqueeze(1).to_broadcast([128, 3, 128]), ALU.mult, r=[ES, MTK], w=[ES])
                    if br == 0 and kb >= 4 * j - 1:
                        tt(S, "dve", es[:], es[:], Z[:, 2 * (kb - 4 * j + 1) + g, :], ALU.mult, r=[ES, "Z"], w=[ES])
                    if br == 1:
                        tt(S, "dve", es[:], es[:], Z[:, 2 * (ZS + kb - 4 * j + 4) + g, :], ALU.mult, r=[ES, "Z"], w=[ES])
                    flush(1)
                    pend.append(lambda es=es, ES=ES, br=br, kb=kb, first=(si == 0), last=(kb == imax): pv_sw(es, ES, br, kb, first, last))
                    if si == 1:
                        while early:
                            early.pop(0)()
                    if si == 6:
                        while deferred:
                            deferred.pop(0)()
                flush()
                while early:
                    early.pop(0)()
                while deferred:
                    deferred.pop(0)()
                ts(S, "dve", rcs[:, 3:6], cS[:, :, 64], 1e-30, None, ALU.max, r=[CS], w=["rcs2"])
                S.op("dve", lambda e: e.reciprocal(rcs[:, 3:6], rcs[:, 3:6]), reads=["rcs2"], writes=["rcs2"])
                S.op("dve", lambda e: e.reciprocal(rcs[:, 6:12].rearrange("p (b h) -> p b h", b=2), pO[:, :, :, 64]), reads=["pO"], writes=["rcs2"])
                gv = gat[p][:, g * 9:(g + 1) * 9].rearrange("p (h b) -> p b h", b=3)
                tt(S, "dve", sc[:].rearrange("p (b h) -> p b h", b=3), rcs[:, 3:12].rearrange("p (b h) -> p b h", b=3), gv, ALU.mult,
                   r=["rcs2", "gat%d" % p], w=["sc"])
                o3 = lambda a: a[:].rearrange("p (h d) -> p h d", h=3)
                tt(S, "dve", o3(oa), cS[:, :, 0:64], sc[:, 0:3].unsqueeze(2).to_broadcast([128, 3, 64]), ALU.mult, r=[CS, "sc"], w=["oa"])
                tt(S, "dve", o3(ob), pO[:, 0, :, 0:64], sc[:, 3:6].unsqueeze(2).to_broadcast([128, 3, 64]), ALU.mult, r=["pO", "sc"], w=["ob"])
                tt(S, "pool", oa[:], oa[:], ob[:], ALU.add, r=["oa", "ob"], w=["oa"])
                tt(S, "dve", o3(ob), pO[:, 1, :, 0:64], sc[:, 6:9].unsqueeze(2).to_broadcast([128, 3, 64]), ALU.mult, r=["pO", "sc"], w=["ob"])
                tt(S, "pool", nsab[:, g * 192:(g + 1) * 192], oa[:], ob[:], ALU.add, r=["oa", "ob"], w=["nsab"])

            def finish(j):
                p = j % 2
                mixT = mixTs[p]; MX = "mixT%d" % p
                S.dma("sp", xres[p][:], T["xin"][j], writes=["xres0"])
                for c in range(3):
                    tr(S, pTr[:, c * 128:(c + 1) * 128], nsab[:, c * 128:(c + 1) * 128], ident[:], r=["nsab", "ident"], w=["pTr"])
                cp(S, "act", mixT[:, 3:6, :], pTr[:, 0:384].rearrange("p (c n) -> p c n", c=3), r=["pTr"], w=[MX])
                if "dbg_mix" in T:
                    S.dma("sp", T["dbg_mix"][j], mixT[:], reads=[MX])
                for half in range(2):
                    for c in range(8):
                        mm(S, pR[:, 0:512], mixT[:, c, :], wo[:, c, half * 512:(half + 1) * 512], start=(c == 0), stop=(c == 7),
                           r=[MX, "wo"], w=["pR"])
                    tt(S, "dve", x1o[p][:, half * 512:(half + 1) * 512], pR[:, 0:512], xres[p][:, half * 512:(half + 1) * 512], ALU.add,
                       r=["pR", "xres0"], w=["x1o0"])
                S.dma("sp", T["xout"][j], x1o[p][:], reads=["x1o0"], writes=["xout%d" % j])

            loads(0)
            pre_round(0, 0)
            while deferred:
                deferred.pop(0)()
            for j in range(NQ):
                if j + 1 < NQ:
                    loads(j + 1)
                ret(j)
                pre_round(j, 1)
                main_round(j, 0)
                if j + 1 < NQ:
                    pre_round(j + 1, 0)
                main_round(j, 1)
                early.append(lambda j=j: finish(j))
            while early:
                early.pop(0)()
            S.fence()
            if STOP == "b1":
                return
        emit_B2(nc, S, T, NQ, ident)
        S.fence()


def emit_B2(nc, S, T, NQ, ident):
    NT = (NQ + 3) // 4
    with ExitStack() as st:
        sb = lambda name, shape, dt: st.enter_context(nc.sbuf_tensor("B2_" + T.get("tag", "") + name, shape, dt))
        ps = lambda name, shape, dt: st.enter_context(nc.psum_tensor("B2_" + T.get("tag", "") + name, shape, dt))
        wgu = sb("wgu", [128, 8, 2 * DFF], BF16)
        wd = sb("wd", [128, 22, 1024], BF16)
        gain2 = sb("gain2", [128, 1024], F32)
        x1 = [sb("x1_%d" % i, [128, 1024], F32) for i in range(4)]
        junk = sb("junk", [128, 1024], BF16)
        ss = sb("ss", [128, 4], F32)
        h2 = sb("h2", [128, 1024], BF16)
        h2T = sb("h2T", [128, 8, 512], BF16)
        sgt = [sb("sgt%d" % i, [128, 512], BF16) for i in range(2)]
        aT = sb("aT", [128, 22, 512], BF16)
        xo = [sb("xo%d" % i, [128, 1024], F32) for i in range(2)]
        pX = ps("pX", [128, 1024], F32)
        pT2 = ps("pT2", [128, 1024], BF16)
        pG = [ps("pG%d" % i, [128, 512], F32) for i in range(2)]
        pUp = [ps("pUp%d" % i, [128, 512], F32) for i in range(2)]
        wgv = T["w_gu"].rearrange("(c p) f -> p c f", p=128)
        wdv = T["w_dn"].rearrange("(c p) f -> p c f", p=128)
        for c in range(8):
            S.dma("pool", wgu[:, c, :], wgv[:, c, :], writes=["wgu"])
        for c in range(0, 22, 2):
            S.dma("pool", wd[:, c:c + 2, :], wdv[:, c:c + 2, :], writes=["wd"])
        S.dma("sp", gain2[:], T["fnorm"], writes=["gain2"])
        nx = 0
        for t in range(NT):
            nb = min(4, NQ - 4 * t)
            W = nb * 128
            for jj in range(nb):
                j = 4 * t + jj
                X1 = "x1_%d" % jj
                S.dma("sp", x1[jj][:], T["xout"][j], reads=["xout%d" % j], writes=[X1])
                act(S, junk[:], x1[jj][:], AF.Square, r=[X1], w=["junk2", "fs0"], accum_out=ss[:, 0:1])
                rsq(S, ss[:, 2:3], ss[:, 1:2], ss[:, 0:1], 1.0 / 1024, EPS, r=["fs0"], wt=["fs1"], w=["fs2"])
                stt(S, "dve", h2[:], x1[jj][:], ss[:, 2:3], gain2[:], ALU.mult, ALU.mult, r=[X1, "fs2", "gain2"], w=["h2"])
                for c in range(8):
                    tr(S, pT2[:, c * 128:(c + 1) * 128], h2[:, c * 128:(c + 1) * 128], ident[:], r=["h2", "ident"], w=["pT2"])
                cp(S, "act", h2T[:, :, jj * 128:(jj + 1) * 128], pT2[:].rearrange("p (c n) -> p c n", c=8), r=["pT2"], w=["h2T"])
            for fc in range(22):
                q = fc % 2
                for c in range(8):
                    mm(S, pG[q][:, 0:W], wgu[:, c, fc * 128:(fc + 1) * 128], h2T[:, c, 0:W], start=(c == 0), stop=(c == 7),
                       r=["wgu", "h2T"], w=["pG%d" % q])
                for c in range(8):
                    mm(S, pUp[q][:, 0:W], wgu[:, c, DFF + fc * 128:DFF + (fc + 1) * 128], h2T[:, c, 0:W], start=(c == 0), stop=(c == 7),
                       r=["wgu", "h2T"], w=["pUp%d" % q])
                act(S, sgt[q][:, 0:W], pG[q][:, 0:W], AF.Silu, r=["pG%d" % q], w=["sgt%d" % q])
                tt(S, "dve", aT[:, fc, 0:W], pUp[q][:, 0:W], sgt[q][:, 0:W], ALU.mult, r=["pUp%d" % q, "sgt%d" % q], w=["aT"])
            for jj in range(nb):
                j = 4 * t + jj
                for half in range(2):
                    for fc in range(22):
                        mm(S, pX[:, half * 512:(half + 1) * 512], aT[:, fc, jj * 128:(jj + 1) * 128], wd[:, fc, half * 512:(half + 1) * 512],
                           start=(fc == 0), stop=(fc == 21), r=["aT", "wd"], w=["pX"])
                o = xo[jj % 2]; OK = "xo%d" % (jj % 2)
                tt(S, "dve", o[:], pX[:], x1[jj][:], ALU.add, r=["pX", "x1_%d" % jj], w=[OK])
                S.dma("sp", T["xout"][j], o[:], reads=[OK], writes=["xout%d" % j])


def host_gather_B(Ao, r, NQ):
    NB = 4 * NQ
    CW = 8 * NB + 40
    P = 32 - 8 * r
    ks = np.stack([np.asarray(Ao[rr]["o_ks"]) for rr in range(4)])
    ks2 = ks.transpose(2, 3, 1, 4, 0, 5).reshape(128, 2, NB * 128)
    v = np.stack([np.asarray(Ao[rr]["o_v"]) for rr in range(4)])
    vs2 = np.ones((128, 2 * NB * 2, 65), v.dtype)
    vs2[:, :, 0:64] = v.reshape(4, 2, NQ, 128, 2, 64).transpose(3, 1, 2, 0, 4, 5).reshape(128, 2 * NB * 2, 64)
    U = np.stack([np.asarray(Ao[rr]["o_U"]) for rr in range(4)])
    U2 = U.transpose(1, 0, 2, 3).reshape(NB, 64, 384)
    cA = np.stack([np.asarray(Ao[rr]["o_cA"]) for rr in range(4)])
    cAg = cA.reshape(4, 4, 2, 64, NQ, 8).transpose(3, 1, 2, 4, 0, 5).reshape(64, 8, NB * 8)
    cAsh = np.zeros((64, 8, CW), np.float32)
    cAsh[:, :, P:P + NB * 8] = cAg
    return {"ks2": np.ascontiguousarray(ks2), "vs2": np.ascontiguousarray(vs2), "U2": np.ascontiguousarray(U2), "cAsh": cAsh}


NQ_FULL = 16
RG = [[0, 1, 2, 3], [4, 5, 6, 7]]


def stack_layers(inp, fn):
    L = inp["w_in"].shape[0]
    per = [fn(inp, l) for l in range(L)]
    return {k: np.ascontiguousarray(np.stack([p[k] for p in per])) for k in per[0]}


def build_fused(NQ, L=2):
    NB = 4 * NQ
    NCHF = (32 * (NQ - 1) + 24 + 127) // 128
    nc = bass.Bass("TRN2", target_bir_lowering=False)
    X = {}
    def di(name, shape, dt=F32):
        X[name] = nc.dram_tensor(name, list(shape), dt, kind="ExternalInput").ap()
    def dn(name, shape, dt=F32):
        t = nc.dram_tensor(name, list(shape), dt)
        X[name] = t.ap()
        return t
    di("xin", [NQ, 128, 1024])
    di("w_in", [L, 1024, IN_W]); di("anorm", [L, 128, 1024]); di("gq", [L, 128, 384]); di("kg", [L, 128, 256])
    di("wsT", [L, 128, 4, 128]); di("gbT", [L, 128, 4]); di("w1", [L, 2, 2048, 64])
    di("w_out", [L, 1024, 1024]); di("fnorm", [L, 128, 1024]); di("w_gu", [L, 1024, 2 * DFF]); di("w_dn", [L, DFF, 1024])
    di("pe", [L, 128, 2, 16]); di("w2", [L, 64, 2, 64]); di("kg0", [L, 128, 64])
    di("ropeq", [NQ, 128, 384]); di("ropek", [NQ, 128, 384]); di("cmask", [128, 128]); di("idn", [128, 128])
    di("OVFr", [128, NCHF, 128]); di("OVBr", [16, 264]); di("Dtr", [128, 264])
    di("rdec", [64, 3, 384]); di("selr", [64, 4]); di("pfx", [64, 10, 384]); di("zf0", [128, 1])
    di("zmask", [ZS + ZW, 128, 384]); di("zneg", [ZS + ZW, 128, 384]); di("ztab", [ZS + ZW, 2, 128, 384]); di("zt31", [2, 128, 384])
    di("ctab", [2, 16, 384]); di("cmaskc", [2, 16, 384]); di("cneg", [2, 16, 384])
    X["xout"] = nc.dram_tensor("xout", [NQ, 128, 1024], F32, kind="ExternalOutput").ap()
    dn("xbuf", [NQ, 128, 1024])
    dn("mscr", [4, 128, 128], BF16)
    bt = {}
    JH = (NQ + 1) // 2
    pairs = []
    for b_ in range(2):
        bt["bks%d" % b_] = dn("bks%d" % b_, [128, NQ * 128], BF16); bt["gks%d" % b_] = dn("gks%d" % b_, [4 * 128, NQ * 128], BF16)
        bt["bv%d" % b_] = dn("bv%d" % b_, [NQ * 128, 130], BF16); bt["gv%d" % b_] = dn("gv%d" % b_, [4 * NQ * 128, 130], BF16)
        bt["bU%d" % b_] = dn("bU%d" % b_, [JH * 64, 384]); bt["gU%d" % b_] = dn("gU%d" % b_, [4 * JH * 64, 384])
        pairs += [("bks%d" % b_, "gks%d" % b_), ("bv%d" % b_, "gv%d" % b_), ("bU%d" % b_, "gU%d" % b_)]
    bt["bcA"] = dn("bcA", [512, NQ * 8]); bt["gcA"] = dn("gcA", [4 * 512, NQ * 8])
    pairs.append(("bcA", "gcA"))
    dn("o_rqT", [NQ, 64, 768], BF16); dn("o_inner", [NQ, 128, 384]); dn("o_sg", [NQ, 128, 384], BF16)
    dn("o_qnT", [NQ, 64, 768], BF16); dn("o_gates", [NQ, 128, 18]); dn("o_gmT", [NQ, 128, 256], BF16)
    with ExitStack() as st:
        S = Sched(nc, st)
        for l in range(L):
            TA = dict(X)
            for k in ("w_in", "anorm", "gq", "kg", "wsT", "gbT", "w1"):
                TA[k] = X[k][l]
            TA["xin"] = X["xin"] if l == 0 else X["xbuf"]
            TA["tag"] = "L%d_" % l
            emit_A(nc, S, TA, NQ)
            S.fence()
            for bn, gn in pairs:
                def mk_fn(bn=bn, gn=gn):
                    return lambda e: e.collective_compute("AllGather", ALU.bypass, replica_groups=RG,
                                                          ins=[bt[bn].ap().opt()], outs=[bt[gn].ap().opt()])
                S.coll(mk_fn(), reads=[bn], writes=[gn])
            TB = dict(X)
            for k in ("w_out", "fnorm", "w_gu", "w_dn", "pe", "w2", "kg0"):
                TB[k] = X[k][l]
            TB["w1f"] = X["w1"][l]
            TB["cmask"] = X["cmaskc"]
            TB["tag"] = "L%d_" % l
            TB["xin"] = X["xin"] if l == 0 else X["xbuf"]
            TB["xout"] = X["xbuf"] if l < L - 1 else X["xout"]
            emit_B(nc, S, TB, NQ)
            S.fence()
        S.wait_all("sp")
        S.emit()
    return nc


def make_in_maps(inp, NQ):
    x = inp["x"].astype(np.float32)
    B_, T_, D_ = x.shape
    cores = [(b, r) for b in range(2) for r in range(4)]
    LA = stack_layers(inp, layer_inputs_A)
    LB = stack_layers(inp, layer_inputs_B)
    maps = []
    for (b, r) in cores:
        m = {}
        m["xin"] = np.ascontiguousarray(x[b].reshape(T_ // 128, 128, D_)[r::4][:NQ])
        cA = host_consts_A(b, r, NQ)
        cB = host_consts_Bu(NQ, r)
        btab = bias_tabs_u(inp["rel_bias"].astype(np.float32), r)
        for k in ("w_in", "anorm", "gq", "kg", "wsT", "gbT", "w1"):
            m[k] = LA[k]
        for k in ("w_out", "fnorm", "w_gu", "w_dn", "pe", "w2", "kg0"):
            m[k] = LB[k]
        for k in ("ropeq", "ropek", "cmask", "idn"):
            m[k] = cA[k]
        for k in ("OVFr", "OVBr", "Dtr", "rdec", "selr", "pfx", "zf0", "zmask", "zneg", "cneg"):
            m[k] = cB[k]
        m["cmaskc"] = cB["cmask"]
        for k in ("ztab", "zt31", "ctab"):
            m[k] = btab[k]
        maps.append({k: np.ascontiguousarray(v, dtype=np.float32) for k, v in m.items()})
    return cores, maps


def kernel(**inp):
    inp = {k: np.asarray(v) for k, v in inp.items()}
    NQ = NQ_FULL
    B_, T_, D_ = inp["x"].shape
    nc = build_fused(NQ, L=inp["w_in"].shape[0])
    cores, maps = make_in_maps(inp, NQ)
    res = run_bass_kernel_spmd(nc, maps, core_ids=list(range(8))).results
    out = np.zeros((B_, T_ // 128, 128, D_), np.float32)
    for ci, (b, r) in enumerate(cores):
        out[b, r::4] = np.asarray(res[ci]["xout"], dtype=np.float32)
    return out.reshape(B_, T_, D_)
```

```python
import math
import ml_dtypes
from contextlib import ExitStack
import numpy as np
import concourse.bass as bass
import concourse.mybir as mybir
from concourse.bass_utils import run_bass_kernel_spmd

F32 = mybir.dt.float32
BF16 = mybir.dt.bfloat16
AF = mybir.ActivationFunctionType
ALU = mybir.AluOpType
AX = mybir.AxisListType

N_DMA_SEMS = 24


class Sched:
    ENGS = ("pe", "act", "dve", "pool", "sp")

    def __init__(self, nc, stack):
        self.nc = nc
        self.stack = stack
        self.sem = {e: stack.enter_context(nc.semaphore("s_" + e)) for e in self.ENGS}
        self.dsem = [stack.enter_context(nc.semaphore("d%d" % i)) for i in range(N_DMA_SEMS)]
        self.dcnt = [0] * N_DMA_SEMS
        self.dnext = 0
        self.cnt = {e: 0 for e in self.ENGS}
        self.clock = {e: {} for e in self.ENGS}
        self.streams = {e: [] for e in self.ENGS}
        self.last_w = {}
        self.reads = {}
        self.n_waits = 0

    def _need(self, eng, ev, waits):
        key, val, snap = ev
        ck = self.clock[eng]
        if ck.get(key, 0) >= val:
            return
        if key == eng and eng == "pe":
            return
        waits[key] = max(waits.get(key, 0), val)

    def _apply(self, eng, evs, waits):
        ck = self.clock[eng]
        for key, val, snap in evs:
            if snap:
                for k2, v2 in snap.items():
                    if ck.get(k2, 0) < v2:
                        ck[k2] = v2
        for key, val in waits.items():
            if ck.get(key, 0) < val:
                ck[key] = val

    def _deps(self, eng, reads, writes):
        waits = {}
        evs = []
        for k in reads:
            ev = self.last_w.get(k)
            if ev is not None:
                self._need(eng, ev, waits)
                evs.append(ev)
        for k in writes:
            ev = self.last_w.get(k)
            if ev is not None:
                self._need(eng, ev, waits)
                evs.append(ev)
            for ev in self.reads.get(k, {}).values():
                self._need(eng, ev, waits)
                evs.append(ev)
        self._apply(eng, [e for e in evs if e[0] in waits and waits[e[0]] >= e[1]], waits)
        return waits

    def _commit(self, ev, reads, writes):
        for k in reads:
            self.reads.setdefault(k, {})[ev[0]] = ev
        for k in writes:
            self.last_w[k] = ev
            self.reads[k] = {}

    def op(self, eng, fn, reads=(), writes=()):
        waits = self._deps(eng, reads, writes)
        self.cnt[eng] += 1
        val = self.cnt[eng]
        snap = dict(self.clock[eng])
        ev = (eng, val, snap)
        self.streams[eng].append((waits, fn, ("c", eng, 1)))
        self.n_waits += len(waits)
        self._commit(ev, reads, writes)

    def dma(self, q, out, in_, reads=(), writes=(), **kw):
        if not hasattr(self, "dnq"):
            self.dnq = {"sp": 0, "pool": 0, "act": 0}
        if q == "pool":
            lo, n = 0, 8
        else:
            lo, n = 8, N_DMA_SEMS - 8
        i = lo + self.dnq[q] % n
        self.dnq[q] += 1
        waits = self._deps(q, reads, writes)
        key = ("d", i)
        prev = self.dcnt[i]
        if prev > 0 and self.clock[q].get(key, 0) < prev:
            waits[key] = max(waits.get(key, 0), prev)
            self.clock[q][key] = prev
        self.dcnt[i] += 16
        ev = (key, self.dcnt[i], dict(self.clock[q]))
        fn = lambda e, out=out, in_=in_, kw=kw: e.dma_start(out=out, in_=in_, **kw)
        self.streams[q].append((waits, fn, ("d", i, 16)))
        self._commit(ev, reads, writes)

    def dma_custom(self, q, fn, reads=(), writes=()):
        if not hasattr(self, "dnq"):
            self.dnq = {"sp": 0, "pool": 0, "act": 0}
        if q == "pool":
            lo, n = 0, 8
        else:
            lo, n = 8, N_DMA_SEMS - 8
        i = lo + self.dnq[q] % n
        self.dnq[q] += 1
        waits = self._deps(q, reads, writes)
        key = ("d", i)
        prev = self.dcnt[i]
        if prev > 0 and self.clock[q].get(key, 0) < prev:
            waits[key] = max(waits.get(key, 0), prev)
            self.clock[q][key] = prev
        self.dcnt[i] += 16
        ev = (key, self.dcnt[i], dict(self.clock[q]))
        self.streams[q].append((waits, fn, ("d", i, 16)))
        self._commit(ev, reads, writes)

    def coll(self, fn, reads=(), writes=()):
        if not hasattr(self, "csem"):
            self.csem = self.stack.enter_context(self.nc.semaphore("s_cc"))
            self.ccnt = 0
        waits = self._deps("pool", reads, writes)
        self.ccnt += 1
        ev = ("cc", self.ccnt, dict(self.clock["pool"]))
        self.streams["pool"].append((waits, fn, ("cc", None, 1)))
        self._commit(ev, reads, writes)

    def fence(self):
        tgt = {}
        if getattr(self, "ccnt", 0) > 0:
            tgt["cc"] = self.ccnt
        for e in self.ENGS:
            if self.cnt[e] > 0:
                tgt[e] = self.cnt[e]
        for i in range(N_DMA_SEMS):
            if self.dcnt[i] > 0:
                tgt[("d", i)] = self.dcnt[i]
        for e in self.ENGS:
            waits = {}
            for k, v in tgt.items():
                if k == e:
                    continue
                if self.clock[e].get(k, 0) < v:
                    waits[k] = v
                    self.clock[e][k] = v
            if waits:
                self.streams[e].append((waits, None, None))
        for e in self.ENGS:
            for k, v in tgt.items():
                if k != e and self.clock[e].get(k, 0) < v:
                    self.clock[e][k] = v

    def wait_all(self, eng):
        waits = {}
        for e in self.ENGS:
            if e != eng and self.cnt[e] > self.clock[eng].get(e, 0):
                waits[e] = self.cnt[e]
        for i in range(N_DMA_SEMS):
            if self.dcnt[i] > self.clock[eng].get(("d", i), 0):
                waits[("d", i)] = self.dcnt[i]
        if getattr(self, "ccnt", 0) > self.clock[eng].get("cc", 0):
            waits["cc"] = self.ccnt
        self.streams[eng].append((waits, None, None))

    def _semof(self, key):
        if isinstance(key, tuple):
            return self.dsem[key[1]]
        if key == "cc":
            return self.csem
        return self.sem[key]

    def emit(self):
        nc = self.nc
        with nc.Block() as block:
            def replay(name):
                def run(e):
                    for waits, fn, inc in self.streams[name]:
                        for key, val in waits.items():
                            e.wait_ge(self._semof(key), val)
                        if fn is None:
                            continue
                        ins = fn(e)
                        if inc[0] == "c":
                            ins.then_inc(self.sem[inc[1]], 1)
                        elif inc[0] == "cc":
                            ins.then_inc(self.csem)
                        else:
                            ins.then_inc(self.dsem[inc[1]], 16)
                return run
            block.tensor(replay("pe"))
            block.scalar(replay("act"))
            block.vector(replay("dve"))
            block.gpsimd(replay("pool"))
            block.sync(replay("sp"))


NPBF = ml_dtypes.bfloat16
EPS = 1e-6
D = 1024
IN_W = 3218
DFF = 2816
SLABS = [(0, 384), (384, 768), (768, 1152), (1152, 1536), (1536, 1920), (1920, 2432), (2432, 2706), (2706, 3218)]


def mm(S, out, lhsT, rhs, start=True, stop=True, r=(), w=(), sgc=False):
    if sgc:
        S.op("pe", lambda e: e.matmul(out, lhsT, rhs, start=start, stop=stop, skip_group_check=True), reads=r, writes=w)
    else:
        S.op("pe", lambda e: e.matmul(out, lhsT, rhs, start=start, stop=stop), reads=r, writes=w)

def tr(S, out, in_, ident, r=(), w=()):
    S.op("pe", lambda e: e.transpose(out, in_, ident), reads=r, writes=w)

def act(S, out, in_, func, r=(), w=(), **kw):
    S.op("act", lambda e: e.activation(out, in_, func, **kw), reads=r, writes=w)

def cp(S, eng, out, in_, r=(), w=()):
    if eng == "act":
        S.op("act", lambda e: e.copy(out, in_), reads=r, writes=w)
    else:
        S.op(eng, lambda e: e.tensor_copy(out, in_), reads=r, writes=w)

def tt(S, eng, out, a, b, op, r=(), w=()):
    S.op(eng, lambda e: e.tensor_tensor(out, a, b, op), reads=r, writes=w)

def ts(S, eng, out, a, s1, s2, op0, op1=None, r=(), w=()):
    if op1 is None:
        S.op(eng, lambda e: e.tensor_scalar(out, a, s1, s2, op0), reads=r, writes=w)
    else:
        S.op(eng, lambda e: e.tensor_scalar(out, a, s1, s2, op0, op1), reads=r, writes=w)

def stt(S, eng, out, in0, scalar, in1, op0, op1, r=(), w=()):
    S.op(eng, lambda e: e.scalar_tensor_tensor(out, in0, scalar, in1, op0, op1), reads=r, writes=w)

def red(S, eng, out, in_, r=(), w=()):
    S.op(eng, lambda e: e.tensor_reduce(out, in_, AX.X, ALU.add), reads=r, writes=w)

def rsq(S, out, tmp, in_, scale, bias, r=(), w=(), wt=()):
    S.op("act", lambda e: e.activation(tmp, in_, AF.Sqrt, bias=bias, scale=scale), reads=r, writes=wt)
    S.op("dve", lambda e: e.reciprocal(out, tmp), reads=wt, writes=w)

def mset(S, eng, ap, val, w=()):
    S.op(eng, lambda e: e.memset(ap, val), writes=w)


def host_consts_A(b, r, NQ):
    c = {}
    inv = (10000.0 ** (-np.arange(32, dtype=np.float32) / 32)).astype(np.float32)
    gam = 1.0 - 2.0 ** (-5.0 - np.arange(6, dtype=np.float64))
    n = np.arange(128, dtype=np.float64)
    gq = gam[None, :] ** n[:, None]
    gk = gam[None, :] ** (-n[:, None]) / 8.0
    rq = np.zeros((NQ, 128, 2, 6, 32), np.float32)
    rk = np.zeros((NQ, 128, 2, 6, 32), np.float32)
    for j in range(NQ):
        i = 4 * j + r
        t = (128 * i + np.arange(128)).astype(np.float32)
        ang = t[:, None] * inv[None, :]
        cs = np.stack([np.cos(ang), np.sin(ang)], 1).astype(np.float64)
        rq[j] = (cs[:, :, None, :] * gq[:, None, :, None]).astype(np.float32)
        rk[j] = (cs[:, :, None, :] * gk[:, None, :, None]).astype(np.float32)
    c["ropeq"] = rq.reshape(NQ, 128, 384)
    c["ropek"] = rk.reshape(NQ, 128, 384)
    m = np.arange(128)
    c["cmask"] = (m[:, None] <= m[None, :]).astype(np.float32)
    c["idn"] = np.eye(128, dtype=np.float32)
    return c


def layer_inputs_A(inp, l):
    d = {}
    d["w_in"] = np.ascontiguousarray(inp["w_in"][l])
    d["anorm"] = np.ascontiguousarray(np.broadcast_to(inp["attn_norm"][l][None, :], (128, 1024)))
    d["gq"] = np.ascontiguousarray(np.broadcast_to(np.tile(inp["nsa_q_gain"][l], 6)[None, :], (128, 384)))
    kg = inp["nsa_k_gain"][l]
    d["kg"] = np.ascontiguousarray(np.broadcast_to(np.concatenate([kg[1], kg[1], kg[2], kg[2]])[None, :], (128, 256)))
    d["wsT"] = np.ascontiguousarray(inp["gm_ws"][l].transpose(2, 0, 1))
    d["gbT"] = np.ascontiguousarray(inp["gm_b"][l].T)
    d["w1"] = np.ascontiguousarray(inp["cmp_w1"][l])
    return d


def declare_A(nc, NQ, xin_kind="ExternalInput", out_kind="ExternalOutput"):
    T = {}
    def di(name, shape, dt=F32):
        T[name] = nc.dram_tensor(name, list(shape), dt, kind="ExternalInput").ap()
    def do(name, shape, dt=F32):
        T[name] = nc.dram_tensor(name, list(shape), dt, kind=out_kind).ap()
    T["xin"] = nc.dram_tensor("xin", [NQ, 128, 1024], F32, kind=xin_kind).ap()
    di("w_in", [1024, IN_W]); di("anorm", [128, 1024]); di("gq", [128, 384]); di("kg", [128, 256])
    di("wsT", [128, 4, 128]); di("gbT", [128, 4]); di("w1", [2, 2048, 64])
    di("ropeq", [NQ, 128, 384]); di("ropek", [NQ, 128, 384]); di("cmask", [128, 128]); di("idn", [128, 128])
    do("o_ks", [2, 2, 64, NQ, 128], BF16)
    do("o_v", [2, NQ, 128, 128], BF16)
    do("o_U", [NQ, 64, 384])
    do("o_cA", [4, 2, 64, NQ * 8])
    do("o_rqT", [NQ, 64, 768], BF16)
    do("o_inner", [NQ, 128, 384])
    do("o_sg", [NQ, 128, 384], BF16)
    do("o_qnT", [NQ, 64, 768], BF16)
    do("o_gates", [NQ, 128, 18])
    do("o_gmT", [NQ, 128, 256], BF16)
    return T


def emit_A(nc, S, T, NQ):
    with ExitStack() as st:
        sb = lambda name, shape, dt: st.enter_context(nc.sbuf_tensor("A_" + T.get("tag", "") + name, shape, dt))
        ps = lambda name, shape, dt: st.enter_context(nc.psum_tensor("A_" + T.get("tag", "") + name, shape, dt))
        w_sb = sb("w_sb", [128, 8, IN_W], BF16)
        gain = sb("gain", [128, 1024], F32)
        gq = sb("gq", [128, 384], F32)
        kg = sb("kg", [128, 256], F32)
        wc = sb("wc", [128, 4, 128], BF16)
        wcf = sb("wcf", [128, 4, 128], F32)
        gbT = sb("gbT", [128, 4], F32)
        w1 = sb("w1", [64, 2, 32, 64], BF16)
        cmask = sb("cmask", [128, 128], F32)
        ident = sb("ident", [128, 128], BF16)
        acT = sb("acT", [64, 4, NQ * 128], BF16)
        x_t = [sb("x%d" % p, [128, 1024], F32) for p in range(2)]
        rq_t = [sb("rq%d" % p, [128, 384], F32) for p in range(2)]
        rk_t = [sb("rk%d" % p, [128, 384], F32) for p in range(2)]
        junk = sb("junk", [128, 1024], BF16)
        ss = sb("ss", [128, 40], F32)
        h = sb("h", [128, 1024], BF16)
        hTs = [sb("hT%d" % i, [128, 1024], BF16) for i in range(2)]
        rf = sb("rf", [128, 384], F32)
        t1 = sb("t1", [128, 192], F32); t2 = sb("t2", [128, 192], F32)
        t3 = sb("t3", [128, 192], F32); t4 = sb("t4", [128, 192], F32)
        qp = sb("qp", [128, 384], BF16)
        kp = sb("kp", [128, 384], BF16)
        qpT = sb("qpT", [64, 768], BF16)
        kpT = sb("kpT", [64, 768], BF16)
        v = sb("v", [128, 384], BF16)
        scT = sb("scT", [128, 768], BF16)
        inner = sb("inner", [128, 384], F32)
        U = sb("U", [64, 384], F32)
        sg = sb("sg", [128, 384], BF16)
        sq = sb("sq", [128, 512], F32)
        qn = sb("qn", [128, 384], F32)
        qnb = sb("qnb", [128, 384], BF16)
        qnT = sb("qnT", [64, 768], BF16)
        ac = sb("ac", [128, 256], BF16)
        kns = [sb("kn%d" % i, [128, 128], F32) for i in range(2)]
        knbs = [sb("knb%d" % i, [128, 128], BF16) for i in range(2)]
        ksT = sb("ksT", [64, 256], BF16)
        vsb = sb("vsb", [128, 130], BF16)
        gates = sb("gates", [128, 18], F32)
        zf = sb("zf", [128, 512], F32)
        z2 = sb("z2", [128, 512], F32)
        zg = sb("zg", [128, 512], F32)
        vn = sb("vn", [128, 256], BF16)
        gm1 = sb("gm1", [128, 256], F32)
        gmb = sb("gmb", [128, 256], BF16)
        gmT = sb("gmT", [128, 256], BF16)
        cA = sb("cA", [64, NQ * 8], F32)
        pT = ps("pT", [128, 1024], BF16)
        pP = [ps("pP%d" % p, [128, 512], F32) for p in range(2)]
        pTr = ps("pTr", [128, 1024], BF16)
        pSc = ps("pSc", [128, 1024], F32)
        pIn = ps("pIn", [128, 512], F32)
        pU = ps("pU", [128, 512], F32)

        wv = T["w_in"].rearrange("(c p) f -> p c f", p=128)
        for c in range(8):
            S.dma("pool", w_sb[:, c, :], wv[:, c, :], writes=["w_sb"])
        S.dma("sp", gain[:], T["anorm"], writes=["gain"])
        S.dma("sp", gq[:], T["gq"], writes=["gq"])
        S.dma("sp", kg[:], T["kg"], writes=["kg"])
        S.dma("sp", wcf[:], T["wsT"], writes=["wcf"])
        S.dma("sp", gbT[:], T["gbT"], writes=["gbT"])
        S.dma("sp", cmask[:], T["cmask"], writes=["cmask"])
        S.dma("pool", ident[:], T["idn"], writes=["ident"])
        S.dma("pool", w1[:], T["w1"].rearrange("k (l d) o -> d k l o", d=64), writes=["w1"])
        mset(S, "pool", vsb[:], 1.0, w=["vsb", "vsb1"])
        tt(S, "dve", wc[:], wcf[:], cmask[:].unsqueeze(1).to_broadcast([128, 4, 128]), ALU.mult, r=["wcf", "cmask"], w=["wc"])

        defer = []
        def run_deferred(keep=0):
            while len(defer) > keep:
                defer.pop(0)()
        v3 = lambda a: a[:].rearrange("p (h d) -> p h d", h=6)

        def head(j):
            p = j % 2
            X = "x%d" % p
            S.dma("sp", x_t[p][:], T["xin"][j], writes=[X])
            S.dma("sp", rq_t[p][:], T["ropeq"][j], writes=["rq%d" % p])
            S.dma("sp", rk_t[p][:], T["ropek"][j], writes=["rk%d" % p])
            act(S, junk[:], x_t[p][:], AF.Square, r=[X], w=["junk", "ss0"], accum_out=ss[:, 0:1])
            rsq(S, ss[:, 2:3], ss[:, 1:2], ss[:, 0:1], 1.0 / 1024, EPS, r=["ss0"], wt=["ss1"], w=["ss2"])
            stt(S, "dve", h[:], x_t[p][:], ss[:, 2:3], gain[:], ALU.mult, ALU.mult, r=[X, "ss2", "gain"], w=["h"])

        def head_pe(j):
            p = j % 2
            for c in range(8):
                tr(S, pT[:, c * 128:(c + 1) * 128], h[:, c * 128:(c + 1) * 128], ident[:], r=["h", "ident"], w=["pT"])
            cp(S, "act", hTs[p][:], pT[:], r=["pT"], w=["hT%d" % p])

        head(0)
        head_pe(0)
        for j in range(NQ):
            p = j % 2
            hT = hTs[p]; HT = "hT%d" % p
            for s, (c0, c1) in enumerate(SLABS):
                wd = c1 - c0
                pp = pP[s % 2]
                PP = "pP%d" % (s % 2)
                for c in range(8):
                    mm(S, pp[:, 0:wd], hT[:, c * 128:(c + 1) * 128], w_sb[:, c, c0:c1], start=(c == 0), stop=(c == 7),
                       r=[HT, "w_sb"], w=[PP])
                run_deferred(1)
                if s == 4 and j == (NQ + 1) // 2 and "coll_cb" in T:
                    T["coll_cb"](0)
                if s == 4 and j + 1 < NQ:
                    head(j + 1)
                if s == 6 and j + 1 < NQ:
                    head_pe(j + 1)
                if s in (0, 1):
                    tab = rq_t[p] if s == 0 else rk_t[p]
                    TAB = ("rq%d" if s == 0 else "rk%d") % p
                    dst = qp if s == 0 else kp
                    DST = "qp" if s == 0 else "kp"
                    dT = qpT if s == 0 else kpT
                    DT = "qpT" if s == 0 else "kpT"
                    cp(S, "act", rf[:], pp[:, 0:384], r=[PP], w=["rf"])
                    xv = rf[:].rearrange("p (h t d) -> p h t d", h=6, t=2)
                    tv = tab[:].rearrange("p (t h d) -> p t h d", t=2, h=6)
                    x1, x2 = xv[:, :, 0, :], xv[:, :, 1, :]
                    co, si = tv[:, 0], tv[:, 1]
                    tt(S, "dve", v3(t1), x1, co, ALU.mult, r=["rf", TAB], w=["t1"])
                    tt(S, "pool", v3(t2), x2, si, ALU.mult, r=["rf", TAB], w=["t2"])
                    tt(S, "dve", v3(t3), x2, co, ALU.mult, r=["rf", TAB], w=["t3"])
                    tt(S, "pool", v3(t4), x1, si, ALU.mult, r=["rf", TAB], w=["t4"])
                    dv = dst[:].rearrange("p (h t d) -> p h t d", h=6, t=2)
                    tt(S, "dve", dv[:, :, 0, :], v3(t1), v3(t2), ALU.subtract, r=["t1", "t2"], w=[DST])
                    tt(S, "pool", dv[:, :, 1, :], v3(t3), v3(t4), ALU.add, r=["t3", "t4"], w=[DST])
                    def st2(j=j, s=s, dst=dst, DST=DST, dT=dT, DT=DT):
                        for hh in range(6):
                            tr(S, pTr[0:64, hh * 128:(hh + 1) * 128], dst[:, hh * 64:(hh + 1) * 64], ident[:], r=[DST, "ident"], w=["pTr"])
                        cp(S, "act", dT[:], pTr[0:64, 0:768], r=["pTr"], w=[DT])
                        if s == 0:
                            S.dma("sp", T["o_rqT"][j], qpT[:], reads=["qpT"])
                    defer.append(st2)
                elif s == 2:
                    cp(S, "act", v[:], pp[:, 0:384], r=[PP], w=["v"])
                    def st2(j=j):
                        for hh in range(6):
                            mm(S, pSc[:, hh * 128:(hh + 1) * 128], kpT[:, hh * 128:(hh + 1) * 128], qpT[:, hh * 128:(hh + 1) * 128],
                               r=["kpT", "qpT"], w=["pSc"])
                        tt(S, "dve", scT[:].rearrange("p (h n) -> p h n", h=6), pSc[:, 0:768].rearrange("p (h n) -> p h n", h=6),
                           cmask[:].unsqueeze(1).to_broadcast([128, 6, 128]), ALU.mult, r=["pSc", "cmask"], w=["scT"])
                        for hh in range(6):
                            mm(S, pU[0:64, hh * 64:(hh + 1) * 64], kp[:, hh * 64:(hh + 1) * 64], v[:, hh * 64:(hh + 1) * 64],
                               r=["kp", "v"], w=["pU"])
                        cp(S, "act", U[:], pU[0:64, 0:384], r=["pU"], w=["U"])
                        S.dma("sp", T["bU%d" % (j // ((NQ + 1) // 2))].rearrange("(j d) e -> j d e", d=64)[j % ((NQ + 1) // 2)], U[:], reads=["U"], writes=["bU%d" % (j // ((NQ + 1) // 2))])
                        for hh in range(6):
                            mm(S, pIn[:, hh * 64:(hh + 1) * 64], scT[:, hh * 128:(hh + 1) * 128], v[:, hh * 64:(hh + 1) * 64],
                               r=["scT", "v"], w=["pIn"])
                        cp(S, "act", inner[:], pIn[:, 0:384], r=["pIn"], w=["inner"])
                        S.dma("sp", T["o_inner"][j], inner[:], reads=["inner"])
                    defer.append(st2)
                elif s == 3:
                    act(S, sg[:], pp[:, 0:384], AF.Silu, r=[PP], w=["sg"])
                    S.dma("sp", T["o_sg"][j], sg[:], reads=["sg"])
                elif s == 4:
                    act(S, sq[:, 0:384], pp[:, 0:384], AF.Square, r=[PP], w=["sq"])
                    red(S, "dve", ss[:, 3:9], sq[:, 0:384].rearrange("p (h d) -> p h d", h=6), r=["sq"], w=["sqa"])
                    rsq(S, ss[:, 9:15], ss[:, 33:39], ss[:, 3:9], 1.0, 64 * EPS, r=["sqa"], wt=["sqt"], w=["sqb"])
                    tt(S, "dve", qn[:].rearrange("p (h d) -> p h d", h=6), pp[:, 0:384].rearrange("p (h d) -> p h d", h=6),
                       ss[:, 9:15].unsqueeze(2).to_broadcast([128, 6, 64]), ALU.mult, r=[PP, "sqb"], w=["qn"])
                    tt(S, "pool", qnb[:], qn[:], gq[:], ALU.mult, r=["qn", "gq"], w=["qnb"])
                    def st2(j=j):
                        for hh in range(6):
                            tr(S, pTr[0:64, hh * 128:(hh + 1) * 128], qnb[:, hh * 64:(hh + 1) * 64], ident[:], r=["qnb", "ident"], w=["pTr"])
                        cp(S, "act", qnT[:], pTr[0:64, 0:768], r=["pTr"], w=["qnT"])
                        S.dma("sp", T["o_qnT"][j], qnT[:], reads=["qnT"])
                    defer.append(st2)
                elif s in (5, 6):
                    if s == 5:
                        cp(S, "act", ac[:], pp[:, 0:256], r=[PP], w=["ac"])
                        ko, vo, br = 256, 384, 0
                    else:
                        ko, vo, br = 0, 128, 1
                        act(S, gates[:], pp[:, 256:274], AF.Sigmoid, r=[PP], w=["gates"])
                        S.dma("sp", T["o_gates"][j], gates[:], reads=["gates"])
                    act(S, sq[:, 0:128], pp[:, ko:ko + 128], AF.Square, r=[PP], w=["sq"])
                    red(S, "dve", ss[:, 15:17], sq[:, 0:128].rearrange("p (h d) -> p h d", h=2), r=["sq"], w=["ska"])
                    rsq(S, ss[:, 19:21], ss[:, 17:19], ss[:, 15:17], 1.0 / 64, EPS, r=["ska"], wt=["skb"], w=["skc"])
                    kn = kns[br]; knb = knbs[br]
                    tt(S, "dve", kn[:].rearrange("p (h d) -> p h d", h=2), pp[:, ko:ko + 128].rearrange("p (h d) -> p h d", h=2),
                       ss[:, 19:21].unsqueeze(2).to_broadcast([128, 2, 64]), ALU.mult, r=[PP, "skc"], w=["kn%d" % br])
                    tt(S, "pool", knb[:], kn[:], kg[:, br * 128:(br + 1) * 128], ALU.mult, r=["kn%d" % br, "kg"], w=["knb%d" % br])
                    cp(S, "act", vsb[:].rearrange("p (g e) -> p g e", g=2)[:, :, 0:64], pp[:, vo:vo + 128].rearrange("p (g d) -> p g d", g=2), r=[PP], w=["vsb"])
                    S.dma("sp", T["bv%d_%d" % (br, j // ((NQ + 1) // 2))].rearrange("(j n) c -> j n c", n=128)[j % ((NQ + 1) // 2)], vsb[:], reads=["vsb", "vsb1"],
                          writes=["bv%d_%d" % (br, j // ((NQ + 1) // 2))])
                    def st2(j=j, s=s, br=br, knb=knb):
                        if s == 5:
                            for sl in range(4):
                                tr(S, pTr[0:64, sl * 128:(sl + 1) * 128], ac[:, sl * 64:(sl + 1) * 64], ident[:], r=["ac", "ident"], w=["pTr"])
                            cp(S, "act", acT[:, :, j * 128:(j + 1) * 128], pTr[0:64, 0:512].rearrange("p (s n) -> p s n", s=4), r=["pTr"], w=["acT"])
                        for g in range(2):
                            tr(S, pTr[0:64, g * 128:(g + 1) * 128], knb[:, g * 64:(g + 1) * 64], ident[:], r=["knb%d" % br, "ident"], w=["pTr"])
                        cp(S, "act", ksT[:], pTr[0:64, 0:256], r=["pTr"], w=["ksT"])
                        JH_ = (NQ + 1) // 2
                        bks4 = T["bks%d_%d" % (br, j // JH_)].rearrange("(g d) (j n) -> g d j n", g=2, j=JH_)
                        S.dma("sp", bks4[:, :, j % JH_, :].rearrange("g d n -> d g n"), ksT[:].rearrange("p (g n) -> p g n", g=2), reads=["ksT"],
                              writes=["bks%d_%d" % (br, j // JH_)])
                    defer.append(st2)
                else:
                    cp(S, "act", zf[:], pp[:, 0:512], r=[PP], w=["zf"])
                    tt(S, "pool", z2[:], zf[:], zf[:], ALU.mult, r=["zf"], w=["z2"])
                    ts(S, "dve", z2[:], z2[:], 0.044715, 1.0, ALU.mult, ALU.add, r=["z2"], w=["z2"])
                    tt(S, "pool", z2[:], z2[:], zf[:], ALU.mult, r=["z2", "zf"], w=["z2"])
                    act(S, zg[:], z2[:], AF.Sigmoid, r=["z2"], w=["zg"], scale=1.5957691216057308)
                    tt(S, "dve", zg[:], zg[:], zf[:], ALU.mult, r=["zg", "zf"], w=["zg"])
                    act(S, sq[:, 0:256], zg[:, 256:512], AF.Square, r=["zg"], w=["sq"])
                    red(S, "dve", ss[:, 21:25], sq[:, 0:256].rearrange("p (h d) -> p h d", h=4), r=["sq"], w=["sga"])
                    rsq(S, ss[:, 29:33], ss[:, 25:29], ss[:, 21:25], 1.0 / 64, EPS, r=["sga"], wt=["sgb"], w=["sgc"])
                    tt(S, "dve", vn[:].rearrange("p (h d) -> p h d", h=4), zg[:, 256:512].rearrange("p (h d) -> p h d", h=4),
                       ss[:, 29:33].unsqueeze(2).to_broadcast([128, 4, 64]), ALU.mult, r=["zg", "sgc"], w=["vn"])
                    def st2(j=j):
                        for g in range(4):
                            mm(S, pU[:, g * 64:(g + 1) * 64], wc[:, g, :], vn[:, g * 64:(g + 1) * 64], r=["wc", "vn"], w=["pU"])
                        tt(S, "dve", gm1[:].rearrange("p (h d) -> p h d", h=4), pU[:, 0:256].rearrange("p (h d) -> p h d", h=4),
                           gbT[:].unsqueeze(2).to_broadcast([128, 4, 64]), ALU.add, r=["pU", "gbT"], w=["gm1"])
                        tt(S, "pool", gmb[:], gm1[:], zg[:, 0:256], ALU.mult, r=["gm1", "zg"], w=["gmb"])
                        for c in range(2):
                            tr(S, pTr[:, c * 128:(c + 1) * 128], gmb[:, c * 128:(c + 1) * 128], ident[:], r=["gmb", "ident"], w=["pTr"])
                        cp(S, "act", gmT[:], pTr[:, 0:256], r=["pTr"], w=["gmT"])
                        S.dma("sp", T["o_gmT"][j], gmT[:], reads=["gmT"])
                    defer.append(st2)
        run_deferred()
        NG = NQ * 8
        for sl in range(4):
            kv = sl // 2
            for half in range(2):
                for lp in range(16):
                    mm(S, pU[0:64, 0:NG], w1[:, kv, half * 16 + lp, :], acT[:, sl, lp::16], start=(lp == 0), stop=(lp == 15),
                       r=["w1", "acT"], w=["pU"])
                cp(S, "act", cA[:], pU[0:64, 0:NG], r=["pU"], w=["cA"])
                S.dma("sp", T["bcA"].rearrange("(s h o) c -> s h o c", s=4, h=2)[sl, half], cA[:], reads=["cA"])


import math

def t5_bucket_np(dist):
    n = np.maximum(dist, 0)
    nf = np.maximum(n, 1).astype(np.float32)
    large = 16 + (np.log(nf / np.float32(16)) / np.float32(math.log(8.0)) * np.float32(16)).astype(np.int32)
    large = np.minimum(large, 31)
    return np.where(n < 16, n, large)


def host_consts_B(NQ):
    NB = 4 * NQ
    c = {}
    c["idn"] = np.eye(128, dtype=np.float32)
    E = np.zeros((128, NB, 128), np.float32)
    for kb in range(NB):
        if 2 * kb < 128:
            E[2 * kb, kb, 0:64] = 1
            E[2 * kb + 1, kb, 64:128] = 1
    c["E"] = E
    def ov(cs, ssv):
        return np.clip(np.minimum(cs + 32, ssv + 64) - np.maximum(cs, ssv), 0, None) // 16
    n = np.arange(512)
    j = np.arange(128)
    OV = ov(n[:, None] * 16, j[None, :] * 64).astype(np.float32)
    c["OVF"] = np.ascontiguousarray(OV.reshape(4, 128, 128).transpose(1, 0, 2))
    npr = np.arange(16)
    dl = np.arange(256) - 128
    c["OVB"] = ov((-128 + 16 * npr)[:, None], (64 * dl)[None, :]).astype(np.float32)
    Dt = np.zeros((128, 256), np.float32)
    for ql in range(128):
        for idx in range(256):
            d = idx - 128
            if ql < 64:
                v = 1e9 if d == -1 else 2e9 if d == 0 else -1e9 if d >= 1 else 0.0
            else:
                v = 1e9 if d == 0 else 2e9 if d == 1 else -1e9 if d >= 2 else 0.0
            Dt[ql, idx] = v
    c["Dt"] = Dt
    gam = 1.0 - 2.0 ** (-5.0 - np.arange(6, dtype=np.float64))
    rdec = np.stack([gam ** 128, gam ** 127, gam], 0)
    c["rdec"] = np.ascontiguousarray(np.broadcast_to(rdec[None, :, :, None], (64, 3, 6, 64)).reshape(64, 3, 384)).astype(np.float32)
    kl = np.arange(128)[:, None]; ql = np.arange(128)[None, :]
    m0 = (ql >= kl).astype(np.float32)
    c["mask0"] = np.ascontiguousarray(np.broadcast_to(m0[:, None, :], (128, 3, 128)).reshape(128, 384))
    c["neg0"] = (c["mask0"] - 1.0) * 30000.0
    m4 = (kl > ql).astype(np.float32)
    c["B4"] = np.ascontiguousarray(np.broadcast_to(((m4 - 1.0) * 30000.0)[:, None, :], (128, 3, 128)).reshape(128, 384))
    distc = ql + 97 - 16 * np.arange(16)[:, None]
    mc = (distc >= 0).astype(np.float32)
    mc0 = mc * (np.arange(16)[:, None] >= 8)
    bc3 = lambda a: np.ascontiguousarray(np.broadcast_to(a[:, None, :], (16, 3, 128)).reshape(16, 384))
    c["maskc"] = np.stack([bc3(mc), bc3(mc0)], 0)
    c["negc"] = (c["maskc"] - 1.0) * 30000.0
    return c


def bias_tabs(rel_bias):
    kl = np.arange(128)[:, None]; ql = np.arange(128)[None, :]
    b0 = t5_bucket_np(ql - kl); b1 = t5_bucket_np(128 + ql - kl)
    bcn = t5_bucket_np(ql + 97 - 16 * np.arange(16)[:, None])
    rb = rel_bias.reshape(32, 2, 3)
    d = {}
    d["tab0"] = np.ascontiguousarray(rb[b0].transpose(2, 0, 3, 1).reshape(2, 128, 384))
    d["tab1"] = np.ascontiguousarray(rb[b1].transpose(2, 0, 3, 1).reshape(2, 128, 384))
    d["tabc"] = np.ascontiguousarray(rb[bcn].transpose(2, 0, 3, 1).reshape(2, 16, 384))
    t31 = rb[31]
    d["tab31"] = np.ascontiguousarray(np.broadcast_to(t31[:, None, :, None], (2, 128, 3, 128)).reshape(2, 128, 384))
    return d


def layer_inputs_B(inp, l):
    d = {}
    d["w_out"] = np.ascontiguousarray(inp["w_out"][l])
    d["fnorm"] = np.ascontiguousarray(np.broadcast_to(inp["ffn_norm"][l][None, :], (128, 1024)))
    d["w_gu"] = np.ascontiguousarray(inp["w_gate_up"][l])
    d["w_dn"] = np.ascontiguousarray(inp["w_down"][l])
    d["pe"] = np.ascontiguousarray(inp["cmp_pe"][l].reshape(2, 16, 128).transpose(2, 0, 1))
    d["w1f"] = np.ascontiguousarray(inp["cmp_w1"][l])
    d["w2"] = np.ascontiguousarray(inp["cmp_w2"][l].transpose(1, 0, 2))
    d["kg0"] = np.ascontiguousarray(np.broadcast_to(inp["nsa_k_gain"][l][0][None, :], (128, 64)))
    return d


def declare_B(nc, NQ, in_kind="ExternalInput", out_kind="ExternalOutput", decl_x=True):
    NB = 4 * NQ
    T = {}
    def di(name, shape, dt=F32, kind="ExternalInput"):
        T[name] = nc.dram_tensor(name, list(shape), dt, kind=kind).ap()
    if decl_x:
        di("xin", [NQ, 128, 1024])
    di("g_ks", [4, 2, 2, 64, NQ, 128], BF16, in_kind)
    di("g_v", [4, 2, NQ, 128, 128], BF16, in_kind)
    di("g_U", [4, NQ, 64, 384], F32, in_kind)
    di("g_cA", [4, 4, 2, 64, NQ * 8], F32, in_kind)
    for nm, shp, dt in (("o_rqT", [NQ, 64, 768], BF16), ("o_inner", [NQ, 128, 384], F32), ("o_sg", [NQ, 128, 384], BF16),
                        ("o_qnT", [NQ, 64, 768], BF16), ("o_gates", [NQ, 128, 18], F32), ("o_gmT", [NQ, 128, 256], BF16)):
        di(nm, shp, dt, in_kind)
    di("w_out", [1024, 1024]); di("fnorm", [128, 1024]); di("w_gu", [1024, 2 * DFF]); di("w_dn", [DFF, 1024])
    di("pe", [128, 2, 16]); di("w1f", [2, 2048, 64]); di("w2", [64, 2, 64]); di("kg0", [128, 64])
    di("idn", [128, 128]); di("E", [128, NB, 128]); di("OVF", [128, 4, 128]); di("OVB", [16, 256]); di("Dt", [128, 256])
    di("rdec", [64, 3, 384]); di("mask0", [128, 384]); di("neg0", [128, 384]); di("B4", [128, 384])
    di("maskc", [2, 16, 384]); di("negc", [2, 16, 384])
    di("tab0", [2, 128, 384]); di("tab1", [2, 128, 384]); di("tabc", [2, 16, 384]); di("tab31", [2, 128, 384])
    T["xout"] = nc.dram_tensor("xout", [NQ, 128, 1024], F32, kind=out_kind).ap()
    return T


def gelu_ops(S, out_bf, x, tmp, tmp2, keys):
    kx, kt, kt2, ko = keys
    tt(S, "pool", tmp, x, x, ALU.mult, r=[kx], w=[kt])
    ts(S, "dve", tmp, tmp, 0.044715, 1.0, ALU.mult, ALU.add, r=[kt], w=[kt])
    tt(S, "pool", tmp, tmp, x, ALU.mult, r=[kt, kx], w=[kt])
    act(S, tmp2, tmp, AF.Sigmoid, r=[kt], w=[kt2], scale=1.5957691216057308)
    tt(S, "dve", out_bf, tmp2, x, ALU.mult, r=[kt2, kx], w=[ko])


GATH = ["gks0_0", "gks1_0", "gv0_0", "gv1_0", "gU0", "gks0_1", "gks1_1", "gv0_1", "gv1_1", "gU1", "gcA"]
ZS = 5
ZW = 8


def host_consts_Bu(NQ, r):
    NB = 4 * NQ
    NC = 8 * NB - 1
    CW = 8 * NB + 40
    NCHF = (32 * (NQ - 1) + 24 + 127) // 128
    P = 32 - 8 * r
    c0 = host_consts_B(NQ)
    c = {"idn": c0["idn"], "E": c0["E"], "rdec": c0["rdec"]}
    OVBr = np.zeros((16, 264), np.float32); OVBr[:, 2 * r:2 * r + 256] = c0["OVB"]
    Dtr = np.zeros((128, 264), np.float32); Dtr[:, 2 * r:2 * r + 256] = c0["Dt"]; Dtr[:, 2 * r + 256:] = -1e9
    c["OVBr"] = OVBr; c["Dtr"] = Dtr
    n = np.arange(128 * NCHF) - P
    j = np.arange(128)
    cs = n[:, None] * 16; ssv = j[None, :] * 64
    OV = (np.clip(np.minimum(cs + 32, ssv + 64) - np.maximum(cs, ssv), 0, None) // 16).astype(np.float32)
    OV[(n < 0) | (n >= NC)] = 0
    c["OVFr"] = np.ascontiguousarray(OV.reshape(NCHF, 128, 128).transpose(1, 0, 2))
    sel = np.zeros((64, 4), np.float32); sel[:, r] = 1
    c["selr"] = sel
    gam_ = 1.0 - 2.0 ** (-5.0 - np.arange(6, dtype=np.float64))
    dec_, c127_ = gam_ ** 128, gam_ ** 127
    pf = np.zeros((10, 6), np.float64)
    for k in range(4):
        pf[k] = dec_ ** (3 - k) * c127_
        pf[4 + k] = gam_ * c127_ * dec_ ** (r - 1 - k) if r > k else 0.0
    pf[8] = gam_ * dec_ ** r
    pf[9] = dec_ ** 4
    c["pfx"] = np.ascontiguousarray(np.broadcast_to(pf[None, :, :, None], (64, 10, 6, 64)).reshape(64, 10, 384)).astype(np.float32)
    zf0 = np.zeros((128, 1), np.float32); zf0[:P] = -30000.0
    c["zf0"] = zf0
    zmask = np.zeros((ZS + ZW, 128, 384), np.float32); zneg = np.zeros((ZS + ZW, 128, 384), np.float32)
    kinds = zone_kinds(r)
    for s_, kd in enumerate(kinds):
        if kd == "B0":
            zmask[s_] = c0["mask0"]; zneg[s_] = c0["neg0"]
        elif kd == "B1":
            zmask[s_] = 1.0
        elif kd == "NEG":
            zneg[s_] = -30000.0
        elif kd == "B4":
            zneg[s_] = c0["B4"]
    c["zmask"] = zmask; c["zneg"] = zneg
    cm = np.stack([c0["maskc"][0], c0["maskc"][1] if r == 0 else c0["maskc"][0]], 0)
    c["cmask"] = cm; c["cneg"] = (cm - 1.0) * 30000.0
    return c


def zone_kinds(r):
    kinds = []
    for s_ in range(ZS):
        dl = r + 1 - s_
        kinds.append("NEG" if dl < 0 else "B0" if dl == 0 else "B1" if dl == 1 else "Z")
    for s_ in range(ZW):
        dl = r + 4 - s_
        kinds.append("NEG" if (dl < 0 or dl > 4) else "B0" if dl == 0 else "B1" if dl == 1 else "B4" if dl == 4 else "Z")
    return kinds


def bias_tabs_u(rel_bias, r):
    t = bias_tabs(rel_bias)
    kinds = zone_kinds(r)
    ztab = np.zeros((ZS + ZW, 2, 128, 384), np.float32)
    for s_, kd in enumerate(kinds):
        ztab[s_] = t["tab0"] if kd == "B0" else t["tab1"] if kd == "B1" else t["tab31"]
    return {"ztab": ztab, "zt31": t["tab31"], "ctab": t["tabc"]}


def declare_Bu(nc, NQ, in_kind="ExternalInput", out_kind="ExternalOutput"):
    NB = 4 * NQ
    CW = 8 * NB + 40
    NCHF = (32 * (NQ - 1) + 24 + 127) // 128
    T = {}
    def di(name, shape, dt=F32, kind="ExternalInput"):
        T[name] = nc.dram_tensor(name, list(shape), dt, kind=kind).ap()
    di("xin", [NQ, 128, 1024])
    di("ks2", [128, 2, NB * 128], BF16, in_kind)
    di("vs2", [128, 2 * NB * 2, 65], BF16, in_kind)
    di("U2", [NB, 64, 384], F32, in_kind)
    di("cAsh", [64, 8, CW], F32, in_kind)
    for nm, shp, dt in (("o_rqT", [NQ, 64, 768], BF16), ("o_inner", [NQ, 128, 384], F32), ("o_sg", [NQ, 128, 384], BF16),
                        ("o_qnT", [NQ, 64, 768], BF16), ("o_gates", [NQ, 128, 18], F32), ("o_gmT", [NQ, 128, 256], BF16)):
        di(nm, shp, dt, in_kind)
    di("w_out", [1024, 1024]); di("fnorm", [128, 1024]); di("w_gu", [1024, 2 * DFF]); di("w_dn", [DFF, 1024])
    di("pe", [128, 2, 16]); di("w1f", [2, 2048, 64]); di("w2", [64, 2, 64]); di("kg0", [128, 64])
    di("idn", [128, 128]); di("E", [128, NB, 128]); di("OVFr", [128, NCHF, 128]); di("OVBr", [16, 264]); di("Dtr", [128, 264])
    di("rdec", [64, 3, 384]); di("selr", [64, 4]); di("pfx", [64, 10, 384]); di("zf0", [128, 1])
    di("zmask", [ZS + ZW, 128, 384]); di("zneg", [ZS + ZW, 128, 384]); di("ztab", [ZS + ZW, 2, 128, 384]); di("zt31", [2, 128, 384])
    di("ctab", [2, 16, 384]); di("cmask", [2, 16, 384]); di("cneg", [2, 16, 384])
    T["xout"] = nc.dram_tensor("xout", [NQ, 128, 1024], F32, kind=out_kind).ap()
    return T


def emit_B(nc, S, T, NQ):
    NB = 4 * NQ
    CW = 8 * NB + 40
    NCHF = (32 * (NQ - 1) + 24 + 127) // 128
    NCHK = (CW + 127) // 128
    NZ = ZS + ZW
    with ExitStack() as st0:
        sb0 = lambda name, shape, dt: st0.enter_context(nc.sbuf_tensor("B_" + T.get("tag", "") + name, shape, dt))
        ident = sb0("ident", [128, 128], BF16)
        S.dma("pool", ident[:], T["idn"], writes=["ident"])
        with ExitStack() as st:
            sb = lambda name, shape, dt: st.enter_context(nc.sbuf_tensor("B1_" + T.get("tag", "") + name, shape, dt))
            ps = lambda name, shape, dt: st.enter_context(nc.psum_tensor("B1_" + T.get("tag", "") + name, shape, dt))
            ksT = sb("ksT", [128, 2, NB * 128], BF16)
            vs = sb("vs", [128, 2 * NB * 2, 65], BF16)
            kcT = sb("kcT", [128, CW], BF16)
            gT = sb("gT", [64, 4, CW], BF16)
            CV = sb("CV", [128, 2 * NCHF, 193], BF16)
            Gs = sb("Gs", [64, NQ, 384], BF16)
            OVB = sb("OVB", [16, 264], BF16)
            Dt = sb("Dt", [128, 264], F32)
            selr = sb("selr", [64, 4], F32)
            Z = sb("Z", [128, NZ * 2, 384], BF16)
            zf0 = sb("zf0", [128, 1], F32)
            Bc = sb("Bc", [16, 4, 384], BF16)
            w2 = sb("w2", [64, 2, 64], BF16)
            kg0 = sb("kg0", [128, 64], F32)
            pS = [ps("pS%d" % i, [128, 512], F32) for i in range(3)]
            pC = ps("pC", [128, 4, 256], F32)
            pO = ps("pO", [128, 2, 3, 65], F32)
            pR = ps("pR", [128, 512], F32)
            pTr = ps("pTr", [128, 1024], BF16)
            pM = pS[2]

            S.dma("pool", OVB[:], T["OVBr"], writes=["OVB"])
            S.dma("pool", CV[:, 0:NCHF, 65:193], T["OVFr"], writes=["CVo"])
            S.dma("pool", CV[:, NCHF:2 * NCHF, 65:193], T["OVFr"], writes=["CVo"])
            mset(S, "pool", CV[:, :, 64:65], 1.0, w=["CV1"])
            S.dma("sp", Dt[:], T["Dtr"], writes=["Dt"])
            S.dma("sp", selr[:], T["selr"], writes=["selr"])
            S.dma("sp", zf0[:], T["zf0"], writes=["zf0"])
            S.dma("pool", w2[:], T["w2"], writes=["w2"])
            S.dma("sp", kg0[:], T["kg0"], writes=["kg0"])
            import os as _os
            STOP = _os.environ.get("STOPB", "")
            if STOP == "loads":
                S.fence(); return
            stm = ExitStack()
            stm.__enter__()
            if True:
                sbb = lambda name, shape, dt: stm.enter_context(nc.sbuf_tensor("Bb_" + T.get("tag", "") + name, shape, dt))
                zt = [sbb("zt%d" % i, [128, 2, 384], F32) for i in range(1)]
                zm = [sbb("zm%d" % i, [128, 2, 384], F32) for i in range(1)]
                t31 = sbb("t31", [128, 2, 384], F32)
                tc_ = sbb("tc", [16, 2, 384], F32)
                mc = sbb("mc", [16, 4, 384], F32)
                Bcf = sbb("Bcf", [16, 4, 384], F32)
                S.dma("sp", t31[:], T["zt31"].rearrange("g p f -> p g f"), writes=["t31"])
                for s_ in range(NZ):
                    q = 0
                    S.dma("sp", zt[q][:], T["ztab"][s_].rearrange("g p f -> p g f"), writes=["zt%d" % q])
                    S.dma("sp", zm[q][:, 0, :], T["zmask"][s_], writes=["zm%d" % q])
                    S.dma("sp", zm[q][:, 1, :], T["zneg"][s_], writes=["zm%d" % q])
                    tt(S, "dve", zt[q][:], zt[q][:], t31[:], ALU.subtract, r=["zt%d" % q, "t31"], w=["zt%d" % q])
                    tt(S, "dve", zt[q][:], zt[q][:], zm[q][:, 0:1, :].to_broadcast([128, 2, 384]), ALU.mult, r=["zt%d" % q, "zm%d" % q], w=["zt%d" % q])
                    tt(S, "dve", zt[q][:], zt[q][:], zm[q][:, 1:2, :].to_broadcast([128, 2, 384]), ALU.add,
                       r=["zt%d" % q, "zm%d" % q], w=["zt%d" % q])
                    act(S, Z[:, 2 * s_:2 * s_ + 2, :], zt[q][:], AF.Exp, r=["zt%d" % q], w=["Z"])
                S.dma("sp", tc_[:], T["ctab"].rearrange("g p f -> p g f"), writes=["tc"])
                S.dma("sp", mc[:, 0:2, :], T["cmask"].rearrange("v p f -> p v f"), writes=["mc"])
                S.dma("sp", mc[:, 2:4, :], T["cneg"].rearrange("v p f -> p v f"), writes=["mc"])
                tt(S, "dve", tc_[:], tc_[:], t31[0:16, :, :], ALU.subtract, r=["tc", "t31"], w=["tc"])
                for var in range(2):
                    tt(S, "dve", Bcf[:, 2 * var:2 * var + 2, :], tc_[:], mc[:, var:var + 1, :].to_broadcast([16, 2, 384]), ALU.mult, r=["tc", "mc"], w=["Bcf"])
                    tt(S, "dve", Bcf[:, 2 * var:2 * var + 2, :], Bcf[:, 2 * var:2 * var + 2, :], mc[:, 2 + var:3 + var, :].to_broadcast([16, 2, 384]), ALU.add,
                       r=["Bcf", "mc"], w=["Bcf"])
                    act(S, Bc[:, 2 * var:2 * var + 2, :], Bcf[:, 2 * var:2 * var + 2, :], AF.Exp, r=["Bcf"], w=["Bc"])
            if True:
                sbc = lambda name, shape, dt: stm.enter_context(nc.sbuf_tensor("Bc_" + T.get("tag", "") + name, shape, dt))
                cAs = sbc("cAs", [64, 8, CW + 24], F32)
                w1c = sbc("w1c", [128, 2, 16, 64], BF16)
                pec = sbc("pec", [128, 2, 16], BF16)
                cvec = sbc("cvec", [64, 2], F32)
                kcb = sbc("kcb", [128, 128], BF16)
                sqc = sbc("sqc", [128, 64], F32)
                ssc = sbc("ssc", [128, 4], F32)
                cAu = sbc("cAu", [64, 8, 4, NQ * 8], F32)
                gcA = T["gcA"].rearrange("(q s o) c -> q o s c", q=4, s=8)
                for rr in range(4):
                    S.dma("sp", cAu[:, :, rr, :], gcA[rr], reads=GATH, writes=["cAu"])
                mset(S, "pool", cAs[:], 0.0, w=["cAs"])
                for rp in range(4):
                    for rr in range(4):
                        off = 32 - 8 * rp + 8 * rr
                        for s8 in range(8):
                            dstc = cAs[:, s8, off:off + 32 * NQ].rearrange("p (j x) -> p j x", x=32)[:, :, 0:8]
                            srcc = cAu[:, s8, rr, :].rearrange("p (j g) -> p j g", g=8)
                            stt(S, "dve", dstc, srcc, selr[:, rp:rp + 1], dstc, ALU.mult, ALU.add, r=["cAu", "selr", "cAs"], w=["cAs"])
                S.dma("pool", w1c[:], T["w1f"].rearrange("k (c p) o -> p k c o", p=128), writes=["w1c"])
                S.dma("pool", pec[:], T["pe"], writes=["pec"])
                for kv in range(2):
                    for c in range(16):
                        mm(S, pM[0:64, kv:kv + 1], w1c[:, kv, c, :], pec[:, kv, c:c + 1], start=(c == 0), stop=(c == 15),
                           r=["w1c", "pec"], w=["pS2"])
                cp(S, "act", cvec[:], pM[0:64, 0:2], r=["pS2"], w=["cvec"])
                pre = cAu[:].rearrange("p s q c -> p (s q c)")[:, 0:4 * CW].rearrange("p (s c) -> p s c", s=4)
                mset(S, "pool", pre[:], 0.0, w=["pre", "cAu"])
                cv4 = cAs[:, :, 0:CW].rearrange("p (s h) c -> p s h c", h=2)
                tt(S, "dve", pre[:, :, 0:CW - 1], cv4[:, :, 0, 0:CW - 1], cv4[:, :, 1, 1:CW], ALU.add, r=["cAs", "pre", "cAu"], w=["pre", "cAu"])
                for sl in range(4):
                    ts(S, "dve", pre[:, sl, :], pre[:, sl, :], cvec[:, sl // 2:sl // 2 + 1], None, ALU.add, r=["pre", "cvec"], w=["pre"])
                gelu_ops(S, gT[:], pre[:], cAs[:, 0:4, 0:CW], cAs[:, 4:8, 0:CW], ("pre", "cAs", "cAs", "gT"))
                for c in range(NCHK):
                    rows = min(128, CW - 128 * c)
                    for g in range(2):
                        mm(S, pM[0:rows, 64:128], gT[:, g, 128 * c:128 * c + rows], w2[:, 0, :], r=["gT", "w2"], w=["pS2"])
                        act(S, sqc[0:rows, :], pM[0:rows, 64:128], AF.Square, r=["pS2"], w=["sqc", "ssc0"], accum_out=ssc[0:rows, 0:1])
                        rsq(S, ssc[0:rows, 2:3], ssc[0:rows, 1:2], ssc[0:rows, 0:1], 1.0 / 64, EPS, r=["ssc0"], wt=["ssc1"], w=["ssc2"])
                        stt(S, "dve", kcb[0:rows, g * 64:(g + 1) * 64], pM[0:rows, 64:128], ssc[0:rows, 2:3], kg0[0:rows, :], ALU.mult, ALU.mult,
                            r=["pS2", "ssc2", "kg0"], w=["kcb"])
                        if c < NCHF:
                            mm(S, pM[0:rows, 128:192], gT[:, 2 + g, 128 * c:128 * c + rows], w2[:, 1, :], r=["gT", "w2"], w=["pS2"])
                            cp(S, "act", CV[0:rows, g * NCHF + c, 0:64], pM[0:rows, 128:192], r=["pS2"], w=["CVv"])
                    tr(S, pTr[:, 0:rows], kcb[0:rows, :], ident[0:rows, 0:rows], r=["kcb", "ident"], w=["pTr"])
                    cp(S, "act", kcT[:, 128 * c:128 * c + rows], pTr[:, 0:rows], r=["pTr"], w=["kcT"])
            if True:
                sbp = lambda name, shape, dt: stm.enter_context(nc.sbuf_tensor("Bp_" + T.get("tag", "") + name, shape, dt))
                Rst = sbp("Rst", [64, 384], F32)
                pfx = sbp("pfx", [64, 10, 384], F32)
                if NQ >= 12:
                    cAu_f = cAu[:].rearrange("p s q c -> p (s q c)")
                    cAs_f = cAs[:].rearrange("p s c -> p (s c)")
                    Ut = [cAu_f[:, i * 1536:(i + 1) * 1536].rearrange("p (k e) -> p k e", k=4) for i in range(2)]
                    tW = cAs_f[:, 0:1536].rearrange("p (k e) -> p k e", k=4)
                    tQ = cAs_f[:, 1536:3072].rearrange("p (k e) -> p k e", k=4)
                else:
                    Ut = [sbp("Ut%d" % i, [64, 4, 384], F32)[:] for i in range(2)]
                    tW = sbp("tW", [64, 4, 384], F32)[:]
                    tQ = sbp("tQ", [64, 4, 384], F32)[:]
                Wt = sbp("Wt", [64, 384], F32)
                Qt = sbp("Qt", [64, 384], F32)
                tG = sbp("tG", [64, 384], F32)
                S.dma("sp", pfx[:], T["pfx"], writes=["pfx"])
                mset(S, "pool", Rst[:], 0.0, w=["Rst"])
                for jj in range(NQ):
                    u = jj % 2
                    UT = "Ut%d" % u
                    extra_w = ["cAu", "pre"] if jj < 2 else []
                    S.dma("sp", Ut[u], T["gU%d" % (jj // ((NQ + 1) // 2))].rearrange("(q j d) e -> d q j e", q=4, d=64)[:, :, jj % ((NQ + 1) // 2), :], reads=GATH, writes=[UT] + extra_w)
                    tt(S, "pool", tQ, Ut[u], pfx[:, 4:8, :], ALU.mult, r=[UT, "pfx"], w=["tQ"] + (["cAs"] if jj == 0 else []))
                    red(S, "dve", Qt[:], tQ.rearrange("p k e -> p e k"), r=["tQ"], w=["Qt"])
                    tt(S, "dve", tG[:], Rst[:], pfx[:, 8, :], ALU.mult, r=["Rst", "pfx"], w=["tG"])
                    tt(S, "pool", Gs[:, jj, :], tG[:], Qt[:], ALU.add, r=["tG", "Qt"], w=["Gs%d" % jj])
                    if jj == NQ - 1:
                        break
                    tt(S, "pool", tW, Ut[u], pfx[:, 0:4, :], ALU.mult, r=[UT, "pfx"], w=["tW"] + (["cAs"] if jj == 0 else []))
                    red(S, "dve", Wt[:], tW.rearrange("p k e -> p e k"), r=["tW"], w=["Wt"])
                    tt(S, "dve", Rst[:], Rst[:], pfx[:, 9, :], ALU.mult, r=["Rst", "pfx"], w=["Rst"])
                    tt(S, "dve", Rst[:], Rst[:], Wt[:], ALU.add, r=["Rst", "Wt"], w=["Rst"])
            JH_ = (NQ + 1) // 2
            for hf in range(2):
                nj = JH_ if hf == 0 else NQ - JH_
                if nj <= 0:
                    continue
                for rr in range(4):
                    for br in range(2):
                        gks_ = T["gks%d_%d" % (br, hf)].rearrange("(q r) c -> q r c", q=4)
                        gv_ = T["gv%d_%d" % (br, hf)].rearrange("(q j n) c -> q j n c", q=4, n=128)
                        for g in range(2):
                            dst = ksT[g * 64:(g + 1) * 64, br, :].rearrange("p (j q n) -> p j q n", q=4, n=128)[:, hf * JH_:hf * JH_ + nj, rr, :]
                            src = gks_[rr, g * 64:(g + 1) * 64, 0:nj * 128].rearrange("d (j n) -> d j n", n=128)
                            S.dma("sp", dst, src, reads=GATH, writes=["ksT"])
                        dstv = vs[:, br * NB * 2:(br + 1) * NB * 2, :].rearrange("p (j q g) e -> p j q (g e)", q=4, g=2)[:, hf * JH_:hf * JH_ + nj, rr, :]
                        S.dma("sp", dstv, gv_[rr, 0:nj].rearrange("j n c -> n j c"), reads=GATH, writes=["vs", "vs1"])
            S.fence()
            stm.__exit__(None, None, None)

            mixTs = [sb("mixT%d" % i, [128, 8, 128], BF16) for i in range(2)]
            wo = sb("wo", [128, 8, 1024], BF16)
            S.dma("pool", wo[:], T["w_out"].rearrange("(c p) f -> p c f", p=128), writes=["wo"])
            xres = [sb("xres%d" % i, [128, 1024], F32) for i in range(1)] * 2
            x1o = [sb("x1o%d" % i, [128, 1024], F32) for i in range(1)] * 2
            rqT = [sb("rqT%d" % p, [64, 768], BF16) for p in range(2)]
            inn = [sb("inn%d" % p, [128, 384], F32) for p in range(2)]
            sgt = [sb("sgt%d" % p, [128, 384], BF16) for p in range(2)]
            qnT = [sb("qnT%d" % p, [128, 768], BF16) for p in range(2)]
            for p_ in range(2):
                mset(S, "pool", qnT[p_][:], 0.0, w=["qnT%d" % p_])
            gat = [sb("gat%d" % p, [128, 18], F32) for p in range(2)]
            ro = sb("ro", [128, 384], F32)
            rsqv = sb("rsqv", [128, 384], F32)
            rss = sb("rss", [128, 24], F32)
            ron = sb("ron", [128, 384], F32)
            rob = sb("rob", [128, 384], BF16)
            eS = [sb("eS%d" % i, [128, 384], BF16) for i in range(4)]
            eC = [sb("eC%d" % i, [128, 384], BF16) for i in range(2)]
            eN = sb("eN", [16, 384], BF16)
            imp = sb("imp", [128, 128], F32)
            imw = sb("imw", [128, 128], F32)
            top = sb("top", [128, 16], F32)
            rcs = sb("rcs", [128, 12], F32)
            Mn = sb("Mn", [128, 128], BF16)
            MnT = sb("MnT", [128, 128], BF16)
            sc = sb("sc", [128, 9], F32)
            oa = sb("oa", [128, 192], F32)
            ob = sb("ob", [128, 192], F32)
            nsab = sb("nsab", [128, 384], BF16)
            MTs = [sb("MT%d" % i, [128, NB, 128], BF16) for i in range(2)]
            cSs = [sb("cS%d" % i, [128, 3, 193], F32) for i in range(2)]
            vnrs = [sb("vnr%d" % i, [16, 65], BF16) for i in range(2)]
            for i_ in range(2):
                mset(S, "pool", vnrs[i_][:, 64:65], 1.0, w=["vnr%d" % i_])
            cnt = {"ne": 0, "nce": 0, "nps": 0}
            deferred = []
            early = []

            def next_ps():
                k = cnt["nps"] % 3; cnt["nps"] += 1
                return pS[k], "pS%d" % k

            def loads(j):
                p = j % 2
                S.dma("sp", rqT[p][:], T["o_rqT"][j], writes=["rqT%d" % p])
                S.dma("sp", inn[p][:], T["o_inner"][j], writes=["inn%d" % p])
                S.dma("sp", sgt[p][:], T["o_sg"][j], writes=["sgt%d" % p])
                S.dma("sp", qnT[p][0:64, 0:384], T["o_qnT"][j][:, 0:384], writes=["qnT%d" % p])
                S.dma("sp", qnT[p][64:128, 384:768], T["o_qnT"][j][:, 384:768], writes=["qnT%d" % p])
                S.dma("sp", gat[p][:], T["o_gates"][j], writes=["gat%d" % p])

            def ret(j):
                p = j % 2
                mixT = mixTs[p]; MX = "mixT%d" % p
                for hh in range(6):
                    mm(S, pR[:, hh * 64:(hh + 1) * 64], rqT[p][:, hh * 128:(hh + 1) * 128], Gs[:, j, hh * 64:(hh + 1) * 64],
                       r=["rqT%d" % p, "Gs%d" % j], w=["pR"])
                tt(S, "dve", ro[:], pR[:, 0:384], inn[p][:], ALU.add, r=["pR", "inn%d" % p], w=["ro"])
                act(S, rsqv[:], ro[:], AF.Square, r=["ro"], w=["rsqv"])
                red(S, "dve", rss[:, 0:6], rsqv[:].rearrange("p (h d) -> p h d", h=6), r=["rsqv"], w=["rss0"])
                rsq(S, rss[:, 12:18], rss[:, 6:12], rss[:, 0:6], 1.0 / 64, EPS, r=["rss0"], wt=["rss1"], w=["rss2"])
                tt(S, "dve", ron[:].rearrange("p (h d) -> p h d", h=6), ro[:].rearrange("p (h d) -> p h d", h=6),
                   rss[:, 12:18].unsqueeze(2).to_broadcast([128, 6, 64]), ALU.mult, r=["ro", "rss2"], w=["ron"])
                tt(S, "pool", rob[:], ron[:], sgt[p][:], ALU.mult, r=["ron", "sgt%d" % p], w=["rob"])
                def ret2(mixT=mixT, MX=MX, j=j):
                    S.dma("sp", mixT[:, 6:8, :], T["o_gmT"][j].rearrange("p (c n) -> p c n", c=2), writes=[MX])
                    for c in range(3):
                        tr(S, pTr[:, c * 128:(c + 1) * 128], rob[:, c * 128:(c + 1) * 128], ident[:], r=["rob", "ident"], w=["pTr"])
                    cp(S, "act", mixT[:, 0:3, :], pTr[:, 0:384].rearrange("p (c n) -> p c n", c=3), r=["pTr"], w=[MX])
                early.append(ret2)

            def pre_round(j, g):
                nb = (2 * j + g) % 2
                p = j % 2
                imax = 4 * j + 3
                QK = "qnT%d" % p
                gp = slice(g * 64, (g + 1) * 64)
                qg = qnT[p][gp, g * 384:(g + 1) * 384]
                nf = 32 * j + 24
                nfc = (nf + 127) // 128
                var = 1 if j == 0 else 0
                vnr = vnrs[nb]; VN = "vnr%d" % nb
                cS = cSs[nb]; CS = "cS%d" % nb
                MT = MTs[nb]; MTK = "MT%d" % nb
                mm(S, pR[0:16, 448:512], gT[:, 2 + g, nf:nf + 16], w2[:, 1, :], r=["gT", "w2"], w=["pR"])
                cp(S, "act", vnr[:, 0:64], pR[0:16, 448:512], r=["pR"], w=[VN])
                pend = []
                def flush(keep=0):
                    while len(pend) > keep:
                        pend.pop(0)()
                def pv_far(ec, EC, rows, c):
                    for hh in range(3):
                        mm(S, pC[:, hh, 0:193], ec[0:rows, hh * 128:(hh + 1) * 128], CV[0:rows, g * NCHF + c, :], start=(c == 0 and hh != 1), stop=False,
                           r=[EC, "CVv", "CVo", "CV1"], w=["pC"], sgc=True)
                def pv_near():
                    for hh in range(3):
                        mm(S, pC[:, hh, 0:65], eN[:, hh * 128:(hh + 1) * 128], vnr[:, :], start=False, stop=True,
                           r=["eN", VN], w=["pC"], sgc=True)
                        mm(S, pC[:, hh, 65:193], eN[:, hh * 128:(hh + 1) * 128], OVB[:, 128 - 8 * j:256 - 8 * j], start=False, stop=True,
                           r=["eN", "OVB"], w=["pC"], sgc=True)
                for c in range(nfc):
                    rows = min(128, nf - 128 * c)
                    pp, PP = next_ps()
                    mm(S, pp[0:rows, 0:384], kcT[gp, 128 * c:128 * c + rows], qg, start=True, stop=True, r=["kcT", QK], w=[PP])
                    k_ = cnt["nce"] % 2; cnt["nce"] += 1
                    ec = eC[k_]; EC = "eC%d" % k_
                    if c == 0:
                        act(S, ec[0:rows, :], pp[0:rows, 0:384], AF.Exp, r=[PP, "zf0"], w=[EC], bias=zf0[0:rows, 0:1])
                    else:
                        act(S, ec[0:rows, :], pp[0:rows, 0:384], AF.Exp, r=[PP], w=[EC])
                    flush()
                    pend.append(lambda ec=ec, EC=EC, rows=rows, c=c: pv_far(ec, EC, rows, c))
                pp, PP = next_ps()
                mm(S, pp[0:16, 0:384], kcT[gp, nf:nf + 16], qg, start=True, stop=True, r=["kcT", QK], w=[PP])
                act(S, eN[:], pp[0:16, 0:384], AF.Exp, r=[PP], w=["eN"])
                tt(S, "dve", eN[:], eN[:], Bc[:, 2 * var + g, :], ALU.mult, r=["eN", "Bc"], w=["eN"])
                flush()
                pv_near()
                cp(S, "act", cS[:], pC[:, 0:3, 0:193], r=["pC"], w=[CS])
                if j >= 2:
                    ts(S, "dve", rcs[:, 0:3], cS[:, :, 64], 1e-30, None, ALU.max, r=[CS], w=["rcs"])
                    S.op("dve", lambda e: e.reciprocal(rcs[:, 0:3], rcs[:, 0:3]), reads=["rcs"], writes=["rcs"])
                    ts(S, "dve", imp[:], cS[:, 0, 65:193], rcs[:, 0:1], None, ALU.mult, r=[CS, "rcs"], w=["imp"])
                    stt(S, "dve", imp[:], cS[:, 1, 65:193], rcs[:, 1:2], imp[:], ALU.mult, ALU.add, r=[CS, "rcs", "imp"], w=["imp"])
                    stt(S, "dve", imp[:], cS[:, 2, 65:193], rcs[:, 2:3], imp[:], ALU.mult, ALU.add, r=[CS, "rcs", "imp"], w=["imp"])
                    tt(S, "dve", imp[:], imp[:], Dt[:, 128 - 8 * j:256 - 8 * j], ALU.add, r=["imp", "Dt"], w=["imp"])
                    mset(S, "dve", imp[:, 0:1], 3e9, w=["imp"])
                    S.op("dve", lambda e: e.max(out=top[:, 0:8], in_=imp[:]), reads=["imp"], writes=["top"])
                    S.op("dve", lambda e: e.match_replace(out=imw[:], in_to_replace=top[:, 0:8], in_values=imp[:], imm_value=-3e9),
                         reads=["imp", "top"], writes=["imw"])
                    S.op("dve", lambda e: e.max(out=top[:, 8:16], in_=imw[:]), reads=["imw"], writes=["top"])
                    ts(S, "dve", Mn[:], imp[:], top[:, 15:16], None, ALU.is_ge, r=["imp", "top"], w=["Mn"])
                    def fin():
                        tr(S, pTr[:, 512:640], Mn[:], ident[:], r=["Mn", "ident"], w=["pTr"])
                        cp(S, "dve", MnT[:], pTr[:, 512:640], r=["pTr"], w=["MnT"])
                        msl = T["mscr"][(2 * j + g) % 4]
                        MS = "mscr%d" % ((2 * j + g) % 4)
                        S.dma("sp", msl, MnT[:], reads=["MnT"], writes=[MS])
                        nkb = imax + 1
                        for hf in range(2):
                            srcm = msl.rearrange("(k two) q -> two k q", two=2)[hf, 0:nkb, :].unsqueeze(0).to_broadcast([64, nkb, 128])
                            S.dma("sp", MT[hf * 64:(hf + 1) * 64, 0:nkb, :], srcm, reads=[MS], writes=[MTK])
                    deferred.append(fin)

            def main_round(j, g):
                nb = (2 * j + g) % 2
                p = j % 2
                imax = 4 * j + 3
                QK = "qnT%d" % p
                gp = slice(g * 64, (g + 1) * 64)
                qg = qnT[p][gp, g * 384:(g + 1) * 384]
                use_sel = j >= 2
                cS = cSs[nb]; CS = "cS%d" % nb
                MT = MTs[nb]; MTK = "MT%d" % nb
                pend = []
                def flush(keep=0):
                    while len(pend) > keep:
                        pend.pop(0)()
                def pv_sw(es, ES, br, kb, first, last):
                    for hh in range(3):
                        mm(S, pO[:, br, hh, :], es[:, hh * 128:(hh + 1) * 128], vs[:, (br * NB + kb) * 2 + g, :],
                           start=(first and hh == 0), stop=last, r=[ES, "vs", "vs1"], w=["pO"], sgc=True)
                steps = [(1, kb) for kb in range(max(0, 4 * j - 4), imax + 1)] + [(0, kb) for kb in range(0, imax + 1)]
                for si, (br, kb) in enumerate(steps):
                    pp, PP = next_ps()
                    mm(S, pp[:, 0:384], ksT[:, br, kb * 128:(kb + 1) * 128], qnT[p][:, g * 384:(g + 1) * 384], start=True, stop=True, r=["ksT", QK], w=[PP])
                    k_ = cnt["ne"] % 4; cnt["ne"] += 1
                    es = eS[k_]; ES = "eS%d" % k_
                    act(S, es[:], pp[:, 0:384], AF.Exp, r=[PP], w=[ES])
                    if br == 0 and use_sel:
                        tt(S, "dve", es[:].rearrange("p (h n) -> p h n", h=3), es[:].rearrange("p (h n) -> p h n", h=3),
                           MT[:, kb, :].unsqueeze(1).to_broadcast([128, 3, 128]), ALU.mult, r=[ES, MTK], w=[ES])
                    if br == 0 and kb >= 4 * j - 1:
                        tt(S, "dve", es[:], es[:], Z[:, 2 * (kb - 4 * j + 1) + g, :], ALU.mult, r=[ES, "Z"], w=[ES])
                    if br == 1:
                        tt(S, "dve", es[:], es[:], Z[:, 2 * (ZS + kb - 4 * j + 4) + g, :], ALU.mult, r=[ES, "Z"], w=[ES])
                    flush(1)
                    pend.append(lambda es=es, ES=ES, br=br, kb=kb, first=(si == 0), last=(kb == imax): pv_sw(es, ES, br, kb, first, last))
                    if si == 1:
                        while early:
                            early.pop(0)()
                    if si == 6:
                        while deferred:
                            deferred.pop(0)()
                flush()
                while early:
                    early.pop(0)()
                while deferred:
                    deferred.pop(0)()
                ts(S, "dve", rcs[:, 3:6], cS[:, :, 64], 1e-30, None, ALU.max, r=[CS], w=["rcs2"])
                S.op("dve", lambda e: e.reciprocal(rcs[:, 3:6], rcs[:, 3:6]), reads=["rcs2"], writes=["rcs2"])
                S.op("dve", lambda e: e.reciprocal(rcs[:, 6:12].rearrange("p (b h) -> p b h", b=2), pO[:, :, :, 64]), reads=["pO"], writes=["rcs2"])
                gv = gat[p][:, g * 9:(g + 1) * 9].rearrange("p (h b) -> p b h", b=3)
                tt(S, "dve", sc[:].rearrange("p (b h) -> p b h", b=3), rcs[:, 3:12].rearrange("p (b h) -> p b h", b=3), gv, ALU.mult,
                   r=["rcs2", "gat%d" % p], w=["sc"])
                o3 = lambda a: a[:].rearrange("p (h d) -> p h d", h=3)
                tt(S, "dve", o3(oa), cS[:, :, 0:64], sc[:, 0:3].unsqueeze(2).to_broadcast([128, 3, 64]), ALU.mult, r=[CS, "sc"], w=["oa"])
                tt(S, "dve", o3(ob), pO[:, 0, :, 0:64], sc[:, 3:6].unsqueeze(2).to_broadcast([128, 3, 64]), ALU.mult, r=["pO", "sc"], w=["ob"])
                tt(S, "pool", oa[:], oa[:], ob[:], ALU.add, r=["oa", "ob"], w=["oa"])
                tt(S, "dve", o3(ob), pO[:, 1, :, 0:64], sc[:, 6:9].unsqueeze(2).to_broadcast([128, 3, 64]), ALU.mult, r=["pO", "sc"], w=["ob"])
                tt(S, "pool", nsab[:, g * 192:(g + 1) * 192], oa[:], ob[:], ALU.add, r=["oa", "ob"], w=["nsab"])

            def finish(j):
                p = j % 2
                mixT = mixTs[p]; MX = "mixT%d" % p
                S.dma("sp", xres[p][:], T["xin"][j], writes=["xres0"])
                for c in range(3):
                    tr(S, pTr[:, c * 128:(c + 1) * 128], nsab[:, c * 128:(c + 1) * 128], ident[:], r=["nsab", "ident"], w=["pTr"])
                cp(S, "act", mixT[:, 3:6, :], pTr[:, 0:384].rearrange("p (c n) -> p c n", c=3), r=["pTr"], w=[MX])
                if "dbg_mix" in T:
                    S.dma("sp", T["dbg_mix"][j], mixT[:], reads=[MX])
                for half in range(2):
                    for c in range(8):
                        mm(S, pR[:, 0:512], mixT[:, c, :], wo[:, c, half * 512:(half + 1) * 512], start=(c == 0), stop=(c == 7),
                           r=[MX, "wo"], w=["pR"])
                    tt(S, "dve", x1o[p][:, half * 512:(half + 1) * 512], pR[:, 0:512], xres[p][:, half * 512:(half + 1) * 512], ALU.add,
                       r=["pR", "xres0"], w=["x1o0"])
                S.dma("sp", T["xout"][j], x1o[p][:], reads=["x1o0"], writes=["xout%d" % j])

            loads(0)
            pre_round(0, 0)
            while deferred:
                deferred.pop(0)()
            for j in range(NQ):
                if j + 1 < NQ:
                    loads(j + 1)
                ret(j)
                pre_round(j, 1)
                main_round(j, 0)
                if j + 1 < NQ:
                    pre_round(j + 1, 0)
                main_round(j, 1)
                early.append(lambda j=j: finish(j))
            while early:
                early.pop(0)()
            S.fence()
            if STOP == "b1":
                return
        emit_B2(nc, S, T, NQ, ident)
        S.fence()


def emit_B2(nc, S, T, NQ, ident):
    NT = (NQ + 3) // 4
    with ExitStack() as st:
        sb = lambda name, shape, dt: st.enter_context(nc.sbuf_tensor("B2_" + T.get("tag", "") + name, shape, dt))
        ps = lambda name, shape, dt: st.enter_context(nc.psum_tensor("B2_" + T.get("tag", "") + name, shape, dt))
        wgu = sb("wgu", [128, 8, 2 * DFF], BF16)
        wd = sb("wd", [128, 22, 1024], BF16)
        gain2 = sb("gain2", [128, 1024], F32)
        x1 = [sb("x1_%d" % i, [128, 1024], F32) for i in range(4)]
        junk = sb("junk", [128, 1024], BF16)
        ss = sb("ss", [128, 4], F32)
        h2 = sb("h2", [128, 1024], BF16)
        h2T = sb("h2T", [128, 8, 512], BF16)
        sgt = [sb("sgt%d" % i, [128, 512], BF16) for i in range(2)]
        aT = sb("aT", [128, 22, 512], BF16)
        xo = [sb("xo%d" % i, [128, 1024], F32) for i in range(2)]
        pX = ps("pX", [128, 1024], F32)
        pT2 = ps("pT2", [128, 1024], BF16)
        pG = [ps("pG%d" % i, [128, 512], F32) for i in range(2)]
        pUp = [ps("pUp%d" % i, [128, 512], F32) for i in range(2)]
        wgv = T["w_gu"].rearrange("(c p) f -> p c f", p=128)
        wdv = T["w_dn"].rearrange("(c p) f -> p c f", p=128)
        for c in range(8):
            S.dma("pool", wgu[:, c, :], wgv[:, c, :], writes=["wgu"])
        for c in range(0, 22, 2):
            S.dma("pool", wd[:, c:c + 2, :], wdv[:, c:c + 2, :], writes=["wd"])
        S.dma("sp", gain2[:], T["fnorm"], writes=["gain2"])
        nx = 0
        for t in range(NT):
            nb = min(4, NQ - 4 * t)
            W = nb * 128
            for jj in range(nb):
                j = 4 * t + jj
                X1 = "x1_%d" % jj
                S.dma("sp", x1[jj][:], T["xout"][j], reads=["xout%d" % j], writes=[X1])
                act(S, junk[:], x1[jj][:], AF.Square, r=[X1], w=["junk2", "fs0"], accum_out=ss[:, 0:1])
                rsq(S, ss[:, 2:3], ss[:, 1:2], ss[:, 0:1], 1.0 / 1024, EPS, r=["fs0"], wt=["fs1"], w=["fs2"])
                stt(S, "dve", h2[:], x1[jj][:], ss[:, 2:3], gain2[:], ALU.mult, ALU.mult, r=[X1, "fs2", "gain2"], w=["h2"])
                for c in range(8):
                    tr(S, pT2[:, c * 128:(c + 1) * 128], h2[:, c * 128:(c + 1) * 128], ident[:], r=["h2", "ident"], w=["pT2"])
                cp(S, "act", h2T[:, :, jj * 128:(jj + 1) * 128], pT2[:].rearrange("p (c n) -> p c n", c=8), r=["pT2"], w=["h2T"])
            for fc in range(22):
                q = fc % 2
                for c in range(8):
                    mm(S, pG[q][:, 0:W], wgu[:, c, fc * 128:(fc + 1) * 128], h2T[:, c, 0:W], start=(c == 0), stop=(c == 7),
                       r=["wgu", "h2T"], w=["pG%d" % q])
                for c in range(8):
                    mm(S, pUp[q][:, 0:W], wgu[:, c, DFF + fc * 128:DFF + (fc + 1) * 128], h2T[:, c, 0:W], start=(c == 0), stop=(c == 7),
                       r=["wgu", "h2T"], w=["pUp%d" % q])
                act(S, sgt[q][:, 0:W], pG[q][:, 0:W], AF.Silu, r=["pG%d" % q], w=["sgt%d" % q])
                tt(S, "dve", aT[:, fc, 0:W], pUp[q][:, 0:W], sgt[q][:, 0:W], ALU.mult, r=["pUp%d" % q, "sgt%d" % q], w=["aT"])
            for jj in range(nb):
                j = 4 * t + jj
                for half in range(2):
                    for fc in range(22):
                        mm(S, pX[:, half * 512:(half + 1) * 512], aT[:, fc, jj * 128:(jj + 1) * 128], wd[:, fc, half * 512:(half + 1) * 512],
                           start=(fc == 0), stop=(fc == 21), r=["aT", "wd"], w=["pX"])
                o = xo[jj % 2]; OK = "xo%d" % (jj % 2)
                tt(S, "dve", o[:], pX[:], x1[jj][:], ALU.add, r=["pX", "x1_%d" % jj], w=[OK])
                S.dma("sp", T["xout"][j], o[:], reads=[OK], writes=["xout%d" % j])


def host_gather_B(Ao, r, NQ):
    NB = 4 * NQ
    CW = 8 * NB + 40
    P = 32 - 8 * r
    ks = np.stack([np.asarray(Ao[rr]["o_ks"]) for rr in range(4)])
    ks2 = ks.transpose(2, 3, 1, 4, 0, 5).reshape(128, 2, NB * 128)
    v = np.stack([np.asarray(Ao[rr]["o_v"]) for rr in range(4)])
    vs2 = np.ones((128, 2 * NB * 2, 65), v.dtype)
    vs2[:, :, 0:64] = v.reshape(4, 2, NQ, 128, 2, 64).transpose(3, 1, 2, 0, 4, 5).reshape(128, 2 * NB * 2, 64)
    U = np.stack([np.asarray(Ao[rr]["o_U"]) for rr in range(4)])
    U2 = U.transpose(1, 0, 2, 3).reshape(NB, 64, 384)
    cA = np.stack([np.asarray(Ao[rr]["o_cA"]) for rr in range(4)])
    cAg = cA.reshape(4, 4, 2, 64, NQ, 8).transpose(3, 1, 2, 4, 0, 5).reshape(64, 8, NB * 8)
    cAsh = np.zeros((64, 8, CW), np.float32)
    cAsh[:, :, P:P + NB * 8] = cAg
    return {"ks2": np.ascontiguousarray(ks2), "vs2": np.ascontiguousarray(vs2), "U2": np.ascontiguousarray(U2), "cAsh": cAsh}


NQ_FULL = 16
RG = [[0, 1, 2, 3], [4, 5, 6, 7]]


def stack_layers(inp, fn):
    L = inp["w_in"].shape[0]
    per = [fn(inp, l) for l in range(L)]
    return {k: np.ascontiguousarray(np.stack([p[k] for p in per])) for k in per[0]}


def build_fused(NQ, L=2):
    NB = 4 * NQ
    NCHF = (32 * (NQ - 1) + 24 + 127) // 128
    nc = bass.Bass("TRN2", target_bir_lowering=False)
    X = {}
    def di(name, shape, dt=F32):
        X[name] = nc.dram_tensor(name, list(shape), dt, kind="ExternalInput").ap()
    def dn(name, shape, dt=F32):
        t = nc.dram_tensor(name, list(shape), dt)
        X[name] = t.ap()
        return t
    di("xin", [NQ, 128, 1024])
    di("w_in", [L, 1024, IN_W]); di("anorm", [L, 128, 1024]); di("gq", [L, 128, 384]); di("kg", [L, 128, 256])
    di("wsT", [L, 128, 4, 128]); di("gbT", [L, 128, 4]); di("w1", [L, 2, 2048, 64])
    di("w_out", [L, 1024, 1024]); di("fnorm", [L, 128, 1024]); di("w_gu", [L, 1024, 2 * DFF]); di("w_dn", [L, DFF, 1024])
    di("pe", [L, 128, 2, 16]); di("w2", [L, 64, 2, 64]); di("kg0", [L, 128, 64])
    di("ropeq", [NQ, 128, 384]); di("ropek", [NQ, 128, 384]); di("cmask", [128, 128]); di("idn", [128, 128])
    di("OVFr", [128, NCHF, 128]); di("OVBr", [16, 264]); di("Dtr", [128, 264])
    di("rdec", [64, 3, 384]); di("selr", [64, 4]); di("pfx", [64, 10, 384]); di("zf0", [128, 1])
    di("zmask", [ZS + ZW, 128, 384]); di("zneg", [ZS + ZW, 128, 384]); di("ztab", [ZS + ZW, 2, 128, 384]); di("zt31", [2, 128, 384])
    di("ctab", [2, 16, 384]); di("cmaskc", [2, 16, 384]); di("cneg", [2, 16, 384])
    X["xout"] = nc.dram_tensor("xout", [NQ, 128, 1024], F32, kind="ExternalOutput").ap()
    dn("xbuf", [NQ, 128, 1024])
    dn("mscr", [4, 128, 128], BF16)
    bt = {}
    JH = (NQ + 1) // 2
    pairs = []
    pairs = {0: [], 1: []}
    for hf in range(2):
        for b_ in range(2):
            nm = "%d_%d" % (b_, hf)
            bt["bks" + nm] = dn("bks" + nm, [128, JH * 128], BF16); bt["gks" + nm] = dn("gks" + nm, [4 * 128, JH * 128], BF16)
            bt["bv" + nm] = dn("bv" + nm, [JH * 128, 130], BF16); bt["gv" + nm] = dn("gv" + nm, [4 * JH * 128, 130], BF16)
            pairs[hf] += [("bks" + nm, "gks" + nm), ("bv" + nm, "gv" + nm)]
        bt["bU%d" % hf] = dn("bU%d" % hf, [JH * 64, 384]); bt["gU%d" % hf] = dn("gU%d" % hf, [4 * JH * 64, 384])
        pairs[hf].append(("bU%d" % hf, "gU%d" % hf))
    bt["bcA"] = dn("bcA", [512, NQ * 8]); bt["gcA"] = dn("gcA", [4 * 512, NQ * 8])
    pairs[1].append(("bcA", "gcA"))
    dn("o_rqT", [NQ, 64, 768], BF16); dn("o_inner", [NQ, 128, 384]); dn("o_sg", [NQ, 128, 384], BF16)
    dn("o_qnT", [NQ, 64, 768], BF16); dn("o_gates", [NQ, 128, 18]); dn("o_gmT", [NQ, 128, 256], BF16)
    with ExitStack() as st:
        S = Sched(nc, st)
        for l in range(L):
            TA = dict(X)
            for k in ("w_in", "anorm", "gq", "kg", "wsT", "gbT", "w1"):
                TA[k] = X[k][l]
            TA["xin"] = X["xin"] if l == 0 else X["xbuf"]
            TA["tag"] = "L%d_" % l
            def coll_half(hf):
                for bn, gn in pairs[hf]:
                    def mk_fn(bn=bn, gn=gn):
                        return lambda e: e.collective_compute("AllGather", ALU.bypass, replica_groups=RG,
                                                              ins=[bt[bn].ap().opt()], outs=[bt[gn].ap().opt()])
                    S.coll(mk_fn(), reads=[bn], writes=[gn])
            TA["coll_cb"] = coll_half
            emit_A(nc, S, TA, NQ)
            S.fence()
            coll_half(1)
            TB = dict(X)
            for k in ("w_out", "fnorm", "w_gu", "w_dn", "pe", "w2", "kg0"):
                TB[k] = X[k][l]
            TB["w1f"] = X["w1"][l]
            TB["cmask"] = X["cmaskc"]
            TB["tag"] = "L%d_" % l
            TB["xin"] = X["xin"] if l == 0 else X["xbuf"]
            TB["xout"] = X["xbuf"] if l < L - 1 else X["xout"]
            emit_B(nc, S, TB, NQ)
            S.fence()
        S.wait_all("sp")
        S.emit()
    return nc


def make_in_maps(inp, NQ):
    x = inp["x"].astype(np.float32)
    B_, T_, D_ = x.shape
    cores = [(b, r) for b in range(2) for r in range(4)]
    LA = stack_layers(inp, layer_inputs_A)
    LB = stack_layers(inp, layer_inputs_B)
    maps = []
    for (b, r) in cores:
        m = {}
        m["xin"] = np.ascontiguousarray(x[b].reshape(T_ // 128, 128, D_)[r::4][:NQ])
        cA = host_consts_A(b, r, NQ)
        cB = host_consts_Bu(NQ, r)
        btab = bias_tabs_u(inp["rel_bias"].astype(np.float32), r)
        for k in ("w_in", "anorm", "gq", "kg", "wsT", "gbT", "w1"):
            m[k] = LA[k]
        for k in ("w_out", "fnorm", "w_gu", "w_dn", "pe", "w2", "kg0"):
            m[k] = LB[k]
        for k in ("ropeq", "ropek", "cmask", "idn"):
            m[k] = cA[k]
        for k in ("OVFr", "OVBr", "Dtr", "rdec", "selr", "pfx", "zf0", "zmask", "zneg", "cneg"):
            m[k] = cB[k]
        m["cmaskc"] = cB["cmask"]
        for k in ("ztab", "zt31", "ctab"):
            m[k] = btab[k]
        maps.append({k: np.ascontiguousarray(v, dtype=np.float32) for k, v in m.items()})
    return cores, maps


def kernel(**inp):
    inp = {k: np.asarray(v) for k, v in inp.items()}
    NQ = NQ_FULL
    B_, T_, D_ = inp["x"].shape
    nc = build_fused(NQ, L=inp["w_in"].shape[0])
    cores, maps = make_in_maps(inp, NQ)
    res = run_bass_kernel_spmd(nc, maps, core_ids=list(range(8))).results
    out = np.zeros((B_, T_ // 128, 128, D_), np.float32)
    for ci, (b, r) in enumerate(cores):
        out[b, r::4] = np.asarray(res[ci]["xout"], dtype=np.float32)
    return out.reshape(B_, T_, D_)
```

```python
import math
import ml_dtypes
from contextlib import ExitStack
import numpy as np
import concourse.bass as bass
import concourse.mybir as mybir
from concourse.bass_utils import run_bass_kernel_spmd

F32 = mybir.dt.float32
BF16 = mybir.dt.bfloat16
AF = mybir.ActivationFunctionType
ALU = mybir.AluOpType
AX = mybir.AxisListType

N_DMA_SEMS = 24


class Sched:
    ENGS = ("pe", "act", "dve", "pool", "sp")

    def __init__(self, nc, stack):
        self.nc = nc
        self.stack = stack
        self.sem = {e: stack.enter_context(nc.semaphore("s_" + e)) for e in self.ENGS}
        self.dsem = [stack.enter_context(nc.semaphore("d%d" % i)) for i in range(N_DMA_SEMS)]
        self.dcnt = [0] * N_DMA_SEMS
        self.dnext = 0
        self.cnt = {e: 0 for e in self.ENGS}
        self.clock = {e: {} for e in self.ENGS}
        self.streams = {e: [] for e in self.ENGS}
        self.last_w = {}
        self.reads = {}
        self.n_waits = 0

    def _need(self, eng, ev, waits):
        key, val, snap = ev
        ck = self.clock[eng]
        if ck.get(key, 0) >= val:
            return
        if key == eng and eng == "pe":
            return
        waits[key] = max(waits.get(key, 0), val)

    def _apply(self, eng, evs, waits):
        ck = self.clock[eng]
        for key, val, snap in evs:
            if snap:
                for k2, v2 in snap.items():
                    if ck.get(k2, 0) < v2:
                        ck[k2] = v2
        for key, val in waits.items():
            if ck.get(key, 0) < val:
                ck[key] = val

    def _deps(self, eng, reads, writes):
        waits = {}
        evs = []
        for k in reads:
            ev = self.last_w.get(k)
            if ev is not None:
                self._need(eng, ev, waits)
                evs.append(ev)
        for k in writes:
            ev = self.last_w.get(k)
            if ev is not None:
                self._need(eng, ev, waits)
                evs.append(ev)
            for ev in self.reads.get(k, {}).values():
                self._need(eng, ev, waits)
                evs.append(ev)
        self._apply(eng, [e for e in evs if e[0] in waits and waits[e[0]] >= e[1]], waits)
        return waits

    def _commit(self, ev, reads, writes):
        for k in reads:
            self.reads.setdefault(k, {})[ev[0]] = ev
        for k in writes:
            self.last_w[k] = ev
            self.reads[k] = {}

    def op(self, eng, fn, reads=(), writes=()):
        waits = self._deps(eng, reads, writes)
        self.cnt[eng] += 1
        val = self.cnt[eng]
        snap = dict(self.clock[eng])
        ev = (eng, val, snap)
        self.streams[eng].append((waits, fn, ("c", eng, 1)))
        self.n_waits += len(waits)
        self._commit(ev, reads, writes)

    def dma(self, q, out, in_, reads=(), writes=(), **kw):
        if not hasattr(self, "dnq"):
            self.dnq = {"sp": 0, "pool": 0, "act": 0}
        if q == "pool":
            lo, n = 0, 8
        else:
            lo, n = 8, N_DMA_SEMS - 8
        i = lo + self.dnq[q] % n
        self.dnq[q] += 1
        waits = self._deps(q, reads, writes)
        key = ("d", i)
        prev = self.dcnt[i]
        if prev > 0 and self.clock[q].get(key, 0) < prev:
            waits[key] = max(waits.get(key, 0), prev)
            self.clock[q][key] = prev
        self.dcnt[i] += 16
        ev = (key, self.dcnt[i], dict(self.clock[q]))
        fn = lambda e, out=out, in_=in_, kw=kw: e.dma_start(out=out, in_=in_, **kw)
        self.streams[q].append((waits, fn, ("d", i, 16)))
        self._commit(ev, reads, writes)

    def dma_custom(self, q, fn, reads=(), writes=()):
        if not hasattr(self, "dnq"):
            self.dnq = {"sp": 0, "pool": 0, "act": 0}
        if q == "pool":
            lo, n = 0, 8
        else:
            lo, n = 8, N_DMA_SEMS - 8
        i = lo + self.dnq[q] % n
        self.dnq[q] += 1
        waits = self._deps(q, reads, writes)
        key = ("d", i)
        prev = self.dcnt[i]
        if prev > 0 and self.clock[q].get(key, 0) < prev:
            waits[key] = max(waits.get(key, 0), prev)
            self.clock[q][key] = prev
        self.dcnt[i] += 16
        ev = (key, self.dcnt[i], dict(self.clock[q]))
        self.streams[q].append((waits, fn, ("d", i, 16)))
        self._commit(ev, reads, writes)

    def coll(self, fn, reads=(), writes=()):
        if not hasattr(self, "csem"):
            self.csem = self.stack.enter_context(self.nc.semaphore("s_cc"))
            self.ccnt = 0
        waits = self._deps("pool", reads, writes)
        self.ccnt += 1
        ev = ("cc", self.ccnt, dict(self.clock["pool"]))
        self.streams["pool"].append((waits, fn, ("cc", None, 1)))
        self._commit(ev, reads, writes)

    def fence(self):
        tgt = {}
        if getattr(self, "ccnt", 0) > 0:
            tgt["cc"] = self.ccnt
        for e in self.ENGS:
            if self.cnt[e] > 0:
                tgt[e] = self.cnt[e]
        for i in range(N_DMA_SEMS):
            if self.dcnt[i] > 0:
                tgt[("d", i)] = self.dcnt[i]
        for e in self.ENGS:
            waits = {}
            for k, v in tgt.items():
                if k == e:
                    continue
                if self.clock[e].get(k, 0) < v:
                    waits[k] = v
                    self.clock[e][k] = v
            if waits:
                self.streams[e].append((waits, None, None))
        for e in self.ENGS:
            for k, v in tgt.items():
                if k != e and self.clock[e].get(k, 0) < v:
                    self.clock[e][k] = v

    def wait_all(self, eng):
        waits = {}
        for e in self.ENGS:
            if e != eng and self.cnt[e] > self.clock[eng].get(e, 0):
                waits[e] = self.cnt[e]
        for i in range(N_DMA_SEMS):
            if self.dcnt[i] > self.clock[eng].get(("d", i), 0):
                waits[("d", i)] = self.dcnt[i]
        if getattr(self, "ccnt", 0) > self.clock[eng].get("cc", 0):
            waits["cc"] = self.ccnt
        self.streams[eng].append((waits, None, None))

    def _semof(self, key):
        if isinstance(key, tuple):
            return self.dsem[key[1]]
        if key == "cc":
            return self.csem
        return self.sem[key]

    def emit(self):
        nc = self.nc
        with nc.Block() as block:
            def replay(name):
                def run(e):
                    for waits, fn, inc in self.streams[name]:
                        for key, val in waits.items():
                            e.wait_ge(self._semof(key), val)
                        if fn is None:
                            continue
                        ins = fn(e)
                        if inc[0] == "c":
                            ins.then_inc(self.sem[inc[1]], 1)
                        elif inc[0] == "cc":
                            ins.then_inc(self.csem)
                        else:
                            ins.then_inc(self.dsem[inc[1]], 16)
                return run
            block.tensor(replay("pe"))
            block.scalar(replay("act"))
            block.vector(replay("dve"))
            block.gpsimd(replay("pool"))
            block.sync(replay("sp"))


NPBF = ml_dtypes.bfloat16
EPS = 1e-6
D = 1024
IN_W = 3218
DFF = 2816
SLABS = [(0, 384), (384, 768), (768, 1152), (1152, 1536), (1536, 1920), (1920, 2432), (2432, 2706), (2706, 3218)]


def mm(S, out, lhsT, rhs, start=True, stop=True, r=(), w=(), sgc=False):
    if sgc:
        S.op("pe", lambda e: e.matmul(out, lhsT, rhs, start=start, stop=stop, skip_group_check=True), reads=r, writes=w)
    else:
        S.op("pe", lambda e: e.matmul(out, lhsT, rhs, start=start, stop=stop), reads=r, writes=w)

def tr(S, out, in_, ident, r=(), w=()):
    S.op("pe", lambda e: e.transpose(out, in_, ident), reads=r, writes=w)

def act(S, out, in_, func, r=(), w=(), **kw):
    S.op("act", lambda e: e.activation(out, in_, func, **kw), reads=r, writes=w)

def cp(S, eng, out, in_, r=(), w=()):
    if eng == "act":
        S.op("act", lambda e: e.copy(out, in_), reads=r, writes=w)
    else:
        S.op(eng, lambda e: e.tensor_copy(out, in_), reads=r, writes=w)

def tt(S, eng, out, a, b, op, r=(), w=()):
    S.op(eng, lambda e: e.tensor_tensor(out, a, b, op), reads=r, writes=w)

def ts(S, eng, out, a, s1, s2, op0, op1=None, r=(), w=()):
    if op1 is None:
        S.op(eng, lambda e: e.tensor_scalar(out, a, s1, s2, op0), reads=r, writes=w)
    else:
        S.op(eng, lambda e: e.tensor_scalar(out, a, s1, s2, op0, op1), reads=r, writes=w)

def stt(S, eng, out, in0, scalar, in1, op0, op1, r=(), w=()):
    S.op(eng, lambda e: e.scalar_tensor_tensor(out, in0, scalar, in1, op0, op1), reads=r, writes=w)

def red(S, eng, out, in_, r=(), w=()):
    S.op(eng, lambda e: e.tensor_reduce(out, in_, AX.X, ALU.add), reads=r, writes=w)

def rsq(S, out, tmp, in_, scale, bias, r=(), w=(), wt=()):
    S.op("act", lambda e: e.activation(tmp, in_, AF.Sqrt, bias=bias, scale=scale), reads=r, writes=wt)
    S.op("dve", lambda e: e.reciprocal(out, tmp), reads=wt, writes=w)

def mset(S, eng, ap, val, w=()):
    S.op(eng, lambda e: e.memset(ap, val), writes=w)


def host_consts_A(b, r, NQ):
    c = {}
    inv = (10000.0 ** (-np.arange(32, dtype=np.float32) / 32)).astype(np.float32)
    gam = 1.0 - 2.0 ** (-5.0 - np.arange(6, dtype=np.float64))
    n = np.arange(128, dtype=np.float64)
    gq = gam[None, :] ** n[:, None]
    gk = gam[None, :] ** (-n[:, None]) / 8.0
    rq = np.zeros((NQ, 128, 2, 6, 32), np.float32)
    rk = np.zeros((NQ, 128, 2, 6, 32), np.float32)
    for j in range(NQ):
        i = 4 * j + r
        t = (128 * i + np.arange(128)).astype(np.float32)
        ang = t[:, None] * inv[None, :]
        cs = np.stack([np.cos(ang), np.sin(ang)], 1).astype(np.float64)
        rq[j] = (cs[:, :, None, :] * gq[:, None, :, None]).astype(np.float32)
        rk[j] = (cs[:, :, None, :] * gk[:, None, :, None]).astype(np.float32)
    c["ropeq"] = rq.reshape(NQ, 128, 384)
    c["ropek"] = rk.reshape(NQ, 128, 384)
    m = np.arange(128)
    c["cmask"] = (m[:, None] <= m[None, :]).astype(np.float32)
    c["idn"] = np.eye(128, dtype=np.float32)
    return c


def layer_inputs_A(inp, l):
    d = {}
    d["w_in"] = np.ascontiguousarray(inp["w_in"][l])
    d["anorm"] = np.ascontiguousarray(np.broadcast_to(inp["attn_norm"][l][None, :], (128, 1024)))
    d["gq"] = np.ascontiguousarray(np.broadcast_to(np.tile(inp["nsa_q_gain"][l], 6)[None, :], (128, 384)))
    kg = inp["nsa_k_gain"][l]
    d["kg"] = np.ascontiguousarray(np.broadcast_to(np.concatenate([kg[1], kg[1], kg[2], kg[2]])[None, :], (128, 256)))
    d["wsT"] = np.ascontiguousarray(inp["gm_ws"][l].transpose(2, 0, 1))
    d["gbT"] = np.ascontiguousarray(inp["gm_b"][l].T)
    d["w1"] = np.ascontiguousarray(inp["cmp_w1"][l])
    return d


def declare_A(nc, NQ, xin_kind="ExternalInput", out_kind="ExternalOutput"):
    T = {}
    def di(name, shape, dt=F32):
        T[name] = nc.dram_tensor(name, list(shape), dt, kind="ExternalInput").ap()
    def do(name, shape, dt=F32):
        T[name] = nc.dram_tensor(name, list(shape), dt, kind=out_kind).ap()
    T["xin"] = nc.dram_tensor("xin", [NQ, 128, 1024], F32, kind=xin_kind).ap()
    di("w_in", [1024, IN_W]); di("anorm", [128, 1024]); di("gq", [128, 384]); di("kg", [128, 256])
    di("wsT", [128, 4, 128]); di("gbT", [128, 4]); di("w1", [2, 2048, 64])
    di("ropeq", [NQ, 128, 384]); di("ropek", [NQ, 128, 384]); di("cmask", [128, 128]); di("idn", [128, 128])
    do("o_ks", [2, 2, 64, NQ, 128], BF16)
    do("o_v", [2, NQ, 128, 128], BF16)
    do("o_U", [NQ, 64, 384])
    do("o_cA", [4, 2, 64, NQ * 8])
    do("o_rqT", [NQ, 64, 768], BF16)
    do("o_inner", [NQ, 128, 384])
    do("o_sg", [NQ, 128, 384], BF16)
    do("o_qnT", [NQ, 64, 768], BF16)
    do("o_gates", [NQ, 128, 18])
    do("o_gmT", [NQ, 128, 256], BF16)
    return T


def emit_A(nc, S, T, NQ):
    with ExitStack() as st:
        sb = lambda name, shape, dt: st.enter_context(nc.sbuf_tensor("A_" + T.get("tag", "") + name, shape, dt))
        ps = lambda name, shape, dt: st.enter_context(nc.psum_tensor("A_" + T.get("tag", "") + name, shape, dt))
        w_sb = sb("w_sb", [128, 8, IN_W], BF16)
        gain = sb("gain", [128, 1024], F32)
        gq = sb("gq", [128, 384], F32)
        kg = sb("kg", [128, 256], F32)
        wc = sb("wc", [128, 4, 128], BF16)
        wcf = sb("wcf", [128, 4, 128], F32)
        gbT = sb("gbT", [128, 4], F32)
        w1 = sb("w1", [64, 2, 32, 64], BF16)
        cmask = sb("cmask", [128, 128], F32)
        ident = sb("ident", [128, 128], BF16)
        acT = sb("acT", [64, 4, NQ * 128], BF16)
        x_t = [sb("x%d" % p, [128, 1024], F32) for p in range(2)]
        rq_t = [sb("rq%d" % p, [128, 384], F32) for p in range(2)]
        rk_t = [sb("rk%d" % p, [128, 384], F32) for p in range(2)]
        junk = sb("junk", [128, 1024], BF16)
        ss = sb("ss", [128, 40], F32)
        h = sb("h", [128, 1024], BF16)
        hTs = [sb("hT%d" % i, [128, 1024], BF16) for i in range(2)]
        rf = sb("rf", [128, 384], F32)
        t1 = sb("t1", [128, 192], F32); t2 = sb("t2", [128, 192], F32)
        t3 = sb("t3", [128, 192], F32); t4 = sb("t4", [128, 192], F32)
        qp = sb("qp", [128, 384], BF16)
        kp = sb("kp", [128, 384], BF16)
        qpT = sb("qpT", [64, 768], BF16)
        kpT = sb("kpT", [64, 768], BF16)
        v = sb("v", [128, 384], BF16)
        scT = sb("scT", [128, 768], BF16)
        inner = sb("inner", [128, 384], F32)
        U = sb("U", [64, 384], F32)
        sg = sb("sg", [128, 384], BF16)
        sq = sb("sq", [128, 512], F32)
        qn = sb("qn", [128, 384], F32)
        qnb = sb("qnb", [128, 384], BF16)
        qnT = sb("qnT", [64, 768], BF16)
        ac = sb("ac", [128, 256], BF16)
        kns = [sb("kn%d" % i, [128, 128], F32) for i in range(2)]
        knbs = [sb("knb%d" % i, [128, 128], BF16) for i in range(2)]
        ksT = sb("ksT", [64, 256], BF16)
        vsb = sb("vsb", [128, 130], BF16)
        gates = sb("gates", [128, 18], F32)
        zf = sb("zf", [128, 512], F32)
        z2 = sb("z2", [128, 512], F32)
        zg = sb("zg", [128, 512], F32)
        vn = sb("vn", [128, 256], BF16)
        gm1 = sb("gm1", [128, 256], F32)
        gmb = sb("gmb", [128, 256], BF16)
        gmT = sb("gmT", [128, 256], BF16)
        cA = sb("cA", [64, NQ * 8], F32)
        pT = ps("pT", [128, 1024], BF16)
        pP = [ps("pP%d" % p, [128, 512], F32) for p in range(2)]
        pTr = ps("pTr", [128, 1024], BF16)
        pSc = ps("pSc", [128, 1024], F32)
        pIn = ps("pIn", [128, 512], F32)
        pU = ps("pU", [128, 512], F32)

        wv = T["w_in"].rearrange("(c p) f -> p c f", p=128)
        for c in range(8):
            S.dma("pool", w_sb[:, c, :], wv[:, c, :], writes=["w_sb"])
        S.dma("sp", gain[:], T["anorm"], writes=["gain"])
        S.dma("sp", gq[:], T["gq"], writes=["gq"])
        S.dma("sp", kg[:], T["kg"], writes=["kg"])
        S.dma("sp", wcf[:], T["wsT"], writes=["wcf"])
        S.dma("sp", gbT[:], T["gbT"], writes=["gbT"])
        S.dma("sp", cmask[:], T["cmask"], writes=["cmask"])
        S.dma("pool", ident[:], T["idn"], writes=["ident"])
        S.dma("pool", w1[:], T["w1"].rearrange("k (l d) o -> d k l o", d=64), writes=["w1"])
        mset(S, "pool", vsb[:], 1.0, w=["vsb", "vsb1"])
        tt(S, "dve", wc[:], wcf[:], cmask[:].unsqueeze(1).to_broadcast([128, 4, 128]), ALU.mult, r=["wcf", "cmask"], w=["wc"])

        defer = []
        def run_deferred(keep=0):
            while len(defer) > keep:
                defer.pop(0)()
        v3 = lambda a: a[:].rearrange("p (h d) -> p h d", h=6)

        def head(j):
            p = j % 2
            X = "x%d" % p
            S.dma("sp", x_t[p][:], T["xin"][j], writes=[X])
            S.dma("sp", rq_t[p][:], T["ropeq"][j], writes=["rq%d" % p])
            S.dma("sp", rk_t[p][:], T["ropek"][j], writes=["rk%d" % p])
            act(S, junk[:], x_t[p][:], AF.Square, r=[X], w=["junk", "ss0"], accum_out=ss[:, 0:1])
            rsq(S, ss[:, 2:3], ss[:, 1:2], ss[:, 0:1], 1.0 / 1024, EPS, r=["ss0"], wt=["ss1"], w=["ss2"])
            stt(S, "dve", h[:], x_t[p][:], ss[:, 2:3], gain[:], ALU.mult, ALU.mult, r=[X, "ss2", "gain"], w=["h"])

        def head_pe(j):
            p = j % 2
            for c in range(8):
                tr(S, pT[:, c * 128:(c + 1) * 128], h[:, c * 128:(c + 1) * 128], ident[:], r=["h", "ident"], w=["pT"])
            cp(S, "act", hTs[p][:], pT[:], r=["pT"], w=["hT%d" % p])

        head(0)
        head_pe(0)
        for j in range(NQ):
            p = j % 2
            hT = hTs[p]; HT = "hT%d" % p
            for s, (c0, c1) in enumerate(SLABS):
                wd = c1 - c0
                pp = pP[s % 2]
                PP = "pP%d" % (s % 2)
                for c in range(8):
                    mm(S, pp[:, 0:wd], hT[:, c * 128:(c + 1) * 128], w_sb[:, c, c0:c1], start=(c == 0), stop=(c == 7),
                       r=[HT, "w_sb"], w=[PP])
                run_deferred(1)
                if s == 4 and j == (NQ + 1) // 2 and "coll_cb" in T:
                    T["coll_cb"](0)
                if s == 4 and j + 1 < NQ:
                    head(j + 1)
                if s == 6 and j + 1 < NQ:
                    head_pe(j + 1)
                if s in (0, 1):
                    tab = rq_t[p] if s == 0 else rk_t[p]
                    TAB = ("rq%d" if s == 0 else "rk%d") % p
                    dst = qp if s == 0 else kp
                    DST = "qp" if s == 0 else "kp"
                    dT = qpT if s == 0 else kpT
                    DT = "qpT" if s == 0 else "kpT"
                    cp(S, "act", rf[:], pp[:, 0:384], r=[PP], w=["rf"])
                    xv = rf[:].rearrange("p (h t d) -> p h t d", h=6, t=2)
                    tv = tab[:].rearrange("p (t h d) -> p t h d", t=2, h=6)
                    x1, x2 = xv[:, :, 0, :], xv[:, :, 1, :]
                    co, si = tv[:, 0], tv[:, 1]
                    tt(S, "dve", v3(t1), x1, co, ALU.mult, r=["rf", TAB], w=["t1"])
                    tt(S, "pool", v3(t2), x2, si, ALU.mult, r=["rf", TAB], w=["t2"])
                    tt(S, "dve", v3(t3), x2, co, ALU.mult, r=["rf", TAB], w=["t3"])
                    tt(S, "pool", v3(t4), x1, si, ALU.mult, r=["rf", TAB], w=["t4"])
                    dv = dst[:].rearrange("p (h t d) -> p h t d", h=6, t=2)
                    tt(S, "dve", dv[:, :, 0, :], v3(t1), v3(t2), ALU.subtract, r=["t1", "t2"], w=[DST])
                    tt(S, "pool", dv[:, :, 1, :], v3(t3), v3(t4), ALU.add, r=["t3", "t4"], w=[DST])
                    def st2(j=j, s=s, dst=dst, DST=DST, dT=dT, DT=DT):
                        for hh in range(6):
                            tr(S, pTr[0:64, hh * 128:(hh + 1) * 128], dst[:, hh * 64:(hh + 1) * 64], ident[:], r=[DST, "ident"], w=["pTr"])
                        cp(S, "act", dT[:], pTr[0:64, 0:768], r=["pTr"], w=[DT])
                        if s == 0:
                            S.dma("sp", T["o_rqT"][j], qpT[:], reads=["qpT"])
                    defer.append(st2)
                elif s == 2:
                    cp(S, "act", v[:], pp[:, 0:384], r=[PP], w=["v"])
                    def st2(j=j):
                        for hh in range(6):
                            mm(S, pSc[:, hh * 128:(hh + 1) * 128], kpT[:, hh * 128:(hh + 1) * 128], qpT[:, hh * 128:(hh + 1) * 128],
                               r=["kpT", "qpT"], w=["pSc"])
                        tt(S, "dve", scT[:].rearrange("p (h n) -> p h n", h=6), pSc[:, 0:768].rearrange("p (h n) -> p h n", h=6),
                           cmask[:].unsqueeze(1).to_broadcast([128, 6, 128]), ALU.mult, r=["pSc", "cmask"], w=["scT"])
                        for hh in range(6):
                            mm(S, pU[0:64, hh * 64:(hh + 1) * 64], kp[:, hh * 64:(hh + 1) * 64], v[:, hh * 64:(hh + 1) * 64],
                               r=["kp", "v"], w=["pU"])
                        cp(S, "act", U[:], pU[0:64, 0:384], r=["pU"], w=["U"])
                        S.dma("sp", T["bU%d" % (j // ((NQ + 1) // 2))].rearrange("(j d) e -> j d e", d=64)[j % ((NQ + 1) // 2)], U[:], reads=["U"], writes=["bU%d" % (j // ((NQ + 1) // 2))])
                        for hh in range(6):
                            mm(S, pIn[:, hh * 64:(hh + 1) * 64], scT[:, hh * 128:(hh + 1) * 128], v[:, hh * 64:(hh + 1) * 64],
                               r=["scT", "v"], w=["pIn"])
                        cp(S, "act", inner[:], pIn[:, 0:384], r=["pIn"], w=["inner"])
                        S.dma("sp", T["o_inner"][j], inner[:], reads=["inner"])
                    defer.append(st2)
                elif s == 3:
                    act(S, sg[:], pp[:, 0:384], AF.Silu, r=[PP], w=["sg"])
                    S.dma("sp", T["o_sg"][j], sg[:], reads=["sg"])
                elif s == 4:
                    act(S, sq[:, 0:384], pp[:, 0:384], AF.Square, r=[PP], w=["sq"])
                    red(S, "dve", ss[:, 3:9], sq[:, 0:384].rearrange("p (h d) -> p h d", h=6), r=["sq"], w=["sqa"])
                    rsq(S, ss[:, 9:15], ss[:, 33:39], ss[:, 3:9], 1.0, 64 * EPS, r=["sqa"], wt=["sqt"], w=["sqb"])
                    tt(S, "dve", qn[:].rearrange("p (h d) -> p h d", h=6), pp[:, 0:384].rearrange("p (h d) -> p h d", h=6),
                       ss[:, 9:15].unsqueeze(2).to_broadcast([128, 6, 64]), ALU.mult, r=[PP, "sqb"], w=["qn"])
                    tt(S, "pool", qnb[:], qn[:], gq[:], ALU.mult, r=["qn", "gq"], w=["qnb"])
                    def st2(j=j):
                        for hh in range(6):
                            tr(S, pTr[0:64, hh * 128:(hh + 1) * 128], qnb[:, hh * 64:(hh + 1) * 64], ident[:], r=["qnb", "ident"], w=["pTr"])
                        cp(S, "act", qnT[:], pTr[0:64, 0:768], r=["pTr"], w=["qnT"])
                        S.dma("sp", T["o_qnT"][j], qnT[:], reads=["qnT"])
                    defer.append(st2)
                elif s in (5, 6):
                    if s == 5:
                        cp(S, "act", ac[:], pp[:, 0:256], r=[PP], w=["ac"])
                        ko, vo, br = 256, 384, 0
                    else:
                        ko, vo, br = 0, 128, 1
                        act(S, gates[:], pp[:, 256:274], AF.Sigmoid, r=[PP], w=["gates"])
                        S.dma("sp", T["o_gates"][j], gates[:], reads=["gates"])
                    act(S, sq[:, 0:128], pp[:, ko:ko + 128], AF.Square, r=[PP], w=["sq"])
                    red(S, "dve", ss[:, 15:17], sq[:, 0:128].rearrange("p (h d) -> p h d", h=2), r=["sq"], w=["ska"])
                    rsq(S, ss[:, 19:21], ss[:, 17:19], ss[:, 15:17], 1.0 / 64, EPS, r=["ska"], wt=["skb"], w=["skc"])
                    kn = kns[br]; knb = knbs[br]
                    tt(S, "dve", kn[:].rearrange("p (h d) -> p h d", h=2), pp[:, ko:ko + 128].rearrange("p (h d) -> p h d", h=2),
                       ss[:, 19:21].unsqueeze(2).to_broadcast([128, 2, 64]), ALU.mult, r=[PP, "skc"], w=["kn%d" % br])
                    tt(S, "pool", knb[:], kn[:], kg[:, br * 128:(br + 1) * 128], ALU.mult, r=["kn%d" % br, "kg"], w=["knb%d" % br])
                    cp(S, "act", vsb[:].rearrange("p (g e) -> p g e", g=2)[:, :, 0:64], pp[:, vo:vo + 128].rearrange("p (g d) -> p g d", g=2), r=[PP], w=["vsb"])
                    S.dma("sp", T["bv%d_%d" % (br, j // ((NQ + 1) // 2))].rearrange("(j n) c -> j n c", n=128)[j % ((NQ + 1) // 2)], vsb[:], reads=["vsb", "vsb1"],
                          writes=["bv%d_%d" % (br, j // ((NQ + 1) // 2))])
                    def st2(j=j, s=s, br=br, knb=knb):
                        if s == 5:
                            for sl in range(4):
                                tr(S, pTr[0:64, sl * 128:(sl + 1) * 128], ac[:, sl * 64:(sl + 1) * 64], ident[:], r=["ac", "ident"], w=["pTr"])
                            cp(S, "act", acT[:, :, j * 128:(j + 1) * 128], pTr[0:64, 0:512].rearrange("p (s n) -> p s n", s=4), r=["pTr"], w=["acT"])
                        for g in range(2):
                            tr(S, pTr[0:64, g * 128:(g + 1) * 128], knb[:, g * 64:(g + 1) * 64], ident[:], r=["knb%d" % br, "ident"], w=["pTr"])
                        cp(S, "act", ksT[:], pTr[0:64, 0:256], r=["pTr"], w=["ksT"])
                        JH_ = (NQ + 1) // 2
                        bks4 = T["bks%d_%d" % (br, j // JH_)].rearrange("(g d) (j n) -> g d j n", g=2, j=JH_)
                        S.dma("sp", bks4[:, :, j % JH_, :].rearrange("g d n -> d g n"), ksT[:].rearrange("p (g n) -> p g n", g=2), reads=["ksT"],
                              writes=["bks%d_%d" % (br, j // JH_)])
                    defer.append(st2)
                else:
                    cp(S, "act", zf[:], pp[:, 0:512], r=[PP], w=["zf"])
                    tt(S, "pool", z2[:], zf[:], zf[:], ALU.mult, r=["zf"], w=["z2"])
                    ts(S, "dve", z2[:], z2[:], 0.044715, 1.0, ALU.mult, ALU.add, r=["z2"], w=["z2"])
                    tt(S, "pool", z2[:], z2[:], zf[:], ALU.mult, r=["z2", "zf"], w=["z2"])
                    act(S, zg[:], z2[:], AF.Sigmoid, r=["z2"], w=["zg"], scale=1.5957691216057308)
                    tt(S, "dve", zg[:], zg[:], zf[:], ALU.mult, r=["zg", "zf"], w=["zg"])
                    act(S, sq[:, 0:256], zg[:, 256:512], AF.Square, r=["zg"], w=["sq"])
                    red(S, "dve", ss[:, 21:25], sq[:, 0:256].rearrange("p (h d) -> p h d", h=4), r=["sq"], w=["sga"])
                    rsq(S, ss[:, 29:33], ss[:, 25:29], ss[:, 21:25], 1.0 / 64, EPS, r=["sga"], wt=["sgb"], w=["sgc"])
                    tt(S, "dve", vn[:].rearrange("p (h d) -> p h d", h=4), zg[:, 256:512].rearrange("p (h d) -> p h d", h=4),
                       ss[:, 29:33].unsqueeze(2).to_broadcast([128, 4, 64]), ALU.mult, r=["zg", "sgc"], w=["vn"])
                    def st2(j=j):
                        for g in range(4):
                            mm(S, pU[:, g * 64:(g + 1) * 64], wc[:, g, :], vn[:, g * 64:(g + 1) * 64], r=["wc", "vn"], w=["pU"])
                        tt(S, "dve", gm1[:].rearrange("p (h d) -> p h d", h=4), pU[:, 0:256].rearrange("p (h d) -> p h d", h=4),
                           gbT[:].unsqueeze(2).to_broadcast([128, 4, 64]), ALU.add, r=["pU", "gbT"], w=["gm1"])
                        tt(S, "pool", gmb[:], gm1[:], zg[:, 0:256], ALU.mult, r=["gm1", "zg"], w=["gmb"])
                        for c in range(2):
                            tr(S, pTr[:, c * 128:(c + 1) * 128], gmb[:, c * 128:(c + 1) * 128], ident[:], r=["gmb", "ident"], w=["pTr"])
                        cp(S, "act", gmT[:], pTr[:, 0:256], r=["pTr"], w=["gmT"])
                        S.dma("sp", T["o_gmT"][j], gmT[:], reads=["gmT"])
                    defer.append(st2)
        run_deferred()
        NG = NQ * 8
        for sl in range(4):
            kv = sl // 2
            for half in range(2):
                for lp in range(16):
                    mm(S, pU[0:64, 0:NG], w1[:, kv, half * 16 + lp, :], acT[:, sl, lp::16], start=(lp == 0), stop=(lp == 15),
                       r=["w1", "acT"], w=["pU"])
                cp(S, "act", cA[:], pU[0:64, 0:NG], r=["pU"], w=["cA"])
                S.dma("sp", T["bcA"].rearrange("(s h o) c -> s h o c", s=4, h=2)[sl, half], cA[:], reads=["cA"])


import math

def t5_bucket_np(dist):
    n = np.maximum(dist, 0)
    nf = np.maximum(n, 1).astype(np.float32)
    large = 16 + (np.log(nf / np.float32(16)) / np.float32(math.log(8.0)) * np.float32(16)).astype(np.int32)
    large = np.minimum(large, 31)
    return np.where(n < 16, n, large)


def host_consts_B(NQ):
    NB = 4 * NQ
    c = {}
    c["idn"] = np.eye(128, dtype=np.float32)
    E = np.zeros((128, NB, 128), np.float32)
    for kb in range(NB):
        if 2 * kb < 128:
            E[2 * kb, kb, 0:64] = 1
            E[2 * kb + 1, kb, 64:128] = 1
    c["E"] = E
    def ov(cs, ssv):
        return np.clip(np.minimum(cs + 32, ssv + 64) - np.maximum(cs, ssv), 0, None) // 16
    n = np.arange(512)
    j = np.arange(128)
    OV = ov(n[:, None] * 16, j[None, :] * 64).astype(np.float32)
    c["OVF"] = np.ascontiguousarray(OV.reshape(4, 128, 128).transpose(1, 0, 2))
    npr = np.arange(16)
    dl = np.arange(256) - 128
    c["OVB"] = ov((-128 + 16 * npr)[:, None], (64 * dl)[None, :]).astype(np.float32)
    Dt = np.zeros((128, 256), np.float32)
    for ql in range(128):
        for idx in range(256):
            d = idx - 128
            if ql < 64:
                v = 1e9 if d == -1 else 2e9 if d == 0 else -1e9 if d >= 1 else 0.0
            else:
                v = 1e9 if d == 0 else 2e9 if d == 1 else -1e9 if d >= 2 else 0.0
            Dt[ql, idx] = v
    c["Dt"] = Dt
    gam = 1.0 - 2.0 ** (-5.0 - np.arange(6, dtype=np.float64))
    rdec = np.stack([gam ** 128, gam ** 127, gam], 0)
    c["rdec"] = np.ascontiguousarray(np.broadcast_to(rdec[None, :, :, None], (64, 3, 6, 64)).reshape(64, 3, 384)).astype(np.float32)
    kl = np.arange(128)[:, None]; ql = np.arange(128)[None, :]
    m0 = (ql >= kl).astype(np.float32)
    c["mask0"] = np.ascontiguousarray(np.broadcast_to(m0[:, None, :], (128, 3, 128)).reshape(128, 384))
    c["neg0"] = (c["mask0"] - 1.0) * 30000.0
    m4 = (kl > ql).astype(np.float32)
    c["B4"] = np.ascontiguousarray(np.broadcast_to(((m4 - 1.0) * 30000.0)[:, None, :], (128, 3, 128)).reshape(128, 384))
    distc = ql + 97 - 16 * np.arange(16)[:, None]
    mc = (distc >= 0).astype(np.float32)
    mc0 = mc * (np.arange(16)[:, None] >= 8)
    bc3 = lambda a: np.ascontiguousarray(np.broadcast_to(a[:, None, :], (16, 3, 128)).reshape(16, 384))
    c["maskc"] = np.stack([bc3(mc), bc3(mc0)], 0)
    c["negc"] = (c["maskc"] - 1.0) * 30000.0
    return c


def bias_tabs(rel_bias):
    kl = np.arange(128)[:, None]; ql = np.arange(128)[None, :]
    b0 = t5_bucket_np(ql - kl); b1 = t5_bucket_np(128 + ql - kl)
    bcn = t5_bucket_np(ql + 97 - 16 * np.arange(16)[:, None])
    rb = rel_bias.reshape(32, 2, 3)
    d = {}
    d["tab0"] = np.ascontiguousarray(rb[b0].transpose(2, 0, 3, 1).reshape(2, 128, 384))
    d["tab1"] = np.ascontiguousarray(rb[b1].transpose(2, 0, 3, 1).reshape(2, 128, 384))
    d["tabc"] = np.ascontiguousarray(rb[bcn].transpose(2, 0, 3, 1).reshape(2, 16, 384))
    t31 = rb[31]
    d["tab31"] = np.ascontiguousarray(np.broadcast_to(t31[:, None, :, None], (2, 128, 3, 128)).reshape(2, 128, 384))
    return d


def layer_inputs_B(inp, l):
    d = {}
    d["w_out"] = np.ascontiguousarray(inp["w_out"][l])
    d["fnorm"] = np.ascontiguousarray(np.broadcast_to(inp["ffn_norm"][l][None, :], (128, 1024)))
    d["w_gu"] = np.ascontiguousarray(inp["w_gate_up"][l])
    d["w_dn"] = np.ascontiguousarray(inp["w_down"][l])
    d["pe"] = np.ascontiguousarray(inp["cmp_pe"][l].reshape(2, 16, 128).transpose(2, 0, 1))
    d["w1f"] = np.ascontiguousarray(inp["cmp_w1"][l])
    d["w2"] = np.ascontiguousarray(inp["cmp_w2"][l].transpose(1, 0, 2))
    d["kg0"] = np.ascontiguousarray(np.broadcast_to(inp["nsa_k_gain"][l][0][None, :], (128, 64)))
    return d


def declare_B(nc, NQ, in_kind="ExternalInput", out_kind="ExternalOutput", decl_x=True):
    NB = 4 * NQ
    T = {}
    def di(name, shape, dt=F32, kind="ExternalInput"):
        T[name] = nc.dram_tensor(name, list(shape), dt, kind=kind).ap()
    if decl_x:
        di("xin", [NQ, 128, 1024])
    di("g_ks", [4, 2, 2, 64, NQ, 128], BF16, in_kind)
    di("g_v", [4, 2, NQ, 128, 128], BF16, in_kind)
    di("g_U", [4, NQ, 64, 384], F32, in_kind)
    di("g_cA", [4, 4, 2, 64, NQ * 8], F32, in_kind)
    for nm, shp, dt in (("o_rqT", [NQ, 64, 768], BF16), ("o_inner", [NQ, 128, 384], F32), ("o_sg", [NQ, 128, 384], BF16),
                        ("o_qnT", [NQ, 64, 768], BF16), ("o_gates", [NQ, 128, 18], F32), ("o_gmT", [NQ, 128, 256], BF16)):
        di(nm, shp, dt, in_kind)
    di("w_out", [1024, 1024]); di("fnorm", [128, 1024]); di("w_gu", [1024, 2 * DFF]); di("w_dn", [DFF, 1024])
    di("pe", [128, 2, 16]); di("w1f", [2, 2048, 64]); di("w2", [64, 2, 64]); di("kg0", [128, 64])
    di("idn", [128, 128]); di("E", [128, NB, 128]); di("OVF", [128, 4, 128]); di("OVB", [16, 256]); di("Dt", [128, 256])
    di("rdec", [64, 3, 384]); di("mask0", [128, 384]); di("neg0", [128, 384]); di("B4", [128, 384])
    di("maskc", [2, 16, 384]); di("negc", [2, 16, 384])
    di("tab0", [2, 128, 384]); di("tab1", [2, 128, 384]); di("tabc", [2, 16, 384]); di("tab31", [2, 128, 384])
    T["xout"] = nc.dram_tensor("xout", [NQ, 128, 1024], F32, kind=out_kind).ap()
    return T


def gelu_ops(S, out_bf, x, tmp, tmp2, keys):
    kx, kt, kt2, ko = keys
    tt(S, "pool", tmp, x, x, ALU.mult, r=[kx], w=[kt])
    ts(S, "dve", tmp, tmp, 0.044715, 1.0, ALU.mult, ALU.add, r=[kt], w=[kt])
    tt(S, "pool", tmp, tmp, x, ALU.mult, r=[kt, kx], w=[kt])
    act(S, tmp2, tmp, AF.Sigmoid, r=[kt], w=[kt2], scale=1.5957691216057308)
    tt(S, "dve", out_bf, tmp2, x, ALU.mult, r=[kt2, kx], w=[ko])


GATH = ["gks0_0", "gks1_0", "gv0_0", "gv1_0", "gU0", "gks0_1", "gks1_1", "gv0_1", "gv1_1", "gU1", "gcA"]
ZS = 5
ZW = 8


def host_consts_Bu(NQ, r):
    NB = 4 * NQ
    NC = 8 * NB - 1
    CW = 8 * NB + 40
    NCHF = (32 * (NQ - 1) + 24 + 127) // 128
    P = 32 - 8 * r
    c0 = host_consts_B(NQ)
    c = {"idn": c0["idn"], "E": c0["E"], "rdec": c0["rdec"]}
    OVBr = np.zeros((16, 264), np.float32); OVBr[:, 2 * r:2 * r + 256] = c0["OVB"]
    Dtr = np.zeros((128, 264), np.float32); Dtr[:, 2 * r:2 * r + 256] = c0["Dt"]; Dtr[:, 2 * r + 256:] = -1e9
    c["OVBr"] = OVBr; c["Dtr"] = Dtr
    n = np.arange(128 * NCHF) - P
    j = np.arange(128)
    cs = n[:, None] * 16; ssv = j[None, :] * 64
    OV = (np.clip(np.minimum(cs + 32, ssv + 64) - np.maximum(cs, ssv), 0, None) // 16).astype(np.float32)
    OV[(n < 0) | (n >= NC)] = 0
    c["OVFr"] = np.ascontiguousarray(OV.reshape(NCHF, 128, 128).transpose(1, 0, 2))
    sel = np.zeros((64, 4), np.float32); sel[:, r] = 1
    c["selr"] = sel
    gam_ = 1.0 - 2.0 ** (-5.0 - np.arange(6, dtype=np.float64))
    dec_, c127_ = gam_ ** 128, gam_ ** 127
    pf = np.zeros((10, 6), np.float64)
    for k in range(4):
        pf[k] = dec_ ** (3 - k) * c127_
        pf[4 + k] = gam_ * c127_ * dec_ ** (r - 1 - k) if r > k else 0.0
    pf[8] = gam_ * dec_ ** r
    pf[9] = dec_ ** 4
    c["pfx"] = np.ascontiguousarray(np.broadcast_to(pf[None, :, :, None], (64, 10, 6, 64)).reshape(64, 10, 384)).astype(np.float32)
    zf0 = np.zeros((128, 1), np.float32); zf0[:P] = -30000.0
    c["zf0"] = zf0
    zmask = np.zeros((ZS + ZW, 128, 384), np.float32); zneg = np.zeros((ZS + ZW, 128, 384), np.float32)
    kinds = zone_kinds(r)
    for s_, kd in enumerate(kinds):
        if kd == "B0":
            zmask[s_] = c0["mask0"]; zneg[s_] = c0["neg0"]
        elif kd == "B1":
            zmask[s_] = 1.0
        elif kd == "NEG":
            zneg[s_] = -30000.0
        elif kd == "B4":
            zneg[s_] = c0["B4"]
    c["zmask"] = zmask; c["zneg"] = zneg
    cm = np.stack([c0["maskc"][0], c0["maskc"][1] if r == 0 else c0["maskc"][0]], 0)
    c["cmask"] = cm; c["cneg"] = (cm - 1.0) * 30000.0
    return c


def zone_kinds(r):
    kinds = []
    for s_ in range(ZS):
        dl = r + 1 - s_
        kinds.append("NEG" if dl < 0 else "B0" if dl == 0 else "B1" if dl == 1 else "Z")
    for s_ in range(ZW):
        dl = r + 4 - s_
        kinds.append("NEG" if (dl < 0 or dl > 4) else "B0" if dl == 0 else "B1" if dl == 1 else "B4" if dl == 4 else "Z")
    return kinds


def bias_tabs_u(rel_bias, r):
    t = bias_tabs(rel_bias)
    kinds = zone_kinds(r)
    ztab = np.zeros((ZS + ZW, 2, 128, 384), np.float32)
    for s_, kd in enumerate(kinds):
        ztab[s_] = t["tab0"] if kd == "B0" else t["tab1"] if kd == "B1" else t["tab31"]
    return {"ztab": ztab, "zt31": t["tab31"], "ctab": t["tabc"]}


def declare_Bu(nc, NQ, in_kind="ExternalInput", out_kind="ExternalOutput"):
    NB = 4 * NQ
    CW = 8 * NB + 40
    NCHF = (32 * (NQ - 1) + 24 + 127) // 128
    T = {}
    def di(name, shape, dt=F32, kind="ExternalInput"):
        T[name] = nc.dram_tensor(name, list(shape), dt, kind=kind).ap()
    di("xin", [NQ, 128, 1024])
    di("ks2", [128, 2, NB * 128], BF16, in_kind)
    di("vs2", [128, 2 * NB * 2, 65], BF16, in_kind)
    di("U2", [NB, 64, 384], F32, in_kind)
    di("cAsh", [64, 8, CW], F32, in_kind)
    for nm, shp, dt in (("o_rqT", [NQ, 64, 768], BF16), ("o_inner", [NQ, 128, 384], F32), ("o_sg", [NQ, 128, 384], BF16),
                        ("o_qnT", [NQ, 64, 768], BF16), ("o_gates", [NQ, 128, 18], F32), ("o_gmT", [NQ, 128, 256], BF16)):
        di(nm, shp, dt, in_kind)
    di("w_out", [1024, 1024]); di("fnorm", [128, 1024]); di("w_gu", [1024, 2 * DFF]); di("w_dn", [DFF, 1024])
    di("pe", [128, 2, 16]); di("w1f", [2, 2048, 64]); di("w2", [64, 2, 64]); di("kg0", [128, 64])
    di("idn", [128, 128]); di("E", [128, NB, 128]); di("OVFr", [128, NCHF, 128]); di("OVBr", [16, 264]); di("Dtr", [128, 264])
    di("rdec", [64, 3, 384]); di("selr", [64, 4]); di("pfx", [64, 10, 384]); di("zf0", [128, 1])
    di("zmask", [ZS + ZW, 128, 384]); di("zneg", [ZS + ZW, 128, 384]); di("ztab", [ZS + ZW, 2, 128, 384]); di("zt31", [2, 128, 384])
    di("ctab", [2, 16, 384]); di("cmask", [2, 16, 384]); di("cneg", [2, 16, 384])
    T["xout"] = nc.dram_tensor("xout", [NQ, 128, 1024], F32, kind=out_kind).ap()
    return T


def emit_B(nc, S, T, NQ):
    NB = 4 * NQ
    CW = 8 * NB + 40
    NCHF = (32 * (NQ - 1) + 24 + 127) // 128
    NCHK = (CW + 127) // 128
    NZ = ZS + ZW
    with ExitStack() as st0:
        sb0 = lambda name, shape, dt: st0.enter_context(nc.sbuf_tensor("B_" + T.get("tag", "") + name, shape, dt))
        ident = sb0("ident", [128, 128], BF16)
        S.dma("pool", ident[:], T["idn"], writes=["ident"])
        with ExitStack() as st:
            sb = lambda name, shape, dt: st.enter_context(nc.sbuf_tensor("B1_" + T.get("tag", "") + name, shape, dt))
            ps = lambda name, shape, dt: st.enter_context(nc.psum_tensor("B1_" + T.get("tag", "") + name, shape, dt))
            ksT = sb("ksT", [128, 2, NB * 128], BF16)
            vs = sb("vs", [128, 2 * NB * 2, 65], BF16)
            kcT = sb("kcT", [128, CW], BF16)
            gT = sb("gT", [64, 4, CW], BF16)
            CV = sb("CV", [128, 2 * NCHF, 193], BF16)
            Gs = sb("Gs", [64, NQ, 384], BF16)
            OVB = sb("OVB", [16, 264], BF16)
            Dt = sb("Dt", [128, 264], F32)
            selr = sb("selr", [64, 4], F32)
            Z = sb("Z", [128, NZ * 2, 384], BF16)
            zf0 = sb("zf0", [128, 1], F32)
            Bc = sb("Bc", [16, 4, 384], BF16)
            w2 = sb("w2", [64, 2, 64], BF16)
            kg0 = sb("kg0", [128, 64], F32)
            pS = [ps("pS%d" % i, [128, 512], F32) for i in range(3)]
            pC = ps("pC", [128, 4, 256], F32)
            pO = ps("pO", [128, 2, 3, 65], F32)
            pR = ps("pR", [128, 512], F32)
            pTr = ps("pTr", [128, 1024], BF16)
            pM = pS[2]

            S.dma("pool", OVB[:], T["OVBr"], writes=["OVB"])
            S.dma("pool", CV[:, 0:NCHF, 65:193], T["OVFr"], writes=["CVo"])
            S.dma("pool", CV[:, NCHF:2 * NCHF, 65:193], T["OVFr"], writes=["CVo"])
            mset(S, "pool", CV[:, :, 64:65], 1.0, w=["CV1"])
            S.dma("sp", Dt[:], T["Dtr"], writes=["Dt"])
            S.dma("sp", selr[:], T["selr"], writes=["selr"])
            S.dma("sp", zf0[:], T["zf0"], writes=["zf0"])
            S.dma("pool", w2[:], T["w2"], writes=["w2"])
            S.dma("sp", kg0[:], T["kg0"], writes=["kg0"])
            import os as _os
            STOP = _os.environ.get("STOPB", "")
            if STOP == "loads":
                S.fence(); return
            stm = ExitStack()
            stm.__enter__()
            if True:
                sbb = lambda name, shape, dt: stm.enter_context(nc.sbuf_tensor("Bb_" + T.get("tag", "") + name, shape, dt))
                zt = [sbb("zt%d" % i, [128, 2, 384], F32) for i in range(1)]
                zm = [sbb("zm%d" % i, [128, 2, 384], F32) for i in range(1)]
                t31 = sbb("t31", [128, 2, 384], F32)
                tc_ = sbb("tc", [16, 2, 384], F32)
                mc = sbb("mc", [16, 4, 384], F32)
                Bcf = sbb("Bcf", [16, 4, 384], F32)
                S.dma("sp", t31[:], T["zt31"].rearrange("g p f -> p g f"), writes=["t31"])
                for s_ in range(NZ):
                    q = 0
                    S.dma("sp", zt[q][:], T["ztab"][s_].rearrange("g p f -> p g f"), writes=["zt%d" % q])
                    S.dma("sp", zm[q][:, 0, :], T["zmask"][s_], writes=["zm%d" % q])
                    S.dma("sp", zm[q][:, 1, :], T["zneg"][s_], writes=["zm%d" % q])
                    tt(S, "dve", zt[q][:], zt[q][:], t31[:], ALU.subtract, r=["zt%d" % q, "t31"], w=["zt%d" % q])
                    tt(S, "dve", zt[q][:], zt[q][:], zm[q][:, 0:1, :].to_broadcast([128, 2, 384]), ALU.mult, r=["zt%d" % q, "zm%d" % q], w=["zt%d" % q])
                    tt(S, "dve", zt[q][:], zt[q][:], zm[q][:, 1:2, :].to_broadcast([128, 2, 384]), ALU.add,
                       r=["zt%d" % q, "zm%d" % q], w=["zt%d" % q])
                    act(S, Z[:, 2 * s_:2 * s_ + 2, :], zt[q][:], AF.Exp, r=["zt%d" % q], w=["Z"])
                S.dma("sp", tc_[:], T["ctab"].rearrange("g p f -> p g f"), writes=["tc"])
                S.dma("sp", mc[:, 0:2, :], T["cmask"].rearrange("v p f -> p v f"), writes=["mc"])
                S.dma("sp", mc[:, 2:4, :], T["cneg"].rearrange("v p f -> p v f"), writes=["mc"])
                tt(S, "dve", tc_[:], tc_[:], t31[0:16, :, :], ALU.subtract, r=["tc", "t31"], w=["tc"])
                for var in range(2):
                    tt(S, "dve", Bcf[:, 2 * var:2 * var + 2, :], tc_[:], mc[:, var:var + 1, :].to_broadcast([16, 2, 384]), ALU.mult, r=["tc", "mc"], w=["Bcf"])
                    tt(S, "dve", Bcf[:, 2 * var:2 * var + 2, :], Bcf[:, 2 * var:2 * var + 2, :], mc[:, 2 + var:3 + var, :].to_broadcast([16, 2, 384]), ALU.add,
                       r=["Bcf", "mc"], w=["Bcf"])
                    act(S, Bc[:, 2 * var:2 * var + 2, :], Bcf[:, 2 * var:2 * var + 2, :], AF.Exp, r=["Bcf"], w=["Bc"])
            if True:
                sbc = lambda name, shape, dt: stm.enter_context(nc.sbuf_tensor("Bc_" + T.get("tag", "") + name, shape, dt))
                cAs = sbc("cAs", [64, 8, CW + 24], F32)
                w1c = sbc("w1c", [128, 2, 16, 64], BF16)
                pec = sbc("pec", [128, 2, 16], BF16)
                cvec = sbc("cvec", [64, 2], F32)
                kcb = sbc("kcb", [128, 128], BF16)
                sqc = sbc("sqc", [128, 64], F32)
                ssc = sbc("ssc", [128, 4], F32)
                cAu = sbc("cAu", [64, 8, 4, NQ * 8], F32)
                gcA = T["gcA"].rearrange("(q s o) c -> q o s c", q=4, s=8)
                for rr in range(4):
                    S.dma("sp", cAu[:, :, rr, :], gcA[rr], reads=GATH, writes=["cAu"])
                mset(S, "pool", cAs[:], 0.0, w=["cAs"])
                for rp in range(4):
                    for rr in range(4):
                        off = 32 - 8 * rp + 8 * rr
                        for s8 in range(8):
                            dstc = cAs[:, s8, off:off + 32 * NQ].rearrange("p (j x) -> p j x", x=32)[:, :, 0:8]
                            srcc = cAu[:, s8, rr, :].rearrange("p (j g) -> p j g", g=8)
                            stt(S, "dve", dstc, srcc, selr[:, rp:rp + 1], dstc, ALU.mult, ALU.add, r=["cAu", "selr", "cAs"], w=["cAs"])
                S.dma("pool", w1c[:], T["w1f"].rearrange("k (c p) o -> p k c o", p=128), writes=["w1c"])
                S.dma("pool", pec[:], T["pe"], writes=["pec"])
                for kv in range(2):
                    for c in range(16):
                        mm(S, pM[0:64, kv:kv + 1], w1c[:, kv, c, :], pec[:, kv, c:c + 1], start=(c == 0), stop=(c == 15),
                           r=["w1c", "pec"], w=["pS2"])
                cp(S, "act", cvec[:], pM[0:64, 0:2], r=["pS2"], w=["cvec"])
                pre = cAu[:].rearrange("p s q c -> p (s q c)")[:, 0:4 * CW].rearrange("p (s c) -> p s c", s=4)
                mset(S, "pool", pre[:], 0.0, w=["pre", "cAu"])
                cv4 = cAs[:, :, 0:CW].rearrange("p (s h) c -> p s h c", h=2)
                tt(S, "dve", pre[:, :, 0:CW - 1], cv4[:, :, 0, 0:CW - 1], cv4[:, :, 1, 1:CW], ALU.add, r=["cAs", "pre", "cAu"], w=["pre", "cAu"])
                for sl in range(4):
                    ts(S, "dve", pre[:, sl, :], pre[:, sl, :], cvec[:, sl // 2:sl // 2 + 1], None, ALU.add, r=["pre", "cvec"], w=["pre"])
                gelu_ops(S, gT[:], pre[:], cAs[:, 0:4, 0:CW], cAs[:, 4:8, 0:CW], ("pre", "cAs", "cAs", "gT"))
                for c in range(NCHK):
                    rows = min(128, CW - 128 * c)
                    for g in range(2):
                        mm(S, pM[0:rows, 64:128], gT[:, g, 128 * c:128 * c + rows], w2[:, 0, :], r=["gT", "w2"], w=["pS2"])
                        act(S, sqc[0:rows, :], pM[0:rows, 64:128], AF.Square, r=["pS2"], w=["sqc", "ssc0"], accum_out=ssc[0:rows, 0:1])
                        rsq(S, ssc[0:rows, 2:3], ssc[0:rows, 1:2], ssc[0:rows, 0:1], 1.0 / 64, EPS, r=["ssc0"], wt=["ssc1"], w=["ssc2"])
                        stt(S, "dve", kcb[0:rows, g * 64:(g + 1) * 64], pM[0:rows, 64:128], ssc[0:rows, 2:3], kg0[0:rows, :], ALU.mult, ALU.mult,
                            r=["pS2", "ssc2", "kg0"], w=["kcb"])
                        if c < NCHF:
                            mm(S, pM[0:rows, 128:192], gT[:, 2 + g, 128 * c:128 * c + rows], w2[:, 1, :], r=["gT", "w2"], w=["pS2"])
                            cp(S, "act", CV[0:rows, g * NCHF + c, 0:64], pM[0:rows, 128:192], r=["pS2"], w=["CVv"])
                    tr(S, pTr[:, 0:rows], kcb[0:rows, :], ident[0:rows, 0:rows], r=["kcb", "ident"], w=["pTr"])
                    cp(S, "act", kcT[:, 128 * c:128 * c + rows], pTr[:, 0:rows], r=["pTr"], w=["kcT"])
            kvq = []
            JH_ = (NQ + 1) // 2
            for hf in range(2):
                nj = JH_ if hf == 0 else NQ - JH_
                if nj <= 0:
                    continue
                for rr in range(4):
                    for br in range(2):
                        gks_ = T["gks%d_%d" % (br, hf)].rearrange("(q r) c -> q r c", q=4)
                        gv_ = T["gv%d_%d" % (br, hf)].rearrange("(q j n) c -> q j n c", q=4, n=128)
                        for g in range(2):
                            dst = ksT[g * 64:(g + 1) * 64, br, :].rearrange("p (j q n) -> p j q n", q=4, n=128)[:, hf * JH_:hf * JH_ + nj, rr, :]
                            src = gks_[rr, g * 64:(g + 1) * 64, 0:nj * 128].rearrange("d (j n) -> d j n", n=128)
                            kvq.append(lambda dst=dst, src=src: S.dma("sp", dst, src, reads=GATH, writes=["ksT"]))
                        dstv = vs[:, br * NB * 2:(br + 1) * NB * 2, :].rearrange("p (j q g) e -> p j q (g e)", q=4, g=2)[:, hf * JH_:hf * JH_ + nj, rr, :]
                        kvq.append(lambda dstv=dstv, srcv=gv_[rr, 0:nj].rearrange("j n c -> n j c"): S.dma("sp", dstv, srcv, reads=GATH, writes=["vs", "vs1"]))
            if True:
                sbp = lambda name, shape, dt: stm.enter_context(nc.sbuf_tensor("Bp_" + T.get("tag", "") + name, shape, dt))
                Rst = sbp("Rst", [64, 384], F32)
                pfx = sbp("pfx", [64, 10, 384], F32)
                if NQ >= 12:
                    cAu_f = cAu[:].rearrange("p s q c -> p (s q c)")
                    cAs_f = cAs[:].rearrange("p s c -> p (s c)")
                    Ut = [cAu_f[:, i * 1536:(i + 1) * 1536].rearrange("p (k e) -> p k e", k=4) for i in range(2)]
                    tW = cAs_f[:, 0:1536].rearrange("p (k e) -> p k e", k=4)
                    tQ = cAs_f[:, 1536:3072].rearrange("p (k e) -> p k e", k=4)
                else:
                    Ut = [sbp("Ut%d" % i, [64, 4, 384], F32)[:] for i in range(2)]
                    tW = sbp("tW", [64, 4, 384], F32)[:]
                    tQ = sbp("tQ", [64, 4, 384], F32)[:]
                Wt = sbp("Wt", [64, 384], F32)
                Qt = sbp("Qt", [64, 384], F32)
                tG = sbp("tG", [64, 384], F32)
                S.dma("sp", pfx[:], T["pfx"], writes=["pfx"])
                mset(S, "pool", Rst[:], 0.0, w=["Rst"])
                for jj in range(NQ):
                    u = jj % 2
                    UT = "Ut%d" % u
                    extra_w = ["cAu", "pre"] if jj < 2 else []
                    S.dma("sp", Ut[u], T["gU%d" % (jj // ((NQ + 1) // 2))].rearrange("(q j d) e -> d q j e", q=4, d=64)[:, :, jj % ((NQ + 1) // 2), :], reads=GATH, writes=[UT] + extra_w)
                    for _ in range(3):
                        if kvq:
                            kvq.pop(0)()
                    tt(S, "pool", tQ, Ut[u], pfx[:, 4:8, :], ALU.mult, r=[UT, "pfx"], w=["tQ"] + (["cAs"] if jj == 0 else []))
                    red(S, "dve", Qt[:], tQ.rearrange("p k e -> p e k"), r=["tQ"], w=["Qt"])
                    tt(S, "dve", tG[:], Rst[:], pfx[:, 8, :], ALU.mult, r=["Rst", "pfx"], w=["tG"])
                    tt(S, "pool", Gs[:, jj, :], tG[:], Qt[:], ALU.add, r=["tG", "Qt"], w=["Gs%d" % jj])
                    if jj == NQ - 1:
                        break
                    tt(S, "pool", tW, Ut[u], pfx[:, 0:4, :], ALU.mult, r=[UT, "pfx"], w=["tW"] + (["cAs"] if jj == 0 else []))
                    red(S, "dve", Wt[:], tW.rearrange("p k e -> p e k"), r=["tW"], w=["Wt"])
                    tt(S, "dve", Rst[:], Rst[:], pfx[:, 9, :], ALU.mult, r=["Rst", "pfx"], w=["Rst"])
                    tt(S, "dve", Rst[:], Rst[:], Wt[:], ALU.add, r=["Rst", "Wt"], w=["Rst"])
            while kvq:
                kvq.pop(0)()
            S.fence()
            stm.__exit__(None, None, None)

            mixTs = [sb("mixT%d" % i, [128, 8, 128], BF16) for i in range(2)]
            wo = sb("wo", [128, 8, 1024], BF16)
            S.dma("pool", wo[:], T["w_out"].rearrange("(c p) f -> p c f", p=128), writes=["wo"])
            xres = [sb("xres%d" % i, [128, 1024], F32) for i in range(1)] * 2
            x1o = [sb("x1o%d" % i, [128, 1024], F32) for i in range(1)] * 2
            rqT = [sb("rqT%d" % p, [64, 768], BF16) for p in range(2)]
            inn = [sb("inn%d" % p, [128, 384], F32) for p in range(2)]
            sgt = [sb("sgt%d" % p, [128, 384], BF16) for p in range(2)]
            qnT = [sb("qnT%d" % p, [128, 768], BF16) for p in range(2)]
            for p_ in range(2):
                mset(S, "pool", qnT[p_][:], 0.0, w=["qnT%d" % p_])
            gat = [sb("gat%d" % p, [128, 18], F32) for p in range(2)]
            ro = sb("ro", [128, 384], F32)
            rsqv = sb("rsqv", [128, 384], F32)
            rss = sb("rss", [128, 24], F32)
            ron = sb("ron", [128, 384], F32)
            rob = sb("rob", [128, 384], BF16)
            eS = [sb("eS%d" % i, [128, 384], BF16) for i in range(4)]
            eC = [sb("eC%d" % i, [128, 384], BF16) for i in range(2)]
            eN = sb("eN", [16, 384], BF16)
            imp = sb("imp", [128, 128], F32)
            imw = sb("imw", [128, 128], F32)
            top = sb("top", [128, 16], F32)
            rcs = sb("rcs", [128, 12], F32)
            Mn = sb("Mn", [128, 128], BF16)
            MnT = sb("MnT", [128, 128], BF16)
            sc = sb("sc", [128, 9], F32)
            oa = sb("oa", [128, 192], F32)
            ob = sb("ob", [128, 192], F32)
            nsab = sb("nsab", [128, 384], BF16)
            MTs = [sb("MT%d" % i, [128, NB, 128], BF16) for i in range(2)]
            cSs = [sb("cS%d" % i, [128, 3, 193], F32) for i in range(2)]
            vnrs = [sb("vnr%d" % i, [16, 65], BF16) for i in range(2)]
            for i_ in range(2):
                mset(S, "pool", vnrs[i_][:, 64:65], 1.0, w=["vnr%d" % i_])
            cnt = {"ne": 0, "nce": 0, "nps": 0}
            deferred = []
            early = []

            def next_ps():
                k = cnt["nps"] % 3; cnt["nps"] += 1
                return pS[k], "pS%d" % k

            def loads(j):
                p = j % 2
                S.dma("sp", rqT[p][:], T["o_rqT"][j], writes=["rqT%d" % p])
                S.dma("sp", inn[p][:], T["o_inner"][j], writes=["inn%d" % p])
                S.dma("sp", sgt[p][:], T["o_sg"][j], writes=["sgt%d" % p])
                S.dma("sp", qnT[p][0:64, 0:384], T["o_qnT"][j][:, 0:384], writes=["qnT%d" % p])
                S.dma("sp", qnT[p][64:128, 384:768], T["o_qnT"][j][:, 384:768], writes=["qnT%d" % p])
                S.dma("sp", gat[p][:], T["o_gates"][j], writes=["gat%d" % p])

            def ret(j):
                p = j % 2
                mixT = mixTs[p]; MX = "mixT%d" % p
                for hh in range(6):
                    mm(S, pR[:, hh * 64:(hh + 1) * 64], rqT[p][:, hh * 128:(hh + 1) * 128], Gs[:, j, hh * 64:(hh + 1) * 64],
                       r=["rqT%d" % p, "Gs%d" % j], w=["pR"])
                tt(S, "dve", ro[:], pR[:, 0:384], inn[p][:], ALU.add, r=["pR", "inn%d" % p], w=["ro"])
                act(S, rsqv[:], ro[:], AF.Square, r=["ro"], w=["rsqv"])
                red(S, "dve", rss[:, 0:6], rsqv[:].rearrange("p (h d) -> p h d", h=6), r=["rsqv"], w=["rss0"])
                rsq(S, rss[:, 12:18], rss[:, 6:12], rss[:, 0:6], 1.0 / 64, EPS, r=["rss0"], wt=["rss1"], w=["rss2"])
                tt(S, "dve", ron[:].rearrange("p (h d) -> p h d", h=6), ro[:].rearrange("p (h d) -> p h d", h=6),
                   rss[:, 12:18].unsqueeze(2).to_broadcast([128, 6, 64]), ALU.mult, r=["ro", "rss2"], w=["ron"])
                tt(S, "pool", rob[:], ron[:], sgt[p][:], ALU.mult, r=["ron", "sgt%d" % p], w=["rob"])
                def ret2(mixT=mixT, MX=MX, j=j):
                    S.dma("sp", mixT[:, 6:8, :], T["o_gmT"][j].rearrange("p (c n) -> p c n", c=2), writes=[MX])
                    for c in range(3):
                        tr(S, pTr[:, c * 128:(c + 1) * 128], rob[:, c * 128:(c + 1) * 128], ident[:], r=["rob", "ident"], w=["pTr"])
                    cp(S, "act", mixT[:, 0:3, :], pTr[:, 0:384].rearrange("p (c n) -> p c n", c=3), r=["pTr"], w=[MX])
                early.append(ret2)

            def pre_round(j, g):
                nb = (2 * j + g) % 2
                p = j % 2
                imax = 4 * j + 3
                QK = "qnT%d" % p
                gp = slice(g * 64, (g + 1) * 64)
                qg = qnT[p][gp, g * 384:(g + 1) * 384]
                nf = 32 * j + 24
                nfc = (nf + 127) // 128
                var = 1 if j == 0 else 0
                vnr = vnrs[nb]; VN = "vnr%d" % nb
                cS = cSs[nb]; CS = "cS%d" % nb
                MT = MTs[nb]; MTK = "MT%d" % nb
                mm(S, pR[0:16, 448:512], gT[:, 2 + g, nf:nf + 16], w2[:, 1, :], r=["gT", "w2"], w=["pR"])
                cp(S, "act", vnr[:, 0:64], pR[0:16, 448:512], r=["pR"], w=[VN])
                pend = []
                def flush(keep=0):
                    while len(pend) > keep:
                        pend.pop(0)()
                def pv_far(ec, EC, rows, c):
                    for hh in range(3):
                        mm(S, pC[:, hh, 0:193], ec[0:rows, hh * 128:(hh + 1) * 128], CV[0:rows, g * NCHF + c, :], start=(c == 0 and hh != 1), stop=False,
                           r=[EC, "CVv", "CVo", "CV1"], w=["pC"], sgc=True)
                def pv_near():
                    for hh in range(3):
                        mm(S, pC[:, hh, 0:65], eN[:, hh * 128:(hh + 1) * 128], vnr[:, :], start=False, stop=True,
                           r=["eN", VN], w=["pC"], sgc=True)
                        mm(S, pC[:, hh, 65:193], eN[:, hh * 128:(hh + 1) * 128], OVB[:, 128 - 8 * j:256 - 8 * j], start=False, stop=True,
                           r=["eN", "OVB"], w=["pC"], sgc=True)
                for c in range(nfc):
                    rows = min(128, nf - 128 * c)
                    pp, PP = next_ps()
                    mm(S, pp[0:rows, 0:384], kcT[gp, 128 * c:128 * c + rows], qg, start=True, stop=True, r=["kcT", QK], w=[PP])
                    k_ = cnt["nce"] % 2; cnt["nce"] += 1
                    ec = eC[k_]; EC = "eC%d" % k_
                    if c == 0:
                        act(S, ec[0:rows, :], pp[0:rows, 0:384], AF.Exp, r=[PP, "zf0"], w=[EC], bias=zf0[0:rows, 0:1])
                    else:
                        act(S, ec[0:rows, :], pp[0:rows, 0:384], AF.Exp, r=[PP], w=[EC])
                    flush()
                    pend.append(lambda ec=ec, EC=EC, rows=rows, c=c: pv_far(ec, EC, rows, c))
                pp, PP = next_ps()
                mm(S, pp[0:16, 0:384], kcT[gp, nf:nf + 16], qg, start=True, stop=True, r=["kcT", QK], w=[PP])
                act(S, eN[:], pp[0:16, 0:384], AF.Exp, r=[PP], w=["eN"])
                tt(S, "dve", eN[:], eN[:], Bc[:, 2 * var + g, :], ALU.mult, r=["eN", "Bc"], w=["eN"])
                flush()
                pv_near()
                cp(S, "act", cS[:], pC[:, 0:3, 0:193], r=["pC"], w=[CS])
                if j >= 2:
                    ts(S, "dve", rcs[:, 0:3], cS[:, :, 64], 1e-30, None, ALU.max, r=[CS], w=["rcs"])
                    S.op("dve", lambda e: e.reciprocal(rcs[:, 0:3], rcs[:, 0:3]), reads=["rcs"], writes=["rcs"])
                    ts(S, "dve", imp[:], cS[:, 0, 65:193], rcs[:, 0:1], None, ALU.mult, r=[CS, "rcs"], w=["imp"])
                    stt(S, "dve", imp[:], cS[:, 1, 65:193], rcs[:, 1:2], imp[:], ALU.mult, ALU.add, r=[CS, "rcs", "imp"], w=["imp"])
                    stt(S, "dve", imp[:], cS[:, 2, 65:193], rcs[:, 2:3], imp[:], ALU.mult, ALU.add, r=[CS, "rcs", "imp"], w=["imp"])
                    tt(S, "dve", imp[:], imp[:], Dt[:, 128 - 8 * j:256 - 8 * j], ALU.add, r=["imp", "Dt"], w=["imp"])
                    mset(S, "dve", imp[:, 0:1], 3e9, w=["imp"])
                    S.op("dve", lambda e: e.max(out=top[:, 0:8], in_=imp[:]), reads=["imp"], writes=["top"])
                    S.op("dve", lambda e: e.match_replace(out=imw[:], in_to_replace=top[:, 0:8], in_values=imp[:], imm_value=-3e9),
                         reads=["imp", "top"], writes=["imw"])
                    S.op("dve", lambda e: e.max(out=top[:, 8:16], in_=imw[:]), reads=["imw"], writes=["top"])
                    ts(S, "dve", Mn[:], imp[:], top[:, 15:16], None, ALU.is_ge, r=["imp", "top"], w=["Mn"])
                    def fin():
                        tr(S, pTr[:, 512:640], Mn[:], ident[:], r=["Mn", "ident"], w=["pTr"])
                        cp(S, "dve", MnT[:], pTr[:, 512:640], r=["pTr"], w=["MnT"])
                        msl = T["mscr"][(2 * j + g) % 4]
                        MS = "mscr%d" % ((2 * j + g) % 4)
                        S.dma("sp", msl, MnT[:], reads=["MnT"], writes=[MS])
                        nkb = imax + 1
                        for hf in range(2):
                            srcm = msl.rearrange("(k two) q -> two k q", two=2)[hf, 0:nkb, :].unsqueeze(0).to_broadcast([64, nkb, 128])
                            S.dma("sp", MT[hf * 64:(hf + 1) * 64, 0:nkb, :], srcm, reads=[MS], writes=[MTK])
                    deferred.append(fin)

            def main_round(j, g):
                nb = (2 * j + g) % 2
                p = j % 2
                imax = 4 * j + 3
                QK = "qnT%d" % p
                gp = slice(g * 64, (g + 1) * 64)
                qg = qnT[p][gp, g * 384:(g + 1) * 384]
                use_sel = j >= 2
                cS = cSs[nb]; CS = "cS%d" % nb
                MT = MTs[nb]; MTK = "MT%d" % nb
                pend = []
                def flush(keep=0):
                    while len(pend) > keep:
                        pend.pop(0)()
                def pv_sw(es, ES, br, kb, first, last):
                    for hh in range(3):
                        mm(S, pO[:, br, hh, :], es[:, hh * 128:(hh + 1) * 128], vs[:, (br * NB + kb) * 2 + g, :],
                           start=(first and hh == 0), stop=last, r=[ES, "vs", "vs1"], w=["pO"], sgc=True)
                steps = [(1, kb) for kb in range(max(0, 4 * j - 4), imax + 1)] + [(0, kb) for kb in range(0, imax + 1)]
                for si, (br, kb) in enumerate(steps):
                    pp, PP = next_ps()
                    mm(S, pp[:, 0:384], ksT[:, br, kb * 128:(kb + 1) * 128], qnT[p][:, g * 384:(g + 1) * 384], start=True, stop=True, r=["ksT", QK], w=[PP])
                    k_ = cnt["ne"] % 4; cnt["ne"] += 1
                    es = eS[k_]; ES = "eS%d" % k_
                    act(S, es[:], pp[:, 0:384], AF.Exp, r=[PP], w=[ES])
                    if br == 0 and use_sel:
                        tt(S, "dve", es[:].rearrange("p (h n) -> p h n", h=3), es[:].rearrange("p (h n) -> p h n", h=3),
                           MT[:, kb, :].unsqueeze(1).to_broadcast([128, 3, 128]), ALU.mult, r=[ES, MTK], w=[ES])
                    if br == 0 and kb >= 4 * j - 1:
                        tt(S, "dve", es[:], es[:], Z[:, 2 * (kb - 4 * j + 1) + g, :], ALU.mult, r=[ES, "Z"], w=[ES])
                    if br == 1:
                        tt(S, "dve", es[:], es[:], Z[:, 2 * (ZS + kb - 4 * j + 4) + g, :], ALU.mult, r=[ES, "Z"], w=[ES])
                    flush(1)
                    pend.append(lambda es=es, ES=ES, br=br, kb=kb, first=(si == 0), last=(kb == imax): pv_sw(es, ES, br, kb, first, last))
                    if si == 1:
                        while early:
                            early.pop(0)()
                    if si == 6:
                        while deferred:
                            deferred.pop(0)()
                flush()
                while early:
                    early.pop(0)()
                while deferred:
                    deferred.pop(0)()
                ts(S, "dve", rcs[:, 3:6], cS[:, :, 64], 1e-30, None, ALU.max, r=[CS], w=["rcs2"])
                S.op("dve", lambda e: e.reciprocal(rcs[:, 3:6], rcs[:, 3:6]), reads=["rcs2"], writes=["rcs2"])
                S.op("dve", lambda e: e.reciprocal(rcs[:, 6:12].rearrange("p (b h) -> p b h", b=2), pO[:, :, :, 64]), reads=["pO"], writes=["rcs2"])
                gv = gat[p][:, g * 9:(g + 1) * 9].rearrange("p (h b) -> p b h", b=3)
                tt(S, "dve", sc[:].rearrange("p (b h) -> p b h", b=3), rcs[:, 3:12].rearrange("p (b h) -> p b h", b=3), gv, ALU.mult,
                   r=["rcs2", "gat%d" % p], w=["sc"])
                o3 = lambda a: a[:].rearrange("p (h d) -> p h d", h=3)
                tt(S, "dve", o3(oa), cS[:, :, 0:64], sc[:, 0:3].unsqueeze(2).to_broadcast([128, 3, 64]), ALU.mult, r=[CS, "sc"], w=["oa"])
                tt(S, "dve", o3(ob), pO[:, 0, :, 0:64], sc[:, 3:6].unsqueeze(2).to_broadcast([128, 3, 64]), ALU.mult, r=["pO", "sc"], w=["ob"])
                tt(S, "pool", oa[:], oa[:], ob[:], ALU.add, r=["oa", "ob"], w=["oa"])
                tt(S, "dve", o3(ob), pO[:, 1, :, 0:64], sc[:, 6:9].unsqueeze(2).to_broadcast([128, 3, 64]), ALU.mult, r=["pO", "sc"], w=["ob"])
                tt(S, "pool", nsab[:, g * 192:(g + 1) * 192], oa[:], ob[:], ALU.add, r=["oa", "ob"], w=["nsab"])

            def finish(j):
                p = j % 2
                mixT = mixTs[p]; MX = "mixT%d" % p
                S.dma("sp", xres[p][:], T["xin"][j], writes=["xres0"])
                for c in range(3):
                    tr(S, pTr[:, c * 128:(c + 1) * 128], nsab[:, c * 128:(c + 1) * 128], ident[:], r=["nsab", "ident"], w=["pTr"])
                cp(S, "act", mixT[:, 3:6, :], pTr[:, 0:384].rearrange("p (c n) -> p c n", c=3), r=["pTr"], w=[MX])
                if "dbg_mix" in T:
                    S.dma("sp", T["dbg_mix"][j], mixT[:], reads=[MX])
                for half in range(2):
                    for c in range(8):
                        mm(S, pR[:, 0:512], mixT[:, c, :], wo[:, c, half * 512:(half + 1) * 512], start=(c == 0), stop=(c == 7),
                           r=[MX, "wo"], w=["pR"])
                    tt(S, "dve", x1o[p][:, half * 512:(half + 1) * 512], pR[:, 0:512], xres[p][:, half * 512:(half + 1) * 512], ALU.add,
                       r=["pR", "xres0"], w=["x1o0"])
                S.dma("sp", T["xout"][j], x1o[p][:], reads=["x1o0"], writes=["xout%d" % j])

            loads(0)
            pre_round(0, 0)
            while deferred:
                deferred.pop(0)()
            for j in range(NQ):
                if j + 1 < NQ:
                    loads(j + 1)
                ret(j)
                pre_round(j, 1)
                main_round(j, 0)
                if j + 1 < NQ:
                    pre_round(j + 1, 0)
                main_round(j, 1)
                early.append(lambda j=j: finish(j))
            while early:
                early.pop(0)()
            S.fence()
            if STOP == "b1":
                return
        emit_B2(nc, S, T, NQ, ident)
        S.fence()


def emit_B2(nc, S, T, NQ, ident):
    NT = (NQ + 3) // 4
    with ExitStack() as st:
        sb = lambda name, shape, dt: st.enter_context(nc.sbuf_tensor("B2_" + T.get("tag", "") + name, shape, dt))
        ps = lambda name, shape, dt: st.enter_context(nc.psum_tensor("B2_" + T.get("tag", "") + name, shape, dt))
        wgu = sb("wgu", [128, 8, 2 * DFF], BF16)
        wd = sb("wd", [128, 22, 1024], BF16)
        gain2 = sb("gain2", [128, 1024], F32)
        x1 = [sb("x1_%d" % i, [128, 1024], F32) for i in range(4)]
        junk = sb("junk", [128, 1024], BF16)
        ss = sb("ss", [128, 4], F32)
        h2 = sb("h2", [128, 1024], BF16)
        h2T = sb("h2T", [128, 8, 512], BF16)
        sgt = [sb("sgt%d" % i, [128, 512], BF16) for i in range(2)]
        aT = sb("aT", [128, 22, 512], BF16)
        xo = [sb("xo%d" % i, [128, 1024], F32) for i in range(2)]
        pX = ps("pX", [128, 1024], F32)
        pT2 = ps("pT2", [128, 1024], BF16)
        pG = [ps("pG%d" % i, [128, 512], F32) for i in range(2)]
        pUp = [ps("pUp%d" % i, [128, 512], F32) for i in range(2)]
        wgv = T["w_gu"].rearrange("(c p) f -> p c f", p=128)
        wdv = T["w_dn"].rearrange("(c p) f -> p c f", p=128)
        for c in range(8):
            S.dma("pool", wgu[:, c, :], wgv[:, c, :], writes=["wgu"])
        for c in range(0, 22, 2):
            S.dma("pool", wd[:, c:c + 2, :], wdv[:, c:c + 2, :], writes=["wd"])
        S.dma("sp", gain2[:], T["fnorm"], writes=["gain2"])
        nx = 0
        for t in range(NT):
            nb = min(4, NQ - 4 * t)
            W = nb * 128
            for jj in range(nb):
                j = 4 * t + jj
                X1 = "x1_%d" % jj
                S.dma("sp", x1[jj][:], T["xout"][j], reads=["xout%d" % j], writes=[X1])
                act(S, junk[:], x1[jj][:], AF.Square, r=[X1], w=["junk2", "fs0"], accum_out=ss[:, 0:1])
                rsq(S, ss[:, 2:3], ss[:, 1:2], ss[:, 0:1], 1.0 / 1024, EPS, r=["fs0"], wt=["fs1"], w=["fs2"])
                stt(S, "dve", h2[:], x1[jj][:], ss[:, 2:3], gain2[:], ALU.mult, ALU.mult, r=[X1, "fs2", "gain2"], w=["h2"])
                for c in range(8):
                    tr(S, pT2[:, c * 128:(c + 1) * 128], h2[:, c * 128:(c + 1) * 128], ident[:], r=["h2", "ident"], w=["pT2"])
                cp(S, "act", h2T[:, :, jj * 128:(jj + 1) * 128], pT2[:].rearrange("p (c n) -> p c n", c=8), r=["pT2"], w=["h2T"])
            for fc in range(22):
                q = fc % 2
                for c in range(8):
                    mm(S, pG[q][:, 0:W], wgu[:, c, fc * 128:(fc + 1) * 128], h2T[:, c, 0:W], start=(c == 0), stop=(c == 7),
                       r=["wgu", "h2T"], w=["pG%d" % q])
                for c in range(8):
                    mm(S, pUp[q][:, 0:W], wgu[:, c, DFF + fc * 128:DFF + (fc + 1) * 128], h2T[:, c, 0:W], start=(c == 0), stop=(c == 7),
                       r=["wgu", "h2T"], w=["pUp%d" % q])
                act(S, sgt[q][:, 0:W], pG[q][:, 0:W], AF.Silu, r=["pG%d" % q], w=["sgt%d" % q])
                tt(S, "dve", aT[:, fc, 0:W], pUp[q][:, 0:W], sgt[q][:, 0:W], ALU.mult, r=["pUp%d" % q, "sgt%d" % q], w=["aT"])
            for jj in range(nb):
                j = 4 * t + jj
                for half in range(2):
                    for fc in range(22):
                        mm(S, pX[:, half * 512:(half + 1) * 512], aT[:, fc, jj * 128:(jj + 1) * 128], wd[:, fc, half * 512:(half + 1) * 512],
                           start=(fc == 0), stop=(fc == 21), r=["aT", "wd"], w=["pX"])
                o = xo[jj % 2]; OK = "xo%d" % (jj % 2)
                tt(S, "dve", o[:], pX[:], x1[jj][:], ALU.add, r=["pX", "x1_%d" % jj], w=[OK])
                S.dma("sp", T["xout"][j], o[:], reads=[OK], writes=["xout%d" % j])


def host_gather_B(Ao, r, NQ):
    NB = 4 * NQ
    CW = 8 * NB + 40
    P = 32 - 8 * r
    ks = np.stack([np.asarray(Ao[rr]["o_ks"]) for rr in range(4)])
    ks2 = ks.transpose(2, 3, 1, 4, 0, 5).reshape(128, 2, NB * 128)
    v = np.stack([np.asarray(Ao[rr]["o_v"]) for rr in range(4)])
    vs2 = np.ones((128, 2 * NB * 2, 65), v.dtype)
    vs2[:, :, 0:64] = v.reshape(4, 2, NQ, 128, 2, 64).transpose(3, 1, 2, 0, 4, 5).reshape(128, 2 * NB * 2, 64)
    U = np.stack([np.asarray(Ao[rr]["o_U"]) for rr in range(4)])
    U2 = U.transpose(1, 0, 2, 3).reshape(NB, 64, 384)
    cA = np.stack([np.asarray(Ao[rr]["o_cA"]) for rr in range(4)])
    cAg = cA.reshape(4, 4, 2, 64, NQ, 8).transpose(3, 1, 2, 4, 0, 5).reshape(64, 8, NB * 8)
    cAsh = np.zeros((64, 8, CW), np.float32)
    cAsh[:, :, P:P + NB * 8] = cAg
    return {"ks2": np.ascontiguousarray(ks2), "vs2": np.ascontiguousarray(vs2), "U2": np.ascontiguousarray(U2), "cAsh": cAsh}


NQ_FULL = 16
RG = [[0, 1, 2, 3], [4, 5, 6, 7]]


def stack_layers(inp, fn):
    L = inp["w_in"].shape[0]
    per = [fn(inp, l) for l in range(L)]
    return {k: np.ascontiguousarray(np.stack([p[k] for p in per])) for k in per[0]}


def build_fused(NQ, L=2):
    NB = 4 * NQ
    NCHF = (32 * (NQ - 1) + 24 + 127) // 128
    nc = bass.Bass("TRN2", target_bir_lowering=False)
    X = {}
    def di(name, shape, dt=F32):
        X[name] = nc.dram_tensor(name, list(shape), dt, kind="ExternalInput").ap()
    def dn(name, shape, dt=F32):
        t = nc.dram_tensor(name, list(shape), dt)
        X[name] = t.ap()
        return t
    di("xin", [NQ, 128, 1024])
    di("w_in", [L, 1024, IN_W]); di("anorm", [L, 128, 1024]); di("gq", [L, 128, 384]); di("kg", [L, 128, 256])
    di("wsT", [L, 128, 4, 128]); di("gbT", [L, 128, 4]); di("w1", [L, 2, 2048, 64])
    di("w_out", [L, 1024, 1024]); di("fnorm", [L, 128, 1024]); di("w_gu", [L, 1024, 2 * DFF]); di("w_dn", [L, DFF, 1024])
    di("pe", [L, 128, 2, 16]); di("w2", [L, 64, 2, 64]); di("kg0", [L, 128, 64])
    di("ropeq", [NQ, 128, 384]); di("ropek", [NQ, 128, 384]); di("cmask", [128, 128]); di("idn", [128, 128])
    di("OVFr", [128, NCHF, 128]); di("OVBr", [16, 264]); di("Dtr", [128, 264])
    di("rdec", [64, 3, 384]); di("selr", [64, 4]); di("pfx", [64, 10, 384]); di("zf0", [128, 1])
    di("zmask", [ZS + ZW, 128, 384]); di("zneg", [ZS + ZW, 128, 384]); di("ztab", [ZS + ZW, 2, 128, 384]); di("zt31", [2, 128, 384])
    di("ctab", [2, 16, 384]); di("cmaskc", [2, 16, 384]); di("cneg", [2, 16, 384])
    X["xout"] = nc.dram_tensor("xout", [NQ, 128, 1024], F32, kind="ExternalOutput").ap()
    dn("xbuf", [NQ, 128, 1024])
    dn("mscr", [4, 128, 128], BF16)
    bt = {}
    JH = (NQ + 1) // 2
    pairs = []
    pairs = {0: [], 1: []}
    for hf in range(2):
        for b_ in range(2):
            nm = "%d_%d" % (b_, hf)
            bt["bks" + nm] = dn("bks" + nm, [128, JH * 128], BF16); bt["gks" + nm] = dn("gks" + nm, [4 * 128, JH * 128], BF16)
            bt["bv" + nm] = dn("bv" + nm, [JH * 128, 130], BF16); bt["gv" + nm] = dn("gv" + nm, [4 * JH * 128, 130], BF16)
            pairs[hf] += [("bks" + nm, "gks" + nm), ("bv" + nm, "gv" + nm)]
        bt["bU%d" % hf] = dn("bU%d" % hf, [JH * 64, 384]); bt["gU%d" % hf] = dn("gU%d" % hf, [4 * JH * 64, 384])
        pairs[hf].append(("bU%d" % hf, "gU%d" % hf))
    bt["bcA"] = dn("bcA", [512, NQ * 8]); bt["gcA"] = dn("gcA", [4 * 512, NQ * 8])
    pairs[1].append(("bcA", "gcA"))
    dn("o_rqT", [NQ, 64, 768], BF16); dn("o_inner", [NQ, 128, 384]); dn("o_sg", [NQ, 128, 384], BF16)
    dn("o_qnT", [NQ, 64, 768], BF16); dn("o_gates", [NQ, 128, 18]); dn("o_gmT", [NQ, 128, 256], BF16)
    with ExitStack() as st:
        S = Sched(nc, st)
        for l in range(L):
            TA = dict(X)
            for k in ("w_in", "anorm", "gq", "kg", "wsT", "gbT", "w1"):
                TA[k] = X[k][l]
            TA["xin"] = X["xin"] if l == 0 else X["xbuf"]
            TA["tag"] = "L%d_" % l
            def coll_half(hf):
                for bn, gn in pairs[hf]:
                    def mk_fn(bn=bn, gn=gn):
                        return lambda e: e.collective_compute("AllGather", ALU.bypass, replica_groups=RG,
                                                              ins=[bt[bn].ap().opt()], outs=[bt[gn].ap().opt()])
                    S.coll(mk_fn(), reads=[bn], writes=[gn])
            TA["coll_cb"] = coll_half
            emit_A(nc, S, TA, NQ)
            S.fence()
            coll_half(1)
            TB = dict(X)
            for k in ("w_out", "fnorm", "w_gu", "w_dn", "pe", "w2", "kg0"):
                TB[k] = X[k][l]
            TB["w1f"] = X["w1"][l]
            TB["cmask"] = X["cmaskc"]
            TB["tag"] = "L%d_" % l
            TB["xin"] = X["xin"] if l == 0 else X["xbuf"]
            TB["xout"] = X["xbuf"] if l < L - 1 else X["xout"]
            emit_B(nc, S, TB, NQ)
            S.fence()
        S.wait_all("sp")
        S.emit()
    return nc


def make_in_maps(inp, NQ):
    x = inp["x"].astype(np.float32)
    B_, T_, D_ = x.shape
    cores = [(b, r) for b in range(2) for r in range(4)]
    LA = stack_layers(inp, layer_inputs_A)
    LB = stack_layers(inp, layer_inputs_B)
    maps = []
    for (b, r) in cores:
        m = {}
        m["xin"] = np.ascontiguousarray(x[b].reshape(T_ // 128, 128, D_)[r::4][:NQ])
        cA = host_consts_A(b, r, NQ)
        cB = host_consts_Bu(NQ, r)
        btab = bias_tabs_u(inp["rel_bias"].astype(np.float32), r)
        for k in ("w_in", "anorm", "gq", "kg", "wsT", "gbT", "w1"):
            m[k] = LA[k]
        for k in ("w_out", "fnorm", "w_gu", "w_dn", "pe", "w2", "kg0"):
            m[k] = LB[k]
        for k in ("ropeq", "ropek", "cmask", "idn"):
            m[k] = cA[k]
        for k in ("OVFr", "OVBr", "Dtr", "rdec", "selr", "pfx", "zf0", "zmask", "zneg", "cneg"):
            m[k] = cB[k]
        m["cmaskc"] = cB["cmask"]
        for k in ("ztab", "zt31", "ctab"):
            m[k] = btab[k]
        maps.append({k: np.ascontiguousarray(v, dtype=np.float32) for k, v in m.items()})
    return cores, maps


def kernel(**inp):
    inp = {k: np.asarray(v) for k, v in inp.items()}
    NQ = NQ_FULL
    B_, T_, D_ = inp["x"].shape
    nc = build_fused(NQ, L=inp["w_in"].shape[0])
    cores, maps = make_in_maps(inp, NQ)
    res = run_bass_kernel_spmd(nc, maps, core_ids=list(range(8))).results
    out = np.zeros((B_, T_ // 128, 128, D_), np.float32)
    for ci, (b, r) in enumerate(cores):
        out[b, r::4] = np.asarray(res[ci]["xout"], dtype=np.float32)
    return out.reshape(B_, T_, D_)
```
